# Optimizing a Trainium2 kernel written in Bass

```python
import numpy as np
import jax, jax.numpy as jnp
from jax import lax

D_MODEL = 1024
BATCH = 8
SEQ = 4096
DEPTH = 4

ROPE_THETA = 500000.0
NORM_EPS = 1e-6
Q_BLOCK = 128
MLA_HEADS = 8
MLA_NOPE_DIM = 64
MLA_ROPE_DIM = 32
MLA_QK_DIM = MLA_NOPE_DIM + MLA_ROPE_DIM
MLA_V_DIM = 64
MLA_Q_RANK = 384
MLA_KV_RANK = 256
CONV_CH = 512
CONV_WIDTH = 31
NSA_HEADS = 8
NSA_KV_GROUPS = 2
NSA_HEAD_DIM = 64
NSA_ROPE_DIM = NSA_HEAD_DIM // 4
NSA_N_BRANCH = 3
CMP_BLOCK = 32
CMP_STRIDE = 16
CMP_HIDDEN = 128
SLC_BLOCK = 64
SLC_TOP_N = 16
WINDOW = 512
NSA_Q_BLOCK = 64
FORCED_BLOCK_SCORE = 1e6
N_MIXERS = 3
D_FF = 4 * D_MODEL

IN_SPLITS = (
    MLA_Q_RANK,
    MLA_KV_RANK,
    MLA_ROPE_DIM,
    2 * CONV_CH,
    NSA_HEADS * NSA_HEAD_DIM,
    2 * NSA_N_BRANCH * NSA_KV_GROUPS * NSA_HEAD_DIM,
    NSA_N_BRANCH * NSA_HEADS,
    N_MIXERS * D_MODEL,
)
D_IN = sum(IN_SPLITS)
SPLIT_IDX = tuple(int(v) for v in np.cumsum(IN_SPLITS)[:-1])

kernel_name = "hybrid_mla_conformer_nsa_block"


def rms_norm(x, g):
    xf = x.astype(jnp.float32)
    y = xf * lax.rsqrt(jnp.mean(xf * xf, axis=-1, keepdims=True) + NORM_EPS)
    return (y * g.astype(jnp.float32)).astype(x.dtype)


def layer_norm(x, g, b):
    xf = x.astype(jnp.float32)
    mu = jnp.mean(xf, axis=-1, keepdims=True)
    var = jnp.mean(jnp.square(xf - mu), axis=-1, keepdims=True)
    y = (xf - mu) * lax.rsqrt(var + NORM_EPS)
    return (y * g.astype(jnp.float32) + b.astype(jnp.float32)).astype(x.dtype)


def rope(x, pos, rot_dim):
    half = rot_dim // 2
    inv = jnp.power(jnp.float32(ROPE_THETA), -jnp.arange(half, dtype=jnp.float32) * (2.0 / rot_dim))
    ang = pos.astype(jnp.float32)[:, None] * inv
    cos = jnp.cos(ang)[:, None, :]
    sin = jnp.sin(ang)[:, None, :]
    xf = x.astype(jnp.float32)
    x1, x2 = xf[..., :half], xf[..., half:rot_dim]
    out = jnp.concatenate([x1 * cos - x2 * sin, x2 * cos + x1 * sin, xf[..., rot_dim:]], axis=-1)
    return out.astype(x.dtype)


def masked_softmax(s, mask):
    s = jnp.where(mask, s.astype(jnp.float32), -jnp.inf)
    m = jnp.max(s, axis=-1, keepdims=True)
    m = jnp.where(jnp.isfinite(m), m, 0.0)
    p = jnp.exp(s - m)
    return p / jnp.maximum(jnp.sum(p, axis=-1, keepdims=True), 1e-30)


def mla_mixer(c_q, c_kv, k_rope, g_cq, g_ckv, w_uq, w_ukv, g_q, g_k, w_o, pos):
    B, S, _ = c_q.shape
    dt = c_q.dtype
    q = (rms_norm(c_q, g_cq) @ w_uq).reshape(B, S, MLA_HEADS, MLA_QK_DIM)
    kv = (rms_norm(c_kv, g_ckv) @ w_ukv).reshape(B, S, MLA_HEADS, MLA_NOPE_DIM + MLA_V_DIM)
    k_nope, v = kv[..., :MLA_NOPE_DIM], kv[..., MLA_NOPE_DIM:]
    k_r = jnp.broadcast_to(k_rope[:, :, None, :], (B, S, MLA_HEADS, MLA_ROPE_DIM))
    k = jnp.concatenate([k_r, k_nope], axis=-1)
    q = rope(rms_norm(q, g_q), pos, MLA_ROPE_DIM)
    k = rope(rms_norm(k, g_k), pos, MLA_ROPE_DIM)
    scale = MLA_QK_DIM ** -0.5

    def block(i):
        qb = lax.dynamic_slice_in_dim(q, i * Q_BLOCK, Q_BLOCK, axis=1)
        t = i * Q_BLOCK + jnp.arange(Q_BLOCK)
        s = jnp.einsum('bqhd,bkhd->bhqk', qb, k) * scale
        p = masked_softmax(s, pos[None, :] <= t[:, None])
        return jnp.einsum('bhqk,bkhd->bqhd', p.astype(dt), v)

    o = lax.map(block, jnp.arange(S // Q_BLOCK))
    o = jnp.moveaxis(o, 0, 1).reshape(B, S, MLA_HEADS * MLA_V_DIM)
    return o @ w_o


def conv_mixer(u2, b_glu, w_dw, b_dw, g_ln, b_ln, w_out, b_out):
    a, gate = jnp.split(u2 + b_glu, 2, axis=-1)
    u = a * jax.nn.sigmoid(gate)
    u = lax.conv_general_dilated(u, w_dw[:, None, :], (1,), [(CONV_WIDTH - 1, 0)],
                                 dimension_numbers=('NWC', 'WIO', 'NWC'),
                                 feature_group_count=CONV_CH) + b_dw
    u = jax.nn.silu(layer_norm(u, g_ln, b_ln))
    return u @ w_out + b_out


def compress(tok, pe, w1, w2):
    B, S, G, dh = tok.shape
    n_cmp = (S - CMP_BLOCK) // CMP_STRIDE + 1
    idx = (jnp.arange(n_cmp) * CMP_STRIDE)[:, None] + jnp.arange(CMP_BLOCK)[None, :]
    blocks = tok[:, idx] + pe[None, None, :, None, :]
    blocks = jnp.moveaxis(blocks, 2, 3).reshape(B, n_cmp, G, CMP_BLOCK * dh)
    return jax.nn.silu(blocks @ w1) @ w2


def nsa_mixer(q_raw, kv_raw, gate_logits, pe_k, pe_v, w_ck1, w_ck2, w_cv1, w_cv2, g_q, g_k, w_o, pos):
    B, S, _ = q_raw.shape
    dt = q_raw.dtype
    H, G, dh = NSA_HEADS, NSA_KV_GROUPS, NSA_HEAD_DIM
    hpg = H // G
    QB = NSA_Q_BLOCK
    q = rope(rms_norm(q_raw.reshape(B, S, H, dh), g_q), pos, NSA_ROPE_DIM)
    kv = kv_raw.reshape(B, S, 2 * NSA_N_BRANCH, G, dh)
    kc, vc, ks, vs, kw, vw = [kv[:, :, j] for j in range(2 * NSA_N_BRANCH)]
    n_cmp = (S - CMP_BLOCK) // CMP_STRIDE + 1
    cmp_pos = jnp.arange(n_cmp) * CMP_STRIDE + CMP_BLOCK - 1
    k_cmp = rope(rms_norm(compress(kc, pe_k, w_ck1, w_ck2), g_k), cmp_pos, NSA_ROPE_DIM)
    v_cmp = compress(vc, pe_v, w_cv1, w_cv2)
    n_slc = S // SLC_BLOCK
    top_n = min(SLC_TOP_N, n_slc)
    k_slc = rope(rms_norm(ks, g_k), pos, NSA_ROPE_DIM)
    k_slc_blk = k_slc.reshape(B, n_slc, SLC_BLOCK, G, dh).transpose(0, 3, 1, 2, 4)
    v_slc_blk = vs.reshape(B, n_slc, SLC_BLOCK, G, dh).transpose(0, 3, 1, 2, 4)
    c_start = jnp.arange(n_cmp) * CMP_STRIDE
    s_start = jnp.arange(n_slc) * SLC_BLOCK
    overlap = ((c_start[:, None] < s_start[None, :] + SLC_BLOCK)
               & (c_start[:, None] + CMP_BLOCK > s_start[None, :])).astype(jnp.float32)
    k_win = rope(rms_norm(kw, g_k), pos, NSA_ROPE_DIM)
    k_win_pad = jnp.pad(k_win, ((0, 0), (WINDOW, 0), (0, 0), (0, 0)))
    v_win_pad = jnp.pad(vw, ((0, 0), (WINDOW, 0), (0, 0), (0, 0)))
    scale = dh ** -0.5
    bi = jnp.arange(B)[:, None, None, None]
    gi = jnp.arange(G)[None, :, None, None]
    j_blk = jnp.arange(n_slc)

    def block(i):
        q0 = i * QB
        t = q0 + jnp.arange(QB)
        qb = lax.dynamic_slice_in_dim(q, q0, QB, axis=1).reshape(B, QB, G, hpg, dh)
        s_c = jnp.einsum('bqghd,bcgd->bghqc', qb, k_cmp) * scale
        p_c = masked_softmax(s_c, cmp_pos[None, :] <= t[:, None])
        o_c = jnp.einsum('bghqc,bcgd->bqghd', p_c.astype(dt), v_cmp)
        imp = jnp.einsum('bghqc,cn->bgqn', p_c, overlap)
        blk_t = t // SLC_BLOCK
        forced = ((j_blk[None, :] == 0) | (j_blk[None, :] == blk_t[:, None])
                  | (j_blk[None, :] == blk_t[:, None] - 1))
        imp = jnp.where(forced, FORCED_BLOCK_SCORE, imp)
        imp = jnp.where(j_blk[None, :] > blk_t[:, None], -jnp.inf, imp)
        _, idx = lax.top_k(imp, top_n)
        k_sel = k_slc_blk[bi, gi, idx]
        v_sel = v_slc_blk[bi, gi, idx]
        s_s = jnp.einsum('bqghd,bgqnkd->bghqnk', qb, k_sel) * scale
        kpos = idx[..., None] * SLC_BLOCK + jnp.arange(SLC_BLOCK)
        m_s = (kpos <= t[None, None, :, None, None])[:, :, None]
        p_s = masked_softmax(s_s.reshape(B, G, hpg, QB, top_n * SLC_BLOCK),
                             m_s.reshape(B, G, 1, QB, top_n * SLC_BLOCK))
        o_s = jnp.einsum('bghqnk,bgqnkd->bqghd', p_s.reshape(s_s.shape).astype(dt), v_sel)
        k_w = lax.dynamic_slice_in_dim(k_win_pad, q0, WINDOW + QB, axis=1)
        v_w = lax.dynamic_slice_in_dim(v_win_pad, q0, WINDOW + QB, axis=1)
        wpos = q0 - WINDOW + jnp.arange(WINDOW + QB)
        m_w = ((wpos[None, :] <= t[:, None]) & (wpos[None, :] > t[:, None] - WINDOW)
               & (wpos[None, :] >= 0))
        s_w = jnp.einsum('bqghd,bkgd->bghqk', qb, k_w) * scale
        p_w = masked_softmax(s_w, m_w)
        o_w = jnp.einsum('bghqk,bkgd->bqghd', p_w.astype(dt), v_w)
        return jnp.stack([o_c, o_s, o_w], axis=-2)

    o = lax.map(block, jnp.arange(S // QB))
    o = jnp.moveaxis(o, 0, 1).reshape(B, S, H, NSA_N_BRANCH, dh)
    g = jax.nn.sigmoid(gate_logits.reshape(B, S, H, NSA_N_BRANCH).astype(jnp.float32)).astype(dt)
    o = jnp.einsum('bshnd,bshn->bshd', o, g).reshape(B, S, H * dh)
    return o @ w_o


def hybrid_layer(x, pos, g_mix, w_in, g_cq, g_ckv, w_uq, w_ukv, g_q_mla, g_k_mla, w_o_mla,
                 b_glu, w_dw, b_dw, g_conv_ln, b_conv_ln, w_conv_out, b_conv_out,
                 pe_cmp_k, pe_cmp_v, w_cmp_k1, w_cmp_k2, w_cmp_v1, w_cmp_v2,
                 g_q_nsa, g_k_nsa, w_o_nsa, w_out, g_ffn, w_ff1, w_ff2):
    dt = x.dtype
    h = rms_norm(x, g_mix)
    z = h @ w_in
    c_q, c_kv, k_rope, u2, q_nsa, kv_nsa, gate_nsa, gate_mix = jnp.split(z, SPLIT_IDX, axis=-1)
    o_a = mla_mixer(c_q, c_kv, k_rope, g_cq, g_ckv, w_uq, w_ukv, g_q_mla, g_k_mla, w_o_mla, pos)
    o_b = conv_mixer(u2, b_glu, w_dw, b_dw, g_conv_ln, b_conv_ln, w_conv_out, b_conv_out)
    o_c = nsa_mixer(q_nsa, kv_nsa, gate_nsa, pe_cmp_k, pe_cmp_v, w_cmp_k1, w_cmp_k2,
                    w_cmp_v1, w_cmp_v2, g_q_nsa, g_k_nsa, w_o_nsa, pos)
    g_a, g_b, g_c = jnp.split(jax.nn.sigmoid(gate_mix.astype(jnp.float32)).astype(dt), N_MIXERS, axis=-1)
    x = x + (g_a * o_a + g_b * o_b + g_c * o_c) @ w_out
    h = rms_norm(x, g_ffn)
    x = x + jnp.square(jax.nn.relu(h @ w_ff1)) @ w_ff2
    return x


def setup_inputs(seed: int = 0) -> dict:
    key = jax.random.key(seed)
    ks = iter(jax.random.split(key, 40))
    L = DEPTH
    res = (2.0 * DEPTH) ** -0.5

    def nrm(shape, scale):
        return jax.random.normal(next(ks), shape, jnp.float32) * scale

    def gain(shape):
        return 1.0 + nrm(shape, 0.02)

    return {
        "x": nrm((BATCH, SEQ, D_MODEL), 1.0),
        "g_mix": gain((L, D_MODEL)),
        "w_in": nrm((L, D_MODEL, D_IN), D_MODEL ** -0.5),
        "g_cq": gain((L, MLA_Q_RANK)),
        "g_ckv": gain((L, MLA_KV_RANK)),
        "w_uq": nrm((L, MLA_Q_RANK, MLA_HEADS * MLA_QK_DIM), MLA_Q_RANK ** -0.5),
        "w_ukv": nrm((L, MLA_KV_RANK, MLA_HEADS * (MLA_NOPE_DIM + MLA_V_DIM)), MLA_KV_RANK ** -0.5),
        "g_q_mla": gain((L, MLA_QK_DIM)),
        "g_k_mla": gain((L, MLA_QK_DIM)),
        "w_o_mla": nrm((L, MLA_HEADS * MLA_V_DIM, D_MODEL), (MLA_HEADS * MLA_V_DIM) ** -0.5),
        "b_glu": nrm((L, 2 * CONV_CH), 0.01),
        "w_dw": nrm((L, CONV_WIDTH, CONV_CH), CONV_WIDTH ** -0.5),
        "b_dw": nrm((L, CONV_CH), 0.01),
        "g_conv_ln": gain((L, CONV_CH)),
        "b_conv_ln": nrm((L, CONV_CH), 0.01),
        "w_conv_out": nrm((L, CONV_CH, D_MODEL), CONV_CH ** -0.5),
        "b_conv_out": nrm((L, D_MODEL), 0.01),
        "pe_cmp_k": nrm((L, CMP_BLOCK, NSA_HEAD_DIM), 0.1),
        "pe_cmp_v": nrm((L, CMP_BLOCK, NSA_HEAD_DIM), 0.1),
        "w_cmp_k1": nrm((L, CMP_BLOCK * NSA_HEAD_DIM, CMP_HIDDEN), (CMP_BLOCK * NSA_HEAD_DIM) ** -0.5),
        "w_cmp_k2": nrm((L, CMP_HIDDEN, NSA_HEAD_DIM), CMP_HIDDEN ** -0.5),
        "w_cmp_v1": nrm((L, CMP_BLOCK * NSA_HEAD_DIM, CMP_HIDDEN), (CMP_BLOCK * NSA_HEAD_DIM) ** -0.5),
        "w_cmp_v2": nrm((L, CMP_HIDDEN, NSA_HEAD_DIM), CMP_HIDDEN ** -0.5),
        "g_q_nsa": gain((L, NSA_HEAD_DIM)),
        "g_k_nsa": gain((L, NSA_HEAD_DIM)),
        "w_o_nsa": nrm((L, NSA_HEADS * NSA_HEAD_DIM, D_MODEL), (NSA_HEADS * NSA_HEAD_DIM) ** -0.5),
        "w_out": nrm((L, D_MODEL, D_MODEL), D_MODEL ** -0.5 * res),
        "g_ffn": gain((L, D_MODEL)),
        "w_ff1": nrm((L, D_MODEL, D_FF), D_MODEL ** -0.5),
        "w_ff2": nrm((L, D_FF, D_MODEL), D_FF ** -0.5 * res),
    }


def reference(x, g_mix, w_in, g_cq, g_ckv, w_uq, w_ukv, g_q_mla, g_k_mla, w_o_mla,
              b_glu, w_dw, b_dw, g_conv_ln, b_conv_ln, w_conv_out, b_conv_out,
              pe_cmp_k, pe_cmp_v, w_cmp_k1, w_cmp_k2, w_cmp_v1, w_cmp_v2,
              g_q_nsa, g_k_nsa, w_o_nsa, w_out, g_ffn, w_ff1, w_ff2):
    pos = jnp.arange(x.shape[1])
    for l in range(DEPTH):
        x = hybrid_layer(x, pos, g_mix[l], w_in[l], g_cq[l], g_ckv[l], w_uq[l], w_ukv[l],
                         g_q_mla[l], g_k_mla[l], w_o_mla[l], b_glu[l], w_dw[l], b_dw[l],
                         g_conv_ln[l], b_conv_ln[l], w_conv_out[l], b_conv_out[l],
                         pe_cmp_k[l], pe_cmp_v[l], w_cmp_k1[l], w_cmp_k2[l], w_cmp_v1[l], w_cmp_v2[l],
                         g_q_nsa[l], g_k_nsa[l], w_o_nsa[l], w_out[l], g_ffn[l], w_ff1[l], w_ff2[l])
    return x
```

```python
import contextlib
import numpy as np
import ml_dtypes
import concourse.bass as bass
import concourse.mybir as mybir
from concourse.bass_utils import run_bass_kernel_spmd

F32 = mybir.dt.float32
BF16 = mybir.dt.bfloat16
AF = mybir.ActivationFunctionType
ALU = mybir.AluOpType
AX = mybir.AxisListType
NPBF = ml_dtypes.bfloat16

SEM_LIMIT = 30000


class DSem:
    def __init__(self, sem):
        self.sem = sem
        self.issued = 0


class Buf:
    def __init__(self, name, t=None):
        self.name = name
        self.t = t
        self.w = {}
        self.r = {}

    def __getitem__(self, idx):
        return self.t[idx]

    def _keys(self, key):
        if key is None:
            return list(set(self.w.keys()) | set(self.r.keys()) | {None})
        return [key, None]

    def deps_read(self, key):
        return [self.w[k] for k in self._keys(key) if k in self.w]

    def deps_write(self, key):
        d = []
        for k in self._keys(key):
            if k in self.w:
                d.append(self.w[k])
            d.extend(self.r.get(k, []))
        return d

    def add_read(self, key, i):
        self.r.setdefault(key, []).append(i)
        if len(self.r[key]) > 24:
            last = {}
            for x in self.r[key]:
                last[(x[0], x[1] if x[0] == 'e' else id(x[1]))] = x
            self.r[key] = list(last.values())

    def set_write(self, key, i):
        if key is None:
            self.w = {None: i}
            self.r = {}
        else:
            self.w[key] = i
            self.r[key] = []


class KB:
    def __init__(self, nc):
        self.nc = nc
        self.eng = {'pe': nc.tensor, 'act': nc.scalar, 'dve': nc.vector, 'pool': nc.gpsimd, 'sp': nc.sync}
        self.stack = contextlib.ExitStack()
        self.sem = {}
        self.cnt = {}
        self.nsem = 0
        self.known = {e: {} for e in self.eng}
        self.dsems = []
        self.allsems = []
        for e in ('pe', 'act', 'dve', 'pool'):
            self._rot(e)
        self.n_ins = 0

    def _newsem(self, name):
        s = self.stack.enter_context(self.nc.semaphore(f"{name}_{self.nsem}"))
        self.nsem += 1
        return s

    def _rot(self, e):
        self.sem[e] = self._newsem("s" + e)
        self.cnt[e] = 0

    def dsem(self, name="d"):
        d = DSem(self._newsem(name))
        self.dsems.append(d)
        return d

    def sb(self, st, name, shape, dt):
        self.nsem += 1
        name = f"sb_{name}_{self.nsem}"
        t = st.enter_context(self.nc.sbuf_tensor(name, list(shape), dt))
        return Buf(name, t)

    def ps(self, st, name, shape, dt=F32):
        self.nsem += 1
        name = f"ps_{name}_{self.nsem}"
        t = st.enter_context(self.nc.psum_tensor(name, list(shape), dt))
        return Buf(name, t)

    def _wait(self, e, dep):
        eng = self.eng[e]
        if dep[0] == 'e':
            _, src, sem, val = dep
            if src == 'pe' and e == 'pe':
                return
            k = id(sem)
            if self.known[e].get(k, 0) >= val:
                return
            eng.wait_ge(sem, val)
            self.known[e][k] = val
        else:
            _, ds, kk = dep
            k = id(ds.sem)
            if self.known[e].get(k, 0) >= kk:
                return
            eng.wait_ge(ds.sem, ds.issued)
            self.known[e][k] = ds.issued

    def _deps(self, e, rd, wr):
        deps = []
        for b, k in rd:
            deps.extend(b.deps_read(k))
        for b, k in wr:
            deps.extend(b.deps_write(k))
        for d in deps:
            self._wait(e, d)

    def _post(self, myid, rd, wr):
        for b, k in rd:
            b.add_read(k, myid)
        for b, k in wr:
            b.set_write(k, myid)

    def op(self, e, fn, rd=(), wr=(), inc=True):
        self._deps(e, rd, wr)
        ins = fn(self.eng[e])
        self.n_ins += 1
        sem = self.sem[e]
        if inc:
            self.cnt[e] += 1
            ins.then_inc(sem, 1)
            myid = ('e', e, sem, self.cnt[e])
            if self.cnt[e] >= SEM_LIMIT:
                self._rot(e)
        else:
            myid = ('e', e, sem, self.cnt[e] + 1)
        self._post(myid, rd, wr)
        return myid

    def dma(self, q, ds, out, in_, rd=(), wr=()):
        self._deps(q, rd, wr)
        ins = self.eng[q].dma_start(out=out, in_=in_)
        self.n_ins += 1
        ds.issued += 16
        ins.then_inc(ds.sem, 16)
        myid = ('d', ds, ds.issued)
        self._post(myid, rd, wr)
        return myid

    def barrier(self, engines=('pe', 'act', 'dve', 'pool', 'sp')):
        for e in engines:
            for src in ('pe', 'act', 'dve', 'pool'):
                if self.cnt[src] > 0:
                    if src == e:
                        pass
                    self._wait_raw(e, self.sem[src], self.cnt[src])
            for ds in self.dsems:
                if ds.issued > 0:
                    self._wait_raw(e, ds.sem, ds.issued)

    def _wait_raw(self, e, sem, val):
        k = id(sem)
        if self.known[e].get(k, 0) >= val:
            return
        self.eng[e].wait_ge(sem, val)
        self.known[e][k] = val


S = 4096
D = 1024
NL = 4
TT = 512
NT = S // TT
EPS = 1e-6
THETA = 500000.0
NEG = -30000.0
C1_CQ, C1_CKV, C1_KR, C1_UA, C1_UG, C1 = 0, 384, 640, 704, 1216, 1728
C2_Q, C2_KC, C2_VC, C2_KS, C2_KW, C2_PP, C2_VS, C2_VW, C2_GN, C2 = \
    0, 512, 640, 768, 896, 1024, 1792, 1920, 2048, 2072
PC_GMIX, PC_GCQ, PC_GCKV, PC_GQM, PC_GKM, PC_BGA, PC_BGG, PC_WDW, PC_BDW, PC_GLN, PC_BLN, PC_BCO, \
    PC_GQN, PC_GQNP, PC_GKN, PC_GKNP, PC_GKCP, PC_GFFN, PC_PEK, PC_PEV, NPC = \
    0, 8, 11, 13, 14, 15, 19, 23, 147, 151, 155, 159, 167, 168, 169, 170, 171, 172, 180, 212, 244


def _host_consts():
    c = {}
    c['ident'] = np.eye(128, dtype=np.float32)
    pos = np.arange(S, dtype=np.float32)
    rt = np.zeros((128, 4, S), np.float32)
    inv = np.power(np.float32(THETA), -np.arange(16, dtype=np.float32) * np.float32(2.0 / 32)).astype(np.float32)
    ang = (pos[None, :] * inv[:, None]).astype(np.float32)
    cos, sin = np.cos(ang).astype(np.float32), np.sin(ang).astype(np.float32)
    rt[64:80, 0], rt[80:96, 0] = cos, cos
    rt[64:80, 1], rt[80:96, 1] = -sin, sin
    rt[96:112, 1], rt[112:128, 1] = -sin, sin
    inv8 = np.power(np.float32(THETA), -np.arange(8, dtype=np.float32) * np.float32(2.0 / 16)).astype(np.float32)

    def nsa_tab(p):
        a = (p[None, :].astype(np.float32) * inv8[:, None]).astype(np.float32)
        cc, ss = np.cos(a).astype(np.float32), np.sin(a).astype(np.float32)
        Cf = np.ones((128, p.shape[0]), np.float32)
        Sf = np.zeros((128, p.shape[0]), np.float32)
        for b in (0, 64):
            Cf[b:b + 8], Cf[b + 8:b + 16] = cc, cc
            Sf[b:b + 8], Sf[b + 8:b + 16] = -ss, ss
        return Cf, Sf
    rt[:, 2], rt[:, 3] = nsa_tab(pos)
    c['rt'] = rt
    cpos = np.arange(256) * 16 + 31
    Cc, Sc = nsa_tab(cpos.astype(np.float32))
    c['rtc'] = np.stack([Cc, Sc], axis=1).astype(np.float32)
    kk = np.arange(128)[:, None]
    qq = np.arange(128)[None, :]
    tri = np.zeros((128, 2, 128), np.float32)
    tri[:, 0] = (kk <= qq)
    tri[:, 1] = (kk > qq)
    c['tri'] = tri.astype(NPBF)
    cidx = np.arange(256).reshape(2, 128).T
    mc = ((cidx[:, :, None] * 16 + 31) <= np.arange(S)[None, None, :]) & (cidx[:, :, None] < 255)
    c['maskc'] = mc.astype(np.float32).astype(NPBF)
    cs = cidx * 16
    ss_ = np.arange(64) * 64
    ov = (cs[:, :, None] < ss_[None, None, :] + 64) & (cs[:, :, None] + 32 > ss_[None, None, :]) & (cidx[:, :, None] < 255)
    ov1 = np.concatenate([ov.astype(np.float32), np.ones((128, 2, 1), np.float32)], axis=2)
    c['ov1'] = ov1.astype(NPBF)
    n = np.arange(64)[:, None, None]
    j = np.arange(32)[None, :, None]
    k2 = np.arange(128)[None, None, :]
    E = (n == 2 * j + k2 // 64).astype(np.float32)
    c['E'] = np.concatenate([E, E], axis=0).astype(NPBF)
    t = (np.arange(32)[None, :] * 128 + np.arange(128)[:, None])
    bt = t // 64
    jb = np.arange(64)[None, None, :]
    forced = (jb == 0) | (jb == bt[:, :, None]) | (jb == bt[:, :, None] - 1)
    fut = jb > bt[:, :, None]
    fb = np.where(fut, -1e9, np.where(forced, 1e6, 0.0)).astype(np.float32)
    c['fb'] = np.ascontiguousarray(np.broadcast_to(fb[:, :, None, :], (128, 32, 2, 64))).astype(np.float32)
    sel = np.zeros((32, 12, 128), np.float32)
    for p in range(4):
        for nb in range(3):
            sel[3 * p + nb, p * 3 + nb, 0:64] = 1.0
            sel[3 * (4 + p) + nb, p * 3 + nb, 64:128] = 1.0
    c['sel'] = sel
    bd = np.zeros((128, 128), np.float32)
    bd[0:64, 0:64] = 1.0
    bd[64:128, 64:128] = 1.0
    c['bd64'] = bd
    return c


def _host_prep(inp):
    f = np.float32
    w_in = inp['w_in']
    Lh = w_in.shape[0]
    cq, ckv, kr = w_in[:, :, 0:384], w_in[:, :, 384:640], w_in[:, :, 640:672]
    u2, qn, kvn = w_in[:, :, 672:1696], w_in[:, :, 1696:2208], w_in[:, :, 2208:2976]
    gn, gm = w_in[:, :, 2976:3000], w_in[:, :, 3000:6072]
    p32 = np.r_[16:32, 0:16]
    p16 = np.r_[8:16, 0:8]
    o = {}
    o['w1'] = np.ascontiguousarray(np.concatenate([cq, ckv, kr, kr[:, :, p32], u2], axis=2))
    qh = qn.reshape(Lh, D, 8, 64)
    kv6 = kvn.reshape(Lh, D, 6, 128)
    z48 = np.zeros((Lh, D, 48), f)

    def pchunk(a0, a1):
        return np.concatenate([a0[:, :, p16], z48, a1[:, :, p16], z48], axis=2)
    PP = [pchunk(qh[:, :, p, :16], qh[:, :, 4 + p, :16]) for p in range(4)]
    PP.append(pchunk(kv6[:, :, 2, 0:16], kv6[:, :, 2, 64:80]))
    PP.append(pchunk(kv6[:, :, 4, 0:16], kv6[:, :, 4, 64:80]))
    qt = [np.concatenate([qh[:, :, p], qh[:, :, 4 + p]], axis=2) for p in range(4)]
    o['w2'] = np.ascontiguousarray(np.concatenate(qt + [kv6[:, :, 0], kv6[:, :, 1], kv6[:, :, 2], kv6[:, :, 4]] + PP + [
                                                        kv6[:, :, 3], kv6[:, :, 5], gn], axis=2))
    o['w3'] = np.ascontiguousarray(gm)
    wuq = inp['w_uq'].reshape(Lh, 384, 8, 96)
    o['wq'] = np.ascontiguousarray(np.concatenate([wuq[..., 32:96], wuq[..., 0:32], wuq[..., 0:32][..., p32]], axis=3).reshape(Lh, 384, 1024))
    wukv = inp['w_ukv'].reshape(Lh, 256, 8, 128)
    o['wkk'] = np.ascontiguousarray(wukv[..., 0:64].reshape(Lh, 256, 512))
    o['wkv'] = np.ascontiguousarray(wukv[..., 64:128].reshape(Lh, 256, 512))
    o['woa'] = inp['w_o_mla']
    o['wob'] = inp['w_conv_out']
    won = inp['w_o_nsa'].reshape(Lh, 8, 64, D)
    o['woc'] = np.ascontiguousarray(np.concatenate([np.concatenate([won[:, p], won[:, 4 + p]], axis=1) for p in range(4)], axis=1))
    o['wout'] = inp['w_out']
    o['wf1'] = inp['w_ff1']
    o['wf2'] = inp['w_ff2']
    for nm, src in (('wck1', 'w_cmp_k1'), ('wcv1', 'w_cmp_v1')):
        a = inp[src].reshape(Lh, 32, 64, 128).transpose(0, 2, 1, 3)
        o[nm] = np.ascontiguousarray(np.concatenate([a, a], axis=1))
    k2 = inp['w_cmp_k2']
    z64 = np.zeros((Lh, 128, 64), f)
    z48 = np.zeros((Lh, 128, 48), f)
    o['wck2'] = np.ascontiguousarray(np.concatenate([z64, k2, z64, z64, k2[:, :, :16][:, :, p16], z48, z64], axis=2))
    o['wcv2'] = inp['w_cmp_v2']
    pc = np.zeros((Lh, 128, NPC), f)

    def colT(v, n):
        return v.reshape(Lh, n, 128).transpose(0, 2, 1)
    pc[:, :, PC_GMIX:PC_GMIX + 8] = colT(inp['g_mix'], 8)
    pc[:, :, PC_GCQ:PC_GCQ + 3] = colT(inp['g_cq'], 3)
    pc[:, :, PC_GCKV:PC_GCKV + 2] = colT(inp['g_ckv'], 2)
    for col, g in ((PC_GQM, inp['g_q_mla']), (PC_GKM, inp['g_k_mla'])):
        pc[:, 0:64, col] = g[:, 32:96]
        pc[:, 64:96, col] = g[:, 0:32]
        pc[:, 96:128, col] = g[:, 0:32][:, p32]
    pc[:, :, PC_BGA:PC_BGA + 4] = colT(inp['b_glu'][:, 0:512], 4)
    pc[:, :, PC_BGG:PC_BGG + 4] = colT(inp['b_glu'][:, 512:1024], 4)
    wd = inp['w_dw'].reshape(Lh, 31, 4, 128)
    pc[:, :, PC_WDW:PC_WDW + 124] = wd.transpose(0, 3, 2, 1).reshape(Lh, 128, 124)
    pc[:, :, PC_BDW:PC_BDW + 4] = colT(inp['b_dw'], 4)
    pc[:, :, PC_GLN:PC_GLN + 4] = colT(inp['g_conv_ln'], 4)
    pc[:, :, PC_BLN:PC_BLN + 4] = colT(inp['b_conv_ln'], 4)
    pc[:, :, PC_BCO:PC_BCO + 8] = colT(inp['b_conv_out'], 8)
    for col, colp, g in ((PC_GQN, PC_GQNP, inp['g_q_nsa']), (PC_GKN, PC_GKNP, inp['g_k_nsa'])):
        pc[:, 0:64, col] = g
        pc[:, 64:128, col] = g
        for s_ in (0, 64):
            pc[:, s_:s_ + 16, colp] = g[:, 0:16][:, p16]
    gk = inp['g_k_nsa']
    pc[:, 0:16, PC_GKCP] = gk[:, 0:16][:, p16]
    pc[:, 64:80, PC_GKCP] = gk[:, 0:16][:, p16]
    pc[:, :, PC_GFFN:PC_GFFN + 8] = colT(inp['g_ffn'], 8)
    for col, pe in ((PC_PEK, inp['pe_cmp_k']), (PC_PEV, inp['pe_cmp_v'])):
        pt = pe.transpose(0, 2, 1)
        pc[:, 0:64, col:col + 32] = pt
        pc[:, 64:128, col:col + 32] = pt
    o['pcol'] = pc
    return {k_: np.ascontiguousarray(v, dtype=v.dtype) for k_, v in o.items()}


class Rot:
    def __init__(self, k, st, name, shape, dt, n, ds=None):
        self.bufs = [k.sb(st, f"{name}{i}", shape, dt) for i in range(n)]
        self.ds = ds
        self.i = -1

    def next(self):
        self.i = (self.i + 1) % len(self.bufs)
        return self.bufs[self.i]

    def d(self):
        return self.ds[self.i]


def build(nl=NL, dbg=False, phases=None):
    ES = contextlib.ExitStack
    nc = bass.Bass("TRN2", target_bir_lowering=False)

    def din(name, shape, dt=F32):
        return nc.dram_tensor(name, list(shape), dt, kind="ExternalInput").ap()

    def scr(name, shape, dt):
        return nc.dram_tensor(name, list(shape), dt, kind="ExternalOutput" if dbg else "Internal").ap()

    x_d = din("x", [S, D])
    out_d = nc.dram_tensor("out", [S, D], F32, kind="ExternalOutput").ap()
    cst = {}
    for nm, shp, dt in (("ident", [128, 128], F32), ("rt", [128, 4, S], F32), ("rtc", [128, 2, 256], F32),
                        ("tri", [128, 2, 128], BF16), ("maskc", [128, 2, S], BF16), ("ov1", [128, 2, 65], BF16),
                        ("E", [128, 32, 128], BF16), ("fb", [128, 32, 2, 64], F32), ("sel", [32, 12, 128], F32),
                        ("bd64", [128, 128], F32)):
        cst[nm] = din(nm, shp, dt)
    Lh = NL
    wd = {}
    for nm, shp in (("w1", [Lh, D, C1]), ("w2", [Lh, D, C2]), ("w3", [Lh, D, 3072]), ("wq", [Lh, 384, 1024]),
                    ("wkk", [Lh, 256, 512]), ("wkv", [Lh, 256, 512]), ("woa", [Lh, 512, D]), ("wob", [Lh, 512, D]),
                    ("woc", [Lh, 512, D]), ("wout", [Lh, D, D]), ("wf1", [Lh, D, 4096]), ("wf2", [Lh, 4096, D]),
                    ("wck1", [Lh, 128, 32, 128]), ("wcv1", [Lh, 128, 32, 128]), ("wck2", [Lh, 128, 384]),
                    ("wcv2", [Lh, 128, 64]), ("pcol", [Lh, 128, NPC])):
        wd[nm] = din(nm, shp)
    xT = scr("xT", [D, S], F32)
    hT = scr("hT", [D, S], BF16)
    h2T = scr("h2T", [D, S], BF16)
    qTm = scr("qTm", [8, 96, S], BF16)
    kTm = scr("kTm", [8, 96, S], BF16)
    vml = scr("vml", [S, 8, 128], BF16)
    uT = scr("uT", [4, 128, 32 + S], BF16)
    oTa = scr("oTa", [4, 128, S], BF16)
    caT = scr("caT", [4, 128, S], BF16)
    qTn = scr("qTn", [4, 128, S], BF16)
    ksT = scr("ksT", [128, S], BF16)
    kwT = scr("kwT", [128, S], BF16)
    kcT = scr("kcT", [128, S], BF16)
    vcT = scr("vcT", [128, S], BF16)
    vsw = scr("vsw", [S, 2, 192], BF16)
    gsT = scr("gsT", [32, S], F32)
    onT = scr("onT", [4, 128, S], BF16)
    xT_v = xT.rearrange("(c p) t -> p c t", p=128)
    hT_v = hT.rearrange("(c p) t -> p c t", p=128)
    h2T_v = h2T.rearrange("(c p) t -> p c t", p=128)

    k = KB(nc)
    R = lambda *bs: [(b, None) for b in bs]

    def A(fn, rd, wr):
        return k.op('act', fn, R(*rd), R(*wr))

    def V(fn, rd, wr):
        return k.op('dve', fn, R(*rd), R(*wr))

    def G(fn, rd, wr):
        return k.op('pool', fn, R(*rd), R(*wr))

    def P(fn, rd, wr, inc=True):
        return k.op('pe', fn, R(*rd), R(*wr), inc=inc)

    def LD(ds, buf, dst, src):
        return k.dma('sp', ds, dst, src, wr=R(buf))

    def STO(ds, buf, dst, src):
        return k.dma('sp', ds, dst, src, rd=R(buf))

    def cp(e, out, in_, rd, wr):
        if e == 'act':
            return A(lambda en: en.activation(out=out, in_=in_, func=AF.Copy), rd, wr)
        return k.op(e, lambda en: en.tensor_copy(out, in_), R(*rd), R(*wr))

    def run(ph):
        return phases is None or ph in phases

    with k.stack, ES() as gst, nc.allow_low_precision(reason="fp32r-rounded stat tiles feed single-pass fp32r ones-matmuls"):
        ident_f = k.sb(gst, "ident_f", [128, 128], F32)
        ident_b = k.sb(gst, "ident_b", [128, 128], BF16)
        ones_f = k.sb(gst, "ones_f", [128, 128], F32)
        bd64 = k.sb(gst, "bd64", [128, 128], F32)
        tri = k.sb(gst, "tri", [128, 2, 128], BF16)
        pcol = k.sb(gst, "pcol", [128, NPC], F32)
        epsc = k.sb(gst, "epsc", [128, 1], F32)
        pb = [k.ps(gst, f"pb{i}", [128, 512]) for i in range(8)]
        dsp = [k.dsem(f"dq{i}") for i in range(28)]
        dsc = k.dsem("dconst")
        LD(dsc, ident_f, ident_f[:], cst["ident"][:, :])
        LD(dsc, bd64, bd64[:], cst["bd64"][:, :])
        LD(dsc, tri, tri[:], cst["tri"][:, :, :])
        V(lambda e: e.tensor_copy(ident_b[:], ident_f[:]), [ident_f], [ident_b])
        V(lambda e: e.memset(ones_f[:], 1.0), [], [ones_f])
        ones_r = k.sb(gst, "ones_r", [128, 128], mybir.dt.float32r)
        bd64_r = k.sb(gst, "bd64_r", [128, 128], mybir.dt.float32r)
        V(lambda e: e.tensor_copy(ones_r[:], ones_f[:]), [ones_f], [ones_r])
        V(lambda e: e.tensor_copy(bd64_r[:], bd64[:]), [bd64], [bd64_r])
        V(lambda e: e.memset(epsc[:], EPS), [], [epsc])

        def pc(col, n=1, rows=slice(0, 128)):
            return pcol[rows, col:col + n]

        conv_rr = [0]

        def load_w(wst, dst, kchunks, ncols, src_fn, scale_col=None, col0=0):
            WST = 1024
            for kc in range(kchunks):
                for c0 in range(0, ncols, WST):
                    w = min(WST, ncols - c0)
                    stg = wst.next()
                    LD(wst.d(), stg, stg[:, 0:w], src_fn(kc)[:, c0:c0 + w])
                    e = ('dve', 'act')[conv_rr[0] % 2]
                    conv_rr[0] += 1
                    o_ = dst[:, kc, col0 + c0:col0 + c0 + w]
                    sc = 1.0 if scale_col is None else pc(scale_col + kc)
                    rd = [stg] if scale_col is None else [stg, pcol]
                    if e == 'act':
                        A(lambda en: en.activation(out=o_, in_=stg[:, 0:w], func=AF.Copy, scale=sc), rd, [dst])
                    else:
                        k.op(e, lambda en: en.tensor_scalar(out=o_, in0=stg[:, 0:w], scalar1=sc, scalar2=1.0,
                                                            op0=ALU.mult, op1=ALU.mult), R(*rd), R(dst))

        F32R = mybir.dt.float32r

        def stat_mm(out_ap, lhs_ap, rhs_ap, rd, wr, start=True, stop=True, inc=True):
            P(lambda e: e.matmul(out_ap, lhs_ap, rhs_ap, start=start, stop=stop), rd, wr, inc=inc)

        def pipeline(units, nst, hooks=None):
            n = len(units)
            for i in range(n + nst - 1):
                for s_ in range(nst):
                    u = i - s_
                    if 0 <= u < n:
                        units[u][s_]()
                if hooks and i in hooks:
                    hooks[i]()

        def rstd_from(psb, rsb, n, rows=slice(0, 128)):
            A(lambda e: e.activation(out=rsb[rows, :], in_=psb[rows, :], func=AF.Ln, bias=epsc[rows, :], scale=1.0 / n), [psb, epsc], [rsb])
            A(lambda e: e.activation(out=rsb[rows, :], in_=rsb[rows, :], func=AF.Exp, scale=-0.5), [rsb], [rsb])

        def phase0():
            with ES() as st:
                xin = Rot(k, st, "xin", [128, 4, D], F32, 2, dsp[0:2])
                xo = Rot(k, st, "xo", [128, 8, TT], F32, 2, dsp[2:4])
                zt = k.sb(st, "zt", [128, 4, 32], BF16)
                V(lambda e: e.memset(zt[:], 0.0), [], [zt])
                STO(dsp[4], zt, uT.rearrange("c p t -> p c t")[:, :, 0:32], zt[:])
                for T in range(NT):
                    b = xin.next()
                    LD(xin.d(), b, b[:], x_d[T * TT:(T + 1) * TT, :].rearrange("(s p) d -> p s d", p=128))
                    o = xo.next()
                    for c in range(8):
                        for s in range(4):
                            P(lambda e: e.transpose(pb[c][:, s * 128:(s + 1) * 128], b[:, s, c * 128:(c + 1) * 128], ident_f[:]),
                              [b, ident_f], [pb[c]], inc=(s == 3))
                        cp('act' if c % 2 else 'dve', o[:, c, :], pb[c][:], [pb[c]], [o])
                    STO(xo.d(), o, xT_v[:, :, T * TT:(T + 1) * TT], o[:])
                k.barrier()

        def phaseA1(l):
            with ES() as st:
                w1 = k.sb(st, "w1", [128, 8, C1], BF16)
                wq = k.sb(st, "wq", [128, 3, 1024], BF16)
                wkk = k.sb(st, "wkk", [128, 2, 512], BF16)
                wkv = k.sb(st, "wkv", [128, 2, 512], BF16)
                xt = Rot(k, st, "xt", [128, 8, TT], F32, 2, dsp[0:2])
                rtt = Rot(k, st, "rtt", [128, 2, TT], F32, 2, dsp[2:4])
                ht = Rot(k, st, "ht", [128, 8, TT], BF16, 2, dsp[4:6])
                qo = Rot(k, st, "qo", [128, TT], BF16, 3, dsp[6:9])
                ko = Rot(k, st, "ko", [128, TT], BF16, 3, dsp[9:12])
                uo = Rot(k, st, "uo", [128, TT], BF16, 3, dsp[12:15])
                vt = Rot(k, st, "vt", [128, 4, 8, 128], BF16, 2, dsp[15:17])
                wst = Rot(k, st, "wst", [128, 1024], F32, 2, dsp[17:19])
                sqb = k.sb(st, "sqb", [128, 8, TT], F32)
                cqn = k.sb(st, "cqn", [128, 3, TT], BF16)
                ckvn = k.sb(st, "ckvn", [128, 2, TT], BF16)
                krt = k.sb(st, "krt", [128, TT], F32)
                sqr = k.sb(st, "sqr", [128, TT], mybir.dt.float32r)
                ss = Rot(k, st, "ss", [128, TT], mybir.dt.float32r, 2)
                rs = Rot(k, st, "rs", [128, TT], F32, 4)
                sqt = Rot(k, st, "sqt", [128, TT], mybir.dt.float32r, 3)
                yq = Rot(k, st, "yq", [128, TT], F32, 2)
                t1 = Rot(k, st, "t1", [128, TT], F32, 2)
                t2 = Rot(k, st, "t2", [128, TT], F32, 2)
                sg = Rot(k, st, "sg", [128, TT], F32, 2)
                LD(dsp[20], pcol, pcol[:], wd["pcol"][l])
                load_w(wst, w1, 8, C1, lambda kc: wd["w1"][l, kc * 128:(kc + 1) * 128, :], PC_GMIX)
                load_w(wst, wq, 3, 1024, lambda kc: wd["wq"][l, kc * 128:(kc + 1) * 128, :], PC_GCQ)
                load_w(wst, wkk, 2, 512, lambda kc: wd["wkk"][l, kc * 128:(kc + 1) * 128, :], PC_GCKV)
                load_w(wst, wkv, 2, 512, lambda kc: wd["wkv"][l, kc * 128:(kc + 1) * 128, :], PC_GCKV)
                for b_ in vt.bufs:
                    V(lambda e: e.memset(b_[:], 1.0), [], [b_])
                bk = [0]

                def bank():
                    bk[0] = (bk[0] + 1) % 6
                    return pb[bk[0]]

                def pre(T):
                    X = xt.next()
                    LD(xt.d(), X, X[:], xT_v[:, :, T * TT:(T + 1) * TT])
                    RT = rtt.next()
                    LD(rtt.d(), RT, RT[:], cst["rt"][:, 0:2, T * TT:(T + 1) * TT])
                    return X, RT
                def norm(T, X):
                    A(lambda e: e.activation(out=sqb[:], in_=X[:], func=AF.Square), [X], [sqb])
                    s_ = ss.next()
                    V(lambda e: e.tensor_reduce(out=s_[:], in_=sqb[:].rearrange("p c t -> p t c"), axis=AX.X, op=ALU.add), [sqb], [s_])
                    stat_mm(pb[6][:], ones_r[:], s_[:], [ones_r, s_], [pb[6]])
                    r_ = rs.next()
                    rstd_from(pb[6], r_, 1024.0)
                    H = ht.next()
                    V(lambda e: e.tensor_tensor(out=H[:], in0=X[:], in1=r_[:].unsqueeze(1).to_broadcast([128, 8, TT]), op=ALU.mult), [X, r_], [H])
                    STO(ht.d(), H, hT_v[:, :, T * TT:(T + 1) * TT], H[:])
                    return H
                nxt = pre(0)
                Hn = norm(0, nxt[0])
                for T in range(NT):
                    X, RT = nxt
                    H = Hn
                    if T + 1 < NT:
                        nxt = pre(T + 1)
                    tsl = slice(T * TT, (T + 1) * TT)

                    def proj(bnk, col0, m):
                        for kc in range(8):
                            P(lambda e: e.matmul(bnk[0:m, :], w1[:, kc, col0:col0 + m], H[:, kc, :], start=(kc == 0), stop=(kc == 7)),
                              [w1, H], [bnk], inc=(kc == 7))
                    for (c0, nch, dst, n) in ((C1_CQ, 3, cqn, 384.0), (C1_CKV, 2, ckvn, 256.0)):
                        bs = [bank() for _ in range(nch)]
                        for c in range(nch):
                            proj(bs[c], c0 + c * 128, 128)
                            A(lambda e: e.activation(out=sqb[:, c, :], in_=bs[c][:], func=AF.Square), [bs[c]], [sqb])
                        s_ = ss.next()
                        V(lambda e: e.tensor_tensor(out=s_[:], in0=sqb[:, 0, :], in1=sqb[:, 1, :], op=ALU.add), [sqb], [s_])
                        if nch == 3:
                            V(lambda e: e.tensor_tensor(out=s_[:], in0=s_[:], in1=sqb[:, 2, :], op=ALU.add), [sqb, s_], [s_])
                        stat_mm(pb[6][:], ones_r[:], s_[:], [ones_r, s_], [pb[6]])
                        r_ = rs.next()
                        rstd_from(pb[6], r_, n)
                        for c in range(nch):
                            V(lambda e: e.tensor_tensor(out=dst[:, c, :], in0=bs[c][:], in1=r_[:], op=ALU.mult), [bs[c], r_], [dst])
                    bq = bank()
                    proj(bq, C1_KR, 64)
                    A(lambda e: e.activation(out=krt[64:128, :], in_=bq[0:64, :], func=AF.Copy), [bq], [krt])
                    A(lambda e: e.activation(out=sqr[64:96, :], in_=krt[64:96, :], func=AF.Square), [krt], [sqr])
                    for c in range(4):
                        ba, bg = bank(), bank()
                        proj(ba, C1_UA + c * 128, 128)
                        proj(bg, C1_UG + c * 128, 128)
                        g_ = sg.next()
                        A(lambda e: e.activation(out=g_[:], in_=bg[:], func=AF.Sigmoid, bias=pc(PC_BGG + c), scale=1.0), [bg, pcol], [g_])
                        U = uo.next()
                        V(lambda e: e.scalar_tensor_tensor(out=U[:], in0=ba[:], scalar=pc(PC_BGA + c), in1=g_[:], op0=ALU.add, op1=ALU.mult), [ba, pcol, g_], [U])
                        STO(uo.d(), U, uT[c, :, 32 + T * TT:32 + (T + 1) * TT], U[:])
                    units = []
                    for h in range(8):
                        for isq in (True, False):
                            stt = {}

                            def s0(h=h, isq=isq, stt=stt):
                                bq = bank()
                                if isq:
                                    for c in range(3):
                                        P(lambda e: e.matmul(bq[:, :], wq[:, c, h * 128:(h + 1) * 128], cqn[:, c, :], start=(c == 0), stop=(c == 2)),
                                          [wq, cqn], [bq], inc=(c == 2))
                                else:
                                    for c in range(2):
                                        P(lambda e: e.matmul(bq[0:64, :], wkk[:, c, h * 64:(h + 1) * 64], ckvn[:, c, :], start=(c == 0), stop=(c == 1)),
                                          [wkk, ckvn], [bq], inc=(c == 1))
                                q_ = sqt.next()
                                nr = 96 if isq else 64
                                A(lambda e: e.activation(out=q_[0:nr, :], in_=bq[0:nr, :], func=AF.Square), [bq], [q_])
                                stt['bq'], stt['q_'] = bq, q_

                            def s1(h=h, isq=isq, stt=stt):
                                q_ = stt['q_']
                                st_ = pb[6 + (h % 2)]
                                if isq:
                                    stat_mm(st_[:], ones_r[0:96, :], q_[0:96, :], [ones_r, q_], [st_])
                                else:
                                    stat_mm(st_[:], ones_r[0:64, :], q_[0:64, :], [ones_r, q_], [st_], start=True, stop=False, inc=False)
                                    stat_mm(st_[:], ones_r[64:96, :], sqr[64:96, :], [ones_r, sqr], [st_], start=False, stop=True)
                                r_ = rs.next()
                                rstd_from(st_, r_, 96.0)
                                stt['r_'] = r_

                            def s2(h=h, isq=isq, stt=stt):
                                bq, r_ = stt['bq'], stt['r_']
                                O_ = (qo if isq else ko).next()
                                ods = (qo if isq else ko).d()
                                gcol = PC_GQM if isq else PC_GKM
                                y_ = yq.next()
                                if isq:
                                    V(lambda e: e.scalar_tensor_tensor(out=y_[:], in0=bq[:], scalar=pc(gcol), in1=r_[:], op0=ALU.mult, op1=ALU.mult), [bq, pcol, r_], [y_])
                                    A(lambda e: e.activation(out=O_[0:64, :], in_=y_[0:64, :], func=AF.Copy), [y_], [O_])
                                else:
                                    V(lambda e: e.scalar_tensor_tensor(out=O_[0:64, :], in0=bq[0:64, :], scalar=pc(gcol, rows=slice(0, 64)), in1=r_[0:64, :], op0=ALU.mult, op1=ALU.mult), [bq, pcol, r_], [O_])
                                    V(lambda e: e.scalar_tensor_tensor(out=y_[64:128, :], in0=krt[64:128, :], scalar=pc(gcol, rows=slice(64, 128)), in1=r_[64:128, :], op0=ALU.mult, op1=ALU.mult), [krt, pcol, r_], [y_])
                                a_, b_ = t1.next(), t2.next()
                                G(lambda e: e.tensor_tensor(out=a_[64:96, :], in0=y_[64:96, :], in1=RT[64:96, 0, :], op=ALU.mult), [y_, RT], [a_])
                                G(lambda e: e.tensor_tensor(out=b_[64:96, :], in0=y_[96:128, :], in1=RT[96:128, 1, :], op=ALU.mult), [y_, RT], [b_])
                                V(lambda e: e.tensor_tensor(out=O_[64:96, :], in0=a_[64:96, :], in1=b_[64:96, :], op=ALU.add), [a_, b_], [O_])
                                STO(ods, O_, (qTm if isq else kTm)[h, :, tsl], O_[0:96, :])
                            units.append([s0, s1, s2])
                    hooks = {}
                    if T + 1 < NT:
                        def hk(T=T):
                            nonlocal Hn
                            Hn = norm(T + 1, nxt[0])
                        hooks[7] = hk
                    pipeline(units, 3, hooks)
                    VT = vt.next()
                    for s in range(4):
                        bq = bank()
                        for c in range(2):
                            P(lambda e: e.matmul(bq[:, :], ckvn[:, c, s * 128:(s + 1) * 128], wkv[:, c, :], start=(c == 0), stop=(c == 1)),
                              [ckvn, wkv], [bq], inc=(c == 1))
                        bqv = bq[:, :].rearrange("p (h d) -> p h d", h=8)
                        cp('dve', VT[:, s, 0:8:2, 0:64], bqv[:, 0:8:2, :], [bq], [VT])
                        cp('act', VT[:, s, 1:8:2, 64:128], bqv[:, 1:8:2, :], [bq], [VT])
                    STO(vt.d(), VT, vml[T * TT:(T + 1) * TT].rearrange("(s p) h c -> p s h c", p=128), VT[:])
                k.barrier()

        def phaseA2(l):
            with ES() as st:
                w2 = k.sb(st, "w2", [128, 8, C2], BF16)
                ht = Rot(k, st, "ht", [128, 8, TT], BF16, 2, dsp[0:2])
                rtt = Rot(k, st, "rtt", [128, 2, TT], F32, 2, dsp[2:4])
                qo = Rot(k, st, "qo", [128, TT], BF16, 3, dsp[4:7])
                vsb = Rot(k, st, "vsb", [128, 4, 2, 192], BF16, 2, dsp[7:9])
                gso = Rot(k, st, "gso", [32, TT], F32, 2, dsp[9:11])
                wst = Rot(k, st, "wst", [128, 1024], F32, 3, dsp[17:20])
                pa = k.sb(st, "pa", [128, 6, TT], F32)
                ypr = Rot(k, st, "ypr", [128, TT], F32, 2)
                rs = Rot(k, st, "rs", [128, TT], F32, 4)
                sqt = Rot(k, st, "sqt", [128, TT], mybir.dt.float32r, 3)
                yq = Rot(k, st, "yq", [128, TT], F32, 2)
                t1 = Rot(k, st, "t1", [128, TT], F32, 2)
                t2 = Rot(k, st, "t2", [128, TT], F32, 2)
                load_w(wst, w2, 8, C2, lambda kc: wd["w2"][l, kc * 128:(kc + 1) * 128, :], PC_GMIX)
                for b_ in vsb.bufs:
                    V(lambda e: e.memset(b_[:], 1.0), [], [b_])
                for b_ in ypr.bufs:
                    V(lambda e: e.memset(b_[:], 0.0), [], [b_])
                bk = [0]

                def bank():
                    bk[0] = (bk[0] + 1) % 6
                    return pb[bk[0]]

                def pre(T):
                    H = ht.next()
                    LD(ht.d(), H, H[:], hT_v[:, :, T * TT:(T + 1) * TT])
                    RT = rtt.next()
                    LD(rtt.d(), RT, RT[:], cst["rt"][:, 2:4, T * TT:(T + 1) * TT])
                    return H, RT
                nxt = pre(0)
                for T in range(NT):
                    H, RT = nxt
                    if T + 1 < NT:
                        nxt = pre(T + 1)
                    tsl = slice(T * TT, (T + 1) * TT)

                    def proj(bnk, col0, m):
                        for kc in range(8):
                            P(lambda e: e.matmul(bnk[0:m, :], w2[:, kc, col0:col0 + m], H[:, kc, :], start=(kc == 0), stop=(kc == 7)),
                              [w2, H], [bnk], inc=(kc == 7))
                    for i in range(6):
                        b = bank()
                        proj(b, C2_PP + i * 128, 128)
                        cp('act' if i % 2 else 'dve', pa[:, i, :], b[:], [b], [pa])
                    units = [(C2_Q + p * 128, PC_GQN, PC_GQNP, p, 0, qTn[p]) for p in range(4)]
                    units += [(C2_KS, PC_GKN, PC_GKNP, 4, 0, ksT), (C2_KW, PC_GKN, PC_GKNP, 5, 0, kwT)]
                    us = []
                    for ui, (col0, gcol, gpcol, pi, s0_, dst) in enumerate(units):
                        stt = {}

                        def s0(col0=col0, stt=stt):
                            b = bank()
                            proj(b, col0, 128)
                            q_ = sqt.next()
                            A(lambda e: e.activation(out=q_[:], in_=b[:], func=AF.Square), [b], [q_])
                            stt['b'], stt['q_'] = b, q_

                        def s1(ui=ui, stt=stt):
                            q_ = stt['q_']
                            st_ = pb[6 + (ui % 2)]
                            stat_mm(st_[:], bd64_r[:], q_[:], [bd64_r, q_], [st_])
                            r_ = rs.next()
                            rstd_from(st_, r_, 64.0)
                            stt['r_'] = r_

                        def s2(gcol=gcol, gpcol=gpcol, pi=pi, dst=dst, stt=stt):
                            b, r_ = stt['b'], stt['r_']
                            y_ = yq.next()
                            V(lambda e: e.scalar_tensor_tensor(out=y_[:], in0=b[:], scalar=pc(gcol), in1=r_[:], op0=ALU.mult, op1=ALU.mult), [b, pcol, r_], [y_])
                            yp = ypr.next()
                            for ro in (0, 64):
                                sr = slice(ro, ro + 16)
                                V(lambda e: e.scalar_tensor_tensor(out=yp[sr, :], in0=pa[sr, pi, :], scalar=pc(gpcol, rows=sr), in1=r_[sr, :],
                                                                   op0=ALU.mult, op1=ALU.mult), [pa, pcol, r_], [yp])
                            a_, b2 = t1.next(), t2.next()
                            G(lambda e: e.tensor_tensor(out=a_[:], in0=y_[:], in1=RT[:, 0, :], op=ALU.mult), [y_, RT], [a_])
                            G(lambda e: e.tensor_tensor(out=b2[:], in0=yp[:], in1=RT[:, 1, :], op=ALU.mult), [yp, RT], [b2])
                            O_ = qo.next()
                            V(lambda e: e.tensor_tensor(out=O_[:], in0=a_[:], in1=b2[:], op=ALU.add), [a_, b2], [O_])
                            STO(qo.d(), O_, dst[:, tsl], O_[:])
                        us.append([s0, s1, s2])
                    pipeline(us, 3)
                    for i, (c0, dst) in enumerate(((C2_KC, kcT), (C2_VC, vcT))):
                        b = bank()
                        proj(b, c0, 128)
                        O_ = qo.next()
                        cp('act' if i % 2 else 'dve', O_[:], b[:], [b], [O_])
                        STO(qo.d(), O_, dst[:, tsl], O_[:])
                    VS = vsb.next()
                    for s in range(4):
                        b = bank()
                        for kc in range(8):
                            P(lambda e: e.matmul(b[:, 0:256], H[:, kc, s * 128:(s + 1) * 128], w2[:, kc, C2_VS:C2_VS + 256], start=(kc == 0), stop=(kc == 7)),
                              [w2, H], [b], inc=(kc == 7))
                        bv = b[:, 0:256].rearrange("p (a g d) -> p a g d", a=2, g=2)
                        cp('dve', VS[:, s, :, 0:64], bv[:, :, 0, :], [b], [VS])
                        cp('act', VS[:, s, :, 128:192], bv[:, :, 1, :], [b], [VS])
                    STO(vsb.d(), VS, vsw[T * TT:(T + 1) * TT, :, :].rearrange("(s p) a c -> p s a c", p=128), VS[:])
                    b = bank()
                    proj(b, C2_GN, 24)
                    GS = gso.next()
                    A(lambda e: e.activation(out=GS[0:24, :], in_=b[0:24, :], func=AF.Sigmoid), [b], [GS])
                    STO(gso.d(), GS, gsT[0:24, tsl], GS[0:24, :])
                k.barrier()

        def two_block(ap64, step):
            return bass.AP(ap64.tensor, ap64.offset, [list(ap64.ap[0]), [step, 2], [1, 64]])

        def phaseB(l):
            with ES() as st:
                vh = k.sb(st, "vh", [128, 32, 8, 128], BF16)
                qh = Rot(k, st, "qh", [128, S], BF16, 2, dsp[0:2])
                kh = Rot(k, st, "kh", [128, S], BF16, 2, dsp[2:4])
                pt = Rot(k, st, "pt", [128, TT], BF16, 4)
                ot = Rot(k, st, "ot", [128, S], BF16, 2, dsp[4:6])
                rl = Rot(k, st, "rl", [128, TT], F32, 2)
                LD(dsp[6], vh, vh[:], vml.rearrange("(j p) h c -> p j h c", p=128))
                sb_ = pb[0:4]
                obs = pb[4:6]
                SC = 96.0 ** -0.5
                LAG = 2

                def ldqk(h):
                    Q = qh.next()
                    LD(qh.d(), Q, Q[0:96, :], qTm[h])
                    K = kh.next()
                    LD(kh.d(), K, K[0:96, :], kTm[h])
                    return Q, K
                nxt = ldqk(0)
                cnt = 0
                for p in range(4):
                    OT = ot.next()
                    for hh in range(2):
                        h = 2 * p + hh
                        Q, K = nxt
                        if h + 1 < 8:
                            nxt = ldqk(h + 1)
                        for T in range(NT):
                            ob = obs[cnt % 2]
                            cnt += 1
                            nj = 4 * T + 4
                            items = []
                            for step in range(nj + LAG):
                                if step < nj:
                                    j = step
                                    r = j - 4 * T
                                    c0 = 128 * r if r > 0 else 0
                                    sk = sb_[step % 4]
                                    PT = pt.next()
                                    P(lambda e: e.matmul(sk[:, c0:TT], K[0:96, j * 128:(j + 1) * 128], Q[0:96, T * TT + c0:(T + 1) * TT], start=True, stop=True),
                                      [K, Q], [sk])
                                    A(lambda e: e.activation(out=PT[:, c0:TT], in_=sk[:, c0:TT], func=AF.Exp, scale=SC), [sk], [PT])
                                    if r >= 0:
                                        V(lambda e: e.tensor_tensor(out=PT[:, c0:c0 + 128], in0=PT[:, c0:c0 + 128], in1=tri[:, 0, :], op=ALU.mult), [PT, tri], [PT])
                                    items.append((j, c0, PT))
                                if step >= LAG:
                                    j, c0, PT = items[step - LAG]
                                    lt = vh[:, j, h, :]
                                    P(lambda e: e.matmul(ob[:, c0:TT], lt, PT[:, c0:TT], start=(j == 0), stop=(j == nj - 1)), [vh, PT], [ob], inc=(j == nj - 1))
                            r_ = rl.next()
                            osl, lsl = (slice(0, 64), slice(64, 128)) if hh == 0 else (slice(64, 128), slice(0, 64))
                            V(lambda e: e.reciprocal(r_[osl, :], ob[lsl, :]), [ob], [r_])
                            V(lambda e: e.tensor_tensor(out=OT[osl, T * TT:(T + 1) * TT], in0=ob[osl, :], in1=r_[osl, :], op=ALU.mult), [ob, r_], [OT])
                    STO(ot.d(), OT, oTa[p], OT[:])
                k.barrier()

        def phaseC(l):
            with ES() as st:
                dg = k.sb(st, "dg", [128, 4, 31, 128], BF16)
                ut = Rot(k, st, "ut", [128, 4, 544], BF16, 2, dsp[0:2])
                ca = Rot(k, st, "ca", [128, 4, TT], BF16, 2, dsp[2:4])
                dt_ = k.sb(st, "cdt", [128, 4, TT], F32)
                sq_ = k.sb(st, "csq", [128, 4, TT], F32)
                s1 = Rot(k, st, "s1", [128, TT], mybir.dt.float32r, 2)
                rs = Rot(k, st, "rs", [128, TT], F32, 2)
                for c in range(4):
                    for kk in range(31):
                        e_ = 'pool' if (c * 31 + kk) % 2 else 'dve'
                        k.op(e_, lambda e: e.tensor_scalar(out=dg[:, c, kk, :], in0=ident_b[:], scalar1=pc(PC_WDW + c * 31 + kk), scalar2=1.0, op0=ALU.mult, op1=ALU.mult),
                             R(ident_b, pcol), R(dg))

                def pre(T):
                    U = ut.next()
                    LD(ut.d(), U, U[:, :, 0:542], uT.rearrange("c p t -> p c t")[:, :, T * TT + 2:T * TT + 544])
                    return U
                vts = Rot(k, st, "cvt2", [128, 4, TT], F32, 2)

                def part1(T, U):
                    vt_ = vts.next()
                    for c in range(4):
                        b = pb[c]
                        for kk in range(31):
                            P(lambda e: e.matmul(b[:], dg[:, c, kk, :], U[:, c, kk:kk + TT], start=(kk == 0), stop=(kk == 30)), [dg, U], [b], inc=(kk == 30))
                        A(lambda e: e.activation(out=vt_[:, c, :], in_=b[:], func=AF.Identity, bias=pc(PC_BDW + c), scale=1.0), [b, pcol], [vt_])
                    return vt_
                nxt = pre(0)
                vnext = part1(0, nxt)
                for T in range(NT):
                    vt_ = vnext
                    if T + 1 < NT:
                        nxt = pre(T + 1)
                        vnext = part1(T + 1, nxt)
                    s_ = s1.next()
                    V(lambda e: e.tensor_reduce(out=s_[:], in_=vt_[:].rearrange("p c t -> p t c"), axis=AX.X, op=ALU.add), [vt_], [s_])
                    stat_mm(pb[4][:], ones_r[:], s_[:], [ones_r, s_], [pb[4]])
                    V(lambda e: e.scalar_tensor_tensor(out=dt_[:], in0=pb[4][:].unsqueeze(1).to_broadcast([128, 4, TT]), scalar=-1.0 / 512, in1=vt_[:], op0=ALU.mult, op1=ALU.add),
                      [pb[4], vt_], [dt_])
                    A(lambda e: e.activation(out=sq_[:], in_=dt_[:], func=AF.Square), [dt_], [sq_])
                    s_ = s1.next()
                    V(lambda e: e.tensor_reduce(out=s_[:], in_=sq_[:].rearrange("p c t -> p t c"), axis=AX.X, op=ALU.add), [sq_], [s_])
                    stat_mm(pb[5][:], ones_r[:], s_[:], [ones_r, s_], [pb[5]])
                    r_ = rs.next()
                    rstd_from(pb[5], r_, 512.0)
                    V(lambda e: e.tensor_tensor(out=dt_[:], in0=dt_[:], in1=r_[:].unsqueeze(1).to_broadcast([128, 4, TT]), op=ALU.mult), [dt_, r_], [dt_])
                    CA = ca.next()
                    for c in range(4):
                        A(lambda e: e.activation(out=CA[:, c, :], in_=dt_[:, c, :], func=AF.Silu, bias=pc(PC_BLN + c), scale=pc(PC_GLN + c)), [dt_, pcol], [CA])
                    STO(ca.d(), CA, caT.rearrange("c p t -> p c t")[:, :, T * TT:(T + 1) * TT], CA[:])
                k.barrier()

        kcmp_g = k.sb(gst, "kcmp_g", [128, 256], BF16)
        vcmp_g = k.sb(gst, "vcmp_g", [128, 2, 192], BF16)

        def phaseD0(l):
            with ES() as st:
                kcs = k.sb(st, "kcs", [128, S], BF16)
                vcs = k.sb(st, "vcs", [128, S], BF16)
                w1k = k.sb(st, "w1k", [128, 1, 4096], BF16)
                w1v = k.sb(st, "w1v", [128, 1, 4096], BF16)
                wk2 = k.sb(st, "wk2", [128, 1, 384], BF16)
                wv2 = k.sb(st, "wv2", [128, 1, 64], BF16)
                pe = k.sb(st, "pe", [128, 64], BF16)
                hk = k.sb(st, "hk", [128, 2, 256], BF16)
                hv = k.sb(st, "hv", [128, 2, 256], BF16)
                bias = k.sb(st, "cbias", [128, 2], F32)
                rtc = k.sb(st, "rtc", [128, 2, 256], F32)
                q_ = k.sb(st, "cq_", [128, 256], F32)
                r_ = k.sb(st, "cr_", [128, 256], F32)
                y_ = k.sb(st, "cy_", [128, 256], F32)
                yp = k.sb(st, "cyp", [128, 256], F32)
                a_ = k.sb(st, "ca_", [128, 256], F32)
                b_ = k.sb(st, "cb_", [128, 256], F32)
                wst = Rot(k, st, "wst", [128, 1024], F32, 3, dsp[17:20])
                LD(dsp[0], kcs, kcs[:], kcT)
                LD(dsp[1], vcs, vcs[:], vcT)
                LD(dsp[2], rtc, rtc[:], cst["rtc"])
                load_w(wst, w1k, 1, 4096, lambda kc: wd["wck1"][l].rearrange("p a b -> p (a b)"))
                load_w(wst, w1v, 1, 4096, lambda kc: wd["wcv1"][l].rearrange("p a b -> p (a b)"))
                load_w(wst, wk2, 1, 384, lambda kc: wd["wck2"][l])
                load_w(wst, wv2, 1, 64, lambda kc: wd["wcv2"][l])
                V(lambda e: e.tensor_copy(pe[:], pcol[:, PC_PEK:PC_PEK + 64]), [pcol], [pe])
                V(lambda e: e.memset(hk[:], 0.0), [], [hk])
                V(lambda e: e.memset(hv[:], 0.0), [], [hv])
                V(lambda e: e.memset(kcmp_g[:], 0.0), [], [kcmp_g])
                V(lambda e: e.memset(vcmp_g[:], 1.0), [], [vcmp_g])
                V(lambda e: e.memset(yp[:], 0.0), [], [yp])
                for i, (w1, src, hdst) in enumerate(((w1k, kcs, hk), (w1v, vcs, hv))):
                    b = pb[0]
                    for l_ in range(32):
                        P(lambda e: e.matmul(b[:, 0:1], w1[0:64, 0, l_ * 128:(l_ + 1) * 128], pe[0:64, i * 32 + l_:i * 32 + l_ + 1], start=(l_ == 0), stop=(l_ == 31)),
                          [w1, pe], [b], inc=(l_ == 31))
                    cp('dve', bias[:, i:i + 1], b[:, 0:1], [b], [bias])
                    for g in range(2):
                        bn = pb[1 + g]
                        rows = slice(g * 64, (g + 1) * 64)
                        for l_ in range(32):
                            P(lambda e: e.matmul(bn[:, 0:255], w1[rows, 0, l_ * 128:(l_ + 1) * 128], src[rows, l_:l_ + 16 * 254 + 1:16], start=(l_ == 0), stop=(l_ == 31)),
                              [w1, src], [bn], inc=(l_ == 31))
                        A(lambda e: e.activation(out=hdst[:, g, 0:255], in_=bn[:, 0:255], func=AF.Silu, bias=bias[:, i:i + 1], scale=1.0), [bn, bias], [hdst])
                b = pb[3]
                P(lambda e: e.matmul(b[:, 0:255], wk2[:, 0, 64:192], hk[:, 0, 0:255], start=True, stop=False), [wk2, hk], [b], inc=False)
                P(lambda e: e.matmul(b[:, 0:255], wk2[:, 0, 0:128], hk[:, 1, 0:255], start=False, stop=True), [wk2, hk], [b])
                b2 = pb[4]
                P(lambda e: e.matmul(b2[:, 0:255], wk2[:, 0, 256:384], hk[:, 0, 0:255], start=True, stop=False), [wk2, hk], [b2], inc=False)
                P(lambda e: e.matmul(b2[:, 0:255], wk2[:, 0, 192:320], hk[:, 1, 0:255], start=False, stop=True), [wk2, hk], [b2])
                A(lambda e: e.activation(out=q_[:, 0:255], in_=b[:, 0:255], func=AF.Square), [b], [q_])
                P(lambda e: e.matmul(pb[5][:, 0:255], bd64[:], q_[:, 0:255], start=True, stop=True), [bd64, q_], [pb[5]])
                A(lambda e: e.activation(out=r_[:, 0:255], in_=pb[5][:, 0:255], func=AF.Sqrt, bias=epsc[:], scale=1.0 / 64), [pb[5], epsc], [r_])
                V(lambda e: e.reciprocal(r_[:, 0:255], r_[:, 0:255]), [r_], [r_])
                V(lambda e: e.scalar_tensor_tensor(out=y_[:, 0:255], in0=b[:, 0:255], scalar=pc(PC_GKN), in1=r_[:, 0:255], op0=ALU.mult, op1=ALU.mult), [b, pcol, r_], [y_])
                for ro in (0, 64):
                    rr = slice(ro, ro + 16)
                    V(lambda e: e.scalar_tensor_tensor(out=yp[rr, 0:255], in0=b2[rr, 0:255], scalar=pc(PC_GKCP, rows=rr), in1=r_[rr, 0:255], op0=ALU.mult, op1=ALU.mult),
                      [b2, pcol, r_], [yp])
                V(lambda e: e.tensor_tensor(out=a_[:, 0:255], in0=y_[:, 0:255], in1=rtc[:, 0, 0:255], op=ALU.mult), [y_, rtc], [a_])
                V(lambda e: e.tensor_tensor(out=b_[:, 0:255], in0=yp[:, 0:255], in1=rtc[:, 1, 0:255], op=ALU.mult), [yp, rtc], [b_])
                V(lambda e: e.tensor_tensor(out=kcmp_g[:, 0:255], in0=a_[:, 0:255], in1=b_[:, 0:255], op=ALU.add), [a_, b_], [kcmp_g])
                for ct in range(2):
                    bv = pb[6 + ct]
                    for g in range(2):
                        P(lambda e: e.matmul(bv[:, g * 64:(g + 1) * 64], hv[:, g, ct * 128:(ct + 1) * 128], wv2[:, 0, :], start=True, stop=True), [hv, wv2], [bv])
                    cp('dve', vcmp_g[:, ct, 0:64], bv[:, 0:64], [bv], [vcmp_g])
                    cp('act', vcmp_g[:, ct, 128:192], bv[:, 64:128], [bv], [vcmp_g])
                k.barrier()

        def phaseD(l):
            with ES() as st:
                ks = k.sb(st, "ks", [128, S], BF16)
                kw = k.sb(st, "kw", [128, S], BF16)
                vs3 = k.sb(st, "vs3", [128, 32, 2, 192], BF16)
                Et = k.sb(st, "Et", [128, 32, 128], BF16)
                selt = k.sb(st, "selt", [32, 12, 128], F32)
                ov1t = k.sb(st, "ov1t", [128, 2, 65], BF16)
                qn = Rot(k, st, "qn", [128, 4, TT], BF16, 2, dsp[0:2])
                mct = Rot(k, st, "mct", [128, 2, TT], BF16, 2, dsp[2:4])
                fbt = Rot(k, st, "fbt", [128, 4, 2, 64], F32, 2, dsp[4:6])
                gst_ = Rot(k, st, "gst", [32, TT], F32, 2, dsp[6:8])
                onb = Rot(k, st, "onb", [128, TT], BF16, 3, dsp[8:11])
                pt = Rot(k, st, "pt", [128, TT], BF16, 8)
                acc = [k.sb(st, f"acc{p}", [128, TT], F32) for p in range(4)]
                impacc = k.sb(st, "impacc", [128, 4, 2, 64], F32)
                selb = Rot(k, st, "selb", [128, 128], F32, 2)
                selbT = Rot(k, st, "selbT", [128, TT], BF16, 2)
                tmp = Rot(k, st, "tmp", [128, TT], F32, 2)
                coef = Rot(k, st, "coef", [128, TT], F32, 2)
                tmp2 = Rot(k, st, "tmp2", [128, TT], F32, 2)
                osb = Rot(k, st, "osb", [128, TT], F32, 4)
                gbs = Rot(k, st, "gbs", [128, TT], F32, 2)
                m8a = Rot(k, st, "m8a", [128, 8], F32, 2)
                m8b = Rot(k, st, "m8b", [128, 8], F32, 2)
                t64 = Rot(k, st, "t64", [128, 64], F32, 2)
                rl4 = Rot(k, st, "rl4", [128, 4], F32, 2)
                LD(dsp[11], ks, ks[:], ksT)
                LD(dsp[12], kw, kw[:], kwT)
                LD(dsp[13], vs3, vs3[:], vsw.rearrange("(j p) a c -> p j a c", p=128))
                LD(dsp[14], Et, Et[:], cst["E"])
                LD(dsp[15], selt, selt[:], cst["sel"])
                selr = k.sb(st, "selr", [32, 12, 128], mybir.dt.float32r)
                V(lambda e: e.tensor_copy(selr[:], selt[:]), [selt], [selr])
                gsr = Rot(k, st, "gsr", [32, TT], mybir.dt.float32r, 2)
                LD(dsp[16], ov1t, ov1t[:], cst["ov1"])
                for b_ in gst_.bufs:
                    V(lambda e: e.memset(b_[:], 0.0), [], [b_])
                sbk = pb[0:4]
                oA, oB, gb, xb = pb[4], pb[5], pb[6], pb[7]
                qTn_v = qTn.rearrange("r p t -> p r t")

                def pre(T):
                    tsl = slice(T * TT, (T + 1) * TT)
                    QN = qn.next()
                    LD(qn.d(), QN, QN[:], qTn_v[:, :, tsl])
                    MC = mct.next()
                    LD(mct.d(), MC, MC[:], cst["maskc"][:, :, tsl])
                    FB = fbt.next()
                    LD(fbt.d(), FB, FB[:], cst["fb"][:, 4 * T:4 * T + 4, :, :])
                    GS = gst_.next()
                    LD(gst_.d(), GS, GS[0:24, :], gsT[0:24, tsl])
                    GR = gsr.next()
                    V(lambda e: e.tensor_copy(GR[:], GS[:]), [GS], [GR])
                    return QN, MC, FB, GR

                def gpre(p, n, GS):
                    P(lambda e: e.matmul(gb[:, :], selr[0:32, p * 3 + n, :], GS[0:32, :], start=True, stop=True), [selr, GS], [gb])
                    g_ = gbs.next()
                    V(lambda e: e.tensor_copy(g_[:], gb[:]), [gb], [g_])
                    return g_

                def finish(p, n, T, g_, first, last):
                    oAs, oBs = osb.next(), osb.next()
                    V(lambda e: e.tensor_copy(oAs[:], oA[:]), [oA], [oAs])
                    V(lambda e: e.tensor_copy(oBs[:], oB[:]), [oB], [oBs])
                    r_ = tmp.next()
                    V(lambda e: e.tensor_scalar(out=r_[0:64, :], in0=oAs[64:128, :], scalar1=1e-18, scalar2=None, op0=ALU.max), [oAs], [r_])
                    V(lambda e: e.tensor_scalar(out=r_[64:128, :], in0=oBs[0:64, :], scalar1=1e-18, scalar2=None, op0=ALU.max), [oBs], [r_])
                    A(lambda e: e.activation(out=r_[:], in_=r_[:], func=AF.Ln), [r_], [r_])
                    A(lambda e: e.activation(out=r_[:], in_=r_[:], func=AF.Exp, scale=-1.0), [r_], [r_])
                    c_ = coef.next()
                    V(lambda e: e.tensor_tensor(out=c_[:], in0=r_[:], in1=g_[:], op=ALU.mult), [r_, g_], [c_])
                    dst = acc[p] if first else tmp2.next()
                    V(lambda e: e.tensor_tensor(out=dst[0:64, :], in0=oAs[0:64, :], in1=c_[0:64, :], op=ALU.mult), [oAs, c_], [dst])
                    V(lambda e: e.tensor_tensor(out=dst[64:128, :], in0=oBs[64:128, :], in1=c_[64:128, :], op=ALU.mult), [oBs, c_], [dst])
                    if last:
                        O_ = onb.next()
                        V(lambda e: e.tensor_tensor(out=O_[:], in0=acc[p][:], in1=dst[:], op=ALU.add), [acc[p], dst], [O_])
                        STO(onb.d(), O_, onT[p, :, T * TT:(T + 1) * TT], O_[:])
                    elif not first:
                        V(lambda e: e.tensor_tensor(out=acc[p][:], in0=acc[p][:], in1=dst[:], op=ALU.add), [acc[p], dst], [acc[p]])

                nxt = pre(0)
                for T in range(NT):
                    QN, MC, FB, GS = nxt
                    if T + 1 < NT:
                        nxt = pre(T + 1)
                    ncts = 2 if T >= 4 else 1
                    for p in range(4):
                        g_c = gpre(p, 0, GS)
                        PTs = {}
                        for ct in range(ncts):
                            sA, sB = sbk[(2 * ct) % 4], sbk[(2 * ct + 1) % 4]
                            P(lambda e: e.matmul(sA[:, :], kcmp_g[0:64, ct * 128:(ct + 1) * 128], QN[0:64, p, :], start=True, stop=True), [kcmp_g, QN], [sA])
                            P(lambda e: e.matmul(sB[:, :], kcmp_g[64:128, ct * 128:(ct + 1) * 128], QN[64:128, p, :], start=True, stop=True), [kcmp_g, QN], [sB])
                            for hd, sk in ((0, sA), (1, sB)):
                                PT = pt.next()
                                A(lambda e: e.activation(out=PT[:], in_=sk[:], func=AF.Exp, scale=0.125), [sk], [PT])
                                if not (ct == 0 and T >= 5):
                                    V(lambda e: e.tensor_tensor(out=PT[:], in0=PT[:], in1=MC[:, ct, :], op=ALU.mult), [PT, MC], [PT])
                                PTs[(hd, ct)] = PT
                        for ct in range(ncts):
                            last = (ct == ncts - 1)
                            P(lambda e: e.matmul(oA[:, :], vcmp_g[:, ct, 0:128], PTs[(0, ct)][:, :], start=(ct == 0), stop=last), [vcmp_g, PTs[(0, ct)]], [oA], inc=last)
                            P(lambda e: e.matmul(oB[:, :], vcmp_g[:, ct, 64:192], PTs[(1, ct)][:, :], start=(ct == 0), stop=last), [vcmp_g, PTs[(1, ct)]], [oB], inc=last)
                        for half in range(2):
                            for qi in range(2):
                                qs = half * 2 + qi
                                for hd in range(2):
                                    col = (qi * 2 + hd) * 65
                                    for ct in range(ncts):
                                        last = (ct == ncts - 1)
                                        P(lambda e: e.matmul(xb[:, col:col + 65], PTs[(hd, ct)][:, qs * 128:(qs + 1) * 128], ov1t[:, ct, :], start=(ct == 0), stop=last),
                                          [PTs[(hd, ct)], ov1t], [xb], inc=last)
                            r4 = rl4.next()
                            V(lambda e: e.tensor_scalar(out=r4[:, 0:4], in0=xb[:, 64:260:65], scalar1=1e-30, scalar2=None, op0=ALU.max), [xb], [r4])
                            V(lambda e: e.reciprocal(r4[:], r4[:]), [r4], [r4])
                            for qi in range(2):
                                qs = half * 2 + qi
                                for hd in range(2):
                                    col = (qi * 2 + hd) * 65
                                    src1 = FB[:, qs, hd, :] if p == 0 else impacc[:, qs, hd, :]
                                    V(lambda e: e.scalar_tensor_tensor(out=impacc[:, qs, hd, :], in0=xb[:, col:col + 64], scalar=r4[:, qi * 2 + hd:qi * 2 + hd + 1], in1=src1,
                                                                       op0=ALU.mult, op1=ALU.add), [xb, r4, FB, impacc], [impacc])
                        finish(p, 0, T, g_c, True, False)
                    SBT = selbT.next()
                    for qs in range(4):
                        SB = selb.next()
                        for g in range(2):
                            a8, t6, b8 = m8a.next(), t64.next(), m8b.next()
                            V(lambda e: e.max(a8[:], impacc[:, qs, g, :]), [impacc], [a8])
                            V(lambda e: e.match_replace(t6[:], a8[:], impacc[:, qs, g, :], -3.0e38), [a8, impacc], [t6])
                            V(lambda e: e.max(b8[:], t6[:]), [t6], [b8])
                            V(lambda e: e.tensor_scalar(out=SB[:, g * 64:(g + 1) * 64], in0=impacc[:, qs, g, :], scalar1=b8[:, 7:8], scalar2=NEG, op0=ALU.is_lt, op1=ALU.mult),
                              [impacc, b8], [SB])
                        P(lambda e: e.transpose(xb[:, qs * 128:(qs + 1) * 128], SB[:, :], ident_f[:]), [SB, ident_f], [xb])
                    cp('dve', SBT[:], xb[:], [xb], [SBT])
                    for br in (2, 1):
                        for p in range(4):
                            g_b = gpre(p, br, GS)
                            if br == 1:
                                js = list(range(0, 4 * T + 4))
                            else:
                                js = [4 * T] + [j_ for j_ in range(max(0, 4 * T - 4), 4 * T + 4) if j_ != 4 * T]
                            nj = len(js)
                            kt = ks if br == 1 else kw
                            LAG = 1
                            items = []
                            for step in range(nj + LAG):
                                if step < nj:
                                    j = js[step]
                                    if j >= 4 * T:
                                        c0, c1, ti = 128 * (j - 4 * T), TT, 0
                                        tc = c0
                                    elif br == 2:
                                        c0, c1, ti = 0, 128 * (j - 4 * T + 5), 1
                                        tc = c1 - 128
                                    else:
                                        c0, c1, ti, tc = 0, TT, None, None
                                    sks = (sbk[(2 * step) % 4], sbk[(2 * step + 1) % 4])
                                    for hd in range(2):
                                        rows = slice(hd * 64, hd * 64 + 64)
                                        P(lambda e: e.matmul(sks[hd][:, c0:c1], kt[rows, j * 128:(j + 1) * 128], QN[rows, p, c0:c1], start=True, stop=(br == 2)),
                                          [kt, QN], [sks[hd]], inc=(br == 2))
                                    if br == 1:
                                        for hd in range(2):
                                            rows = slice(hd * 64, hd * 64 + 64)
                                            P(lambda e: e.matmul(sks[hd][:, c0:c1], Et[rows, j, :], SBT[rows, c0:c1], start=False, stop=True), [Et, SBT], [sks[hd]])
                                    pts = []
                                    for hd in range(2):
                                        PT = pt.next()
                                        A(lambda e: e.activation(out=PT[:, c0:c1], in_=sks[hd][:, c0:c1], func=AF.Exp, scale=0.125), [sks[hd]], [PT])
                                        if ti is not None:
                                            V(lambda e: e.tensor_tensor(out=PT[:, tc:tc + 128], in0=PT[:, tc:tc + 128], in1=tri[:, ti, :], op=ALU.mult), [PT, tri], [PT])
                                        pts.append(PT)
                                    items.append((j, c0, c1, pts))
                                if step >= LAG:
                                    idx = step - LAG
                                    j, c0, c1, pts = items[idx]
                                    a = 0 if br == 1 else 1
                                    P(lambda e: e.matmul(oA[:, c0:c1], vs3[:, j, a, 0:128], pts[0][:, c0:c1], start=(idx == 0), stop=(idx == nj - 1)), [vs3, pts[0]], [oA], inc=(idx == nj - 1))
                                    P(lambda e: e.matmul(oB[:, c0:c1], vs3[:, j, a, 64:192], pts[1][:, c0:c1], start=(idx == 0), stop=(idx == nj - 1)), [vs3, pts[1]], [oB], inc=(idx == nj - 1))
                            finish(p, br, T, g_b, False, br == 1)
                k.barrier()

        def phaseE(l):
            with ES() as st:
                wg = k.sb(st, "wg", [128, 8, 3072], BF16)
                wos = [k.sb(st, f"wo{i}", [128, 4, D], BF16) for i in range(3)]
                wo = k.sb(st, "wout", [128, 8, D], BF16)
                ht = Rot(k, st, "ht", [128, 8, TT], BF16, 2, dsp[0:2])
                oin = [Rot(k, st, f"oin{i}", [128, 4, TT], BF16, 2, dsp[2 + 2 * i:4 + 2 * i]) for i in range(3)]
                xt = Rot(k, st, "xt", [128, 8, TT], F32, 1, dsp[8:9])
                m = k.sb(st, "m", [128, 8, TT], BF16)
                sg = Rot(k, st, "sg", [128, TT], F32, 2)
                ta = Rot(k, st, "ta", [128, TT], F32, 5)
                wst = Rot(k, st, "wst", [128, 1024], F32, 3, dsp[17:20])
                load_w(wst, wg, 8, 3072, lambda kc: wd["w3"][l, kc * 128:(kc + 1) * 128, :], PC_GMIX)
                for i, nm in enumerate(("woa", "wob", "woc")):
                    load_w(wst, wos[i], 4, D, lambda kc: wd[nm][l, kc * 128:(kc + 1) * 128, :])
                load_w(wst, wo, 8, D, lambda kc: wd["wout"][l, kc * 128:(kc + 1) * 128, :])
                srcs = [oTa.rearrange("c p t -> p c t"), caT.rearrange("c p t -> p c t"), onT.rearrange("c p t -> p c t")]
                bk = [0]

                def bank():
                    bk[0] = (bk[0] + 1) % 8
                    return pb[bk[0]]

                def pre(T):
                    tsl = slice(T * TT, (T + 1) * TT)
                    H = ht.next()
                    LD(ht.d(), H, H[:], hT_v[:, :, tsl])
                    Os = []
                    for i in range(3):
                        O_ = oin[i].next()
                        LD(oin[i].d(), O_, O_[:], srcs[i][:, :, tsl])
                        Os.append(O_)
                    return H, Os
                nxt = pre(0)
                for T in range(NT):
                    H, Os = nxt
                    tsl = slice(T * TT, (T + 1) * TT)
                    X = xt.next()
                    LD(xt.d(), X, X[:], xT_v[:, :, tsl])
                    if T + 1 < NT:
                        nxt = pre(T + 1)
                    for r in range(8):
                        terms = []
                        for xi in range(3):
                            bo = bank()
                            for c in range(4):
                                P(lambda e: e.matmul(bo[:], wos[xi][:, c, r * 128:(r + 1) * 128], Os[xi][:, c, :], start=(c == 0), stop=(c == 3)), [wos[xi], Os[xi]], [bo], inc=(c == 3))
                            bg = bank()
                            for kc in range(8):
                                P(lambda e: e.matmul(bg[:], wg[:, kc, xi * 1024 + r * 128:xi * 1024 + (r + 1) * 128], H[:, kc, :], start=(kc == 0), stop=(kc == 7)), [wg, H], [bg], inc=(kc == 7))
                            g_ = sg.next()
                            A(lambda e: e.activation(out=g_[:], in_=bg[:], func=AF.Sigmoid), [bg], [g_])
                            t_ = ta.next()
                            if xi == 1:
                                V(lambda e: e.scalar_tensor_tensor(out=t_[:], in0=bo[:], scalar=pc(PC_BCO + r), in1=g_[:], op0=ALU.add, op1=ALU.mult), [bo, pcol, g_], [t_])
                            else:
                                V(lambda e: e.tensor_tensor(out=t_[:], in0=bo[:], in1=g_[:], op=ALU.mult), [bo, g_], [t_])
                            terms.append(t_)
                        u_ = ta.next()
                        G(lambda e: e.tensor_tensor(out=u_[:], in0=terms[0][:], in1=terms[1][:], op=ALU.add), [terms[0], terms[1]], [u_])
                        G(lambda e: e.tensor_tensor(out=m[:, r, :], in0=u_[:], in1=terms[2][:], op=ALU.add), [u_, terms[2]], [m])
                    for r2 in range(8):
                        bo = bank()
                        for r in range(8):
                            P(lambda e: e.matmul(bo[:], wo[:, r, r2 * 128:(r2 + 1) * 128], m[:, r, :], start=(r == 0), stop=(r == 7)), [wo, m], [bo], inc=(r == 7))
                        V(lambda e: e.tensor_tensor(out=X[:, r2, :], in0=X[:, r2, :], in1=bo[:], op=ALU.add), [X, bo], [X])
                    STO(xt.d(), X, xT_v[:, :, tsl], X[:])
                k.barrier()

        def phaseF(l, hf, last):
            with ES() as st:
                w1h = k.sb(st, "w1h", [128, 8, 2048], BF16)
                w2h = k.sb(st, "w2h", [128, 16, D], BF16)
                xt = Rot(k, st, "xt", [128, 8, TT], F32, 2, dsp[0:2])
                h2 = Rot(k, st, "h2", [128, 8, TT], BF16, 2, dsp[2:4])
                a = k.sb(st, "a", [128, 16, TT], BF16)
                rl = Rot(k, st, "rl", [128, TT], F32, 2)
                wst = Rot(k, st, "wst", [128, 1024], F32, 4, dsp[17:21])
                if hf == 0:
                    sqb = k.sb(st, "sqb", [128, 8, TT], F32)
                    ss = Rot(k, st, "ss", [128, TT], mybir.dt.float32r, 2)
                    rs = Rot(k, st, "rs", [128, TT], F32, 2)
                fin = last and hf == 1
                if fin:
                    outb = Rot(k, st, "outb", [128, 4, D], F32, 1, dsp[4:5])
                load_w(wst, w1h, 8, 2048, lambda kc: wd["wf1"][l, kc * 128:(kc + 1) * 128, hf * 2048:(hf + 1) * 2048], PC_GFFN)
                load_w(wst, w2h, 16, D, lambda kc: wd["wf2"][l, hf * 2048 + kc * 128:hf * 2048 + (kc + 1) * 128, :])
                bk = [0]

                def bank():
                    bk[0] = (bk[0] + 1) % 8
                    return pb[bk[0]]

                def pre(T):
                    tsl = slice(T * TT, (T + 1) * TT)
                    X = xt.next()
                    LD(xt.d(), X, X[:], xT_v[:, :, tsl])
                    H2 = h2.next()
                    if hf == 1:
                        LD(h2.d(), H2, H2[:], h2T_v[:, :, tsl])
                    return X, H2
                nxt = pre(0)
                for T in range(NT):
                    X, H2 = nxt
                    tsl = slice(T * TT, (T + 1) * TT)
                    if T + 1 < NT:
                        nxt = pre(T + 1)
                    if hf == 0:
                        A(lambda e: e.activation(out=sqb[:], in_=X[:], func=AF.Square), [X], [sqb])
                        s_ = ss.next()
                        V(lambda e: e.tensor_reduce(out=s_[:], in_=sqb[:].rearrange("p c t -> p t c"), axis=AX.X, op=ALU.add), [sqb], [s_])
                        bs = bank()
                        stat_mm(bs[:], ones_r[:], s_[:], [ones_r, s_], [bs])
                        r_ = rs.next()
                        rstd_from(bs, r_, 1024.0)
                        V(lambda e: e.tensor_tensor(out=H2[:], in0=X[:], in1=r_[:].unsqueeze(1).to_broadcast([128, 8, TT]), op=ALU.mult), [X, r_], [H2])
                        STO(h2.d(), H2, h2T_v[:, :, tsl], H2[:])
                    for f in range(16):
                        b = bank()
                        for kc in range(8):
                            P(lambda e: e.matmul(b[:], w1h[:, kc, f * 128:(f + 1) * 128], H2[:, kc, :], start=(kc == 0), stop=(kc == 7)), [w1h, H2], [b], inc=(kc == 7))
                        r_ = rl.next()
                        A(lambda e: e.activation(out=r_[:], in_=b[:], func=AF.Relu), [b], [r_])
                        G(lambda e: e.tensor_tensor(out=a[:, f, :], in0=r_[:], in1=r_[:], op=ALU.mult), [r_], [a])
                    for r2 in range(8):
                        b = bank()
                        for f in range(16):
                            P(lambda e: e.matmul(b[:], w2h[:, f, r2 * 128:(r2 + 1) * 128], a[:, f, :], start=(f == 0), stop=(f == 15)), [w2h, a], [b], inc=(f == 15))
                        V(lambda e: e.tensor_tensor(out=X[:, r2, :], in0=X[:, r2, :], in1=b[:], op=ALU.add), [X, b], [X])
                    if fin:
                        OB = outb.next()
                        for s in range(4):
                            for half in range(2):
                                b = bank()
                                for i in range(4):
                                    r2 = half * 4 + i
                                    P(lambda e: e.transpose(b[:, i * 128:(i + 1) * 128], X[:, r2, s * 128:(s + 1) * 128], ident_f[:]), [X, ident_f], [b], inc=(i == 3))
                                cp('act' if half else 'dve', OB[:, s, half * 512:(half + 1) * 512], b[:], [b], [OB])
                        STO(outb.d(), OB, out_d[tsl, :].rearrange("(s p) d -> p s d", p=128), OB[:])
                    else:
                        STO(xt.d(), X, xT_v[:, :, tsl], X[:])
                k.barrier()

        if run('0'):
            phase0()
        for l in range(nl):
            if run('A1'):
                phaseA1(l)
            else:
                LD(dsp[20], pcol, pcol[:], wd["pcol"][l])
            if run('A2'):
                phaseA2(l)
            if run('B'):
                phaseB(l)
            if run('C'):
                phaseC(l)
            if run('D0'):
                phaseD0(l)
            if run('D'):
                phaseD(l)
            if run('E'):
                phaseE(l)
            if run('F'):
                phaseF(l, 0, False)
                phaseF(l, 1, l == nl - 1)
        k.barrier()
    build.n_ins = k.n_ins
    return nc


_CACHE = {}


def kernel(**inputs):
    inputs = {k_: np.asarray(v) for k_, v in inputs.items()}
    consts = _host_consts()
    prep = _host_prep(inputs)
    nc = build()
    x = inputs['x']
    shared = dict(consts)
    shared.update(prep)
    in_maps = []
    for b in range(8):
        m = dict(shared)
        m['x'] = np.ascontiguousarray(x[b])
        in_maps.append(m)
    res = run_bass_kernel_spmd(nc, in_maps, core_ids=list(range(8)))
    return np.stack([np.asarray(r['out'], dtype=np.float32) for r in res.results], axis=0)
```

```python
import contextlib
import numpy as np
import ml_dtypes
import concourse.bass as bass
import concourse.mybir as mybir
from concourse.bass_utils import run_bass_kernel_spmd

F32 = mybir.dt.float32
BF16 = mybir.dt.bfloat16
AF = mybir.ActivationFunctionType
ALU = mybir.AluOpType
AX = mybir.AxisListType
NPBF = ml_dtypes.bfloat16

SEM_LIMIT = 30000


class DSem:
    def __init__(self, sem):
        self.sem = sem
        self.issued = 0


class Buf:
    def __init__(self, name, t=None):
        self.name = name
        self.t = t
        self.w = {}
        self.r = {}

    def __getitem__(self, idx):
        return self.t[idx]

    def _keys(self, key):
        if key is None:
            return list(set(self.w.keys()) | set(self.r.keys()) | {None})
        return [key, None]

    def deps_read(self, key):
        return [self.w[k] for k in self._keys(key) if k in self.w]

    def deps_write(self, key):
        d = []
        for k in self._keys(key):
            if k in self.w:
                d.append(self.w[k])
            d.extend(self.r.get(k, []))
        return d

    def add_read(self, key, i):
        self.r.setdefault(key, []).append(i)
        if len(self.r[key]) > 24:
            last = {}
            for x in self.r[key]:
                last[(x[0], x[1] if x[0] == 'e' else id(x[1]))] = x
            self.r[key] = list(last.values())

    def set_write(self, key, i):
        if key is None:
            self.w = {None: i}
            self.r = {}
        else:
            self.w[key] = i
            self.r[key] = []


class KB:
    def __init__(self, nc):
        self.nc = nc
        self.eng = {'pe': nc.tensor, 'act': nc.scalar, 'dve': nc.vector, 'pool': nc.gpsimd, 'sp': nc.sync}
        self.stack = contextlib.ExitStack()
        self.sem = {}
        self.cnt = {}
        self.nsem = 0
        self.known = {e: {} for e in self.eng}
        self.dsems = []
        self.allsems = []
        for e in ('pe', 'act', 'dve', 'pool'):
            self._rot(e)
        self.n_ins = 0

    def _newsem(self, name):
        s = self.stack.enter_context(self.nc.semaphore(f"{name}_{self.nsem}"))
        self.nsem += 1
        return s

    def _rot(self, e):
        self.sem[e] = self._newsem("s" + e)
        self.cnt[e] = 0

    def dsem(self, name="d"):
        d = DSem(self._newsem(name))
        self.dsems.append(d)
        return d

    def sb(self, st, name, shape, dt):
        self.nsem += 1
        name = f"sb_{name}_{self.nsem}"
        t = st.enter_context(self.nc.sbuf_tensor(name, list(shape), dt))
        return Buf(name, t)

    def ps(self, st, name, shape, dt=F32):
        self.nsem += 1
        name = f"ps_{name}_{self.nsem}"
        t = st.enter_context(self.nc.psum_tensor(name, list(shape), dt))
        return Buf(name, t)

    def _wait(self, e, dep):
        eng = self.eng[e]
        if dep[0] == 'e':
            _, src, sem, val = dep
            if src == 'pe' and e == 'pe':
                return
            k = id(sem)
            if self.known[e].get(k, 0) >= val:
                return
            eng.wait_ge(sem, val)
            self.known[e][k] = val
        else:
            _, ds, kk = dep
            k = id(ds.sem)
            if self.known[e].get(k, 0) >= kk:
                return
            eng.wait_ge(ds.sem, ds.issued)
            self.known[e][k] = ds.issued

    def _deps(self, e, rd, wr):
        deps = []
        for b, k in rd:
            deps.extend(b.deps_read(k))
        for b, k in wr:
            deps.extend(b.deps_write(k))
        for d in deps:
            self._wait(e, d)

    def _post(self, myid, rd, wr):
        for b, k in rd:
            b.add_read(k, myid)
        for b, k in wr:
            b.set_write(k, myid)

    def op(self, e, fn, rd=(), wr=(), inc=True):
        self._deps(e, rd, wr)
        ins = fn(self.eng[e])
        self.n_ins += 1
        sem = self.sem[e]
        if inc:
            self.cnt[e] += 1
            ins.then_inc(sem, 1)
            myid = ('e', e, sem, self.cnt[e])
            if self.cnt[e] >= SEM_LIMIT:
                self._rot(e)
        else:
            myid = ('e', e, sem, self.cnt[e] + 1)
        self._post(myid, rd, wr)
        return myid

    def dma(self, q, ds, out, in_, rd=(), wr=()):
        self._deps(q, rd, wr)
        ins = self.eng[q].dma_start(out=out, in_=in_)
        self.n_ins += 1
        ds.issued += 16
        ins.then_inc(ds.sem, 16)
        myid = ('d', ds, ds.issued)
        self._post(myid, rd, wr)
        return myid

    def barrier(self, engines=('pe', 'act', 'dve', 'pool', 'sp')):
        for e in engines:
            for src in ('pe', 'act', 'dve', 'pool'):
                if self.cnt[src] > 0:
                    if src == e:
                        pass
                    self._wait_raw(e, self.sem[src], self.cnt[src])
            for ds in self.dsems:
                if ds.issued > 0:
                    self._wait_raw(e, ds.sem, ds.issued)

    def _wait_raw(self, e, sem, val):
        k = id(sem)
        if self.known[e].get(k, 0) >= val:
            return
        self.eng[e].wait_ge(sem, val)
        self.known[e][k] = val


S = 4096
D = 1024
NL = 4
TT = 512
NT = S // TT
EPS = 1e-6
THETA = 500000.0
NEG = -30000.0
C1_CQ, C1_CKV, C1_KR, C1_UA, C1_UG, C1 = 0, 384, 640, 704, 1216, 1728
C2_Q, C2_KC, C2_VC, C2_KS, C2_KW, C2_PP, C2_VS, C2_VW, C2_GN, C2 = \
    0, 512, 640, 768, 896, 1024, 1792, 1920, 2048, 2072
PC_GMIX, PC_GCQ, PC_GCKV, PC_GQM, PC_GKM, PC_BGA, PC_BGG, PC_WDW, PC_BDW, PC_GLN, PC_BLN, PC_BCO, \
    PC_GQN, PC_GQNP, PC_GKN, PC_GKNP, PC_GKCP, PC_GFFN, PC_PEK, PC_PEV, NPC = \
    0, 8, 11, 13, 14, 15, 19, 23, 147, 151, 155, 159, 167, 168, 169, 170, 171, 172, 180, 212, 244


def _host_consts():
    c = {}
    c['ident'] = np.eye(128, dtype=np.float32)
    pos = np.arange(S, dtype=np.float32)
    rt = np.zeros((128, 4, S), np.float32)
    inv = np.power(np.float32(THETA), -np.arange(16, dtype=np.float32) * np.float32(2.0 / 32)).astype(np.float32)
    ang = (pos[None, :] * inv[:, None]).astype(np.float32)
    cos, sin = np.cos(ang).astype(np.float32), np.sin(ang).astype(np.float32)
    rt[64:80, 0], rt[80:96, 0] = cos, cos
    rt[64:80, 1], rt[80:96, 1] = -sin, sin
    rt[96:112, 1], rt[112:128, 1] = -sin, sin
    inv8 = np.power(np.float32(THETA), -np.arange(8, dtype=np.float32) * np.float32(2.0 / 16)).astype(np.float32)

    def nsa_tab(p):
        a = (p[None, :].astype(np.float32) * inv8[:, None]).astype(np.float32)
        cc, ss = np.cos(a).astype(np.float32), np.sin(a).astype(np.float32)
        Cf = np.ones((128, p.shape[0]), np.float32)
        Sf = np.zeros((128, p.shape[0]), np.float32)
        for b in (0, 64):
            Cf[b:b + 8], Cf[b + 8:b + 16] = cc, cc
            Sf[b:b + 8], Sf[b + 8:b + 16] = -ss, ss
        return Cf, Sf
    rt[:, 2], rt[:, 3] = nsa_tab(pos)
    c['rt'] = rt
    cpos = np.arange(256) * 16 + 31
    Cc, Sc = nsa_tab(cpos.astype(np.float32))
    c['rtc'] = np.stack([Cc, Sc], axis=1).astype(np.float32)
    kk = np.arange(128)[:, None]
    qq = np.arange(128)[None, :]
    tri = np.zeros((128, 2, 128), np.float32)
    tri[:, 0] = (kk <= qq)
    tri[:, 1] = (kk > qq)
    c['tri'] = tri.astype(NPBF)
    cidx = np.arange(256).reshape(2, 128).T
    mc = ((cidx[:, :, None] * 16 + 31) <= np.arange(S)[None, None, :]) & (cidx[:, :, None] < 255)
    c['maskc'] = mc.astype(np.float32).astype(NPBF)
    cs = cidx * 16
    ss_ = np.arange(64) * 64
    ov = (cs[:, :, None] < ss_[None, None, :] + 64) & (cs[:, :, None] + 32 > ss_[None, None, :]) & (cidx[:, :, None] < 255)
    ov1 = np.concatenate([ov.astype(np.float32), np.ones((128, 2, 1), np.float32)], axis=2)
    c['ov1'] = ov1.astype(NPBF)
    n = np.arange(64)[:, None, None]
    j = np.arange(32)[None, :, None]
    k2 = np.arange(128)[None, None, :]
    E = (n == 2 * j + k2 // 64).astype(np.float32)
    c['E'] = np.concatenate([E, E], axis=0).astype(NPBF)
    t = (np.arange(32)[None, :] * 128 + np.arange(128)[:, None])
    bt = t // 64
    jb = np.arange(64)[None, None, :]
    forced = (jb == 0) | (jb == bt[:, :, None]) | (jb == bt[:, :, None] - 1)
    fut = jb > bt[:, :, None]
    fb = np.where(fut, -1e9, np.where(forced, 1e6, 0.0)).astype(np.float32)
    c['fb'] = np.ascontiguousarray(np.broadcast_to(fb[:, :, None, :], (128, 32, 2, 64))).astype(np.float32)
    sel = np.zeros((32, 12, 128), np.float32)
    for p in range(4):
        for nb in range(3):
            sel[3 * p + nb, p * 3 + nb, 0:64] = 1.0
            sel[3 * (4 + p) + nb, p * 3 + nb, 64:128] = 1.0
    c['sel'] = sel
    bd = np.zeros((128, 128), np.float32)
    bd[0:64, 0:64] = 1.0
    bd[64:128, 64:128] = 1.0
    c['bd64'] = bd
    return c


def _host_prep(inp):
    f = np.float32
    w_in = inp['w_in']
    Lh = w_in.shape[0]
    cq, ckv, kr = w_in[:, :, 0:384], w_in[:, :, 384:640], w_in[:, :, 640:672]
    u2, qn, kvn = w_in[:, :, 672:1696], w_in[:, :, 1696:2208], w_in[:, :, 2208:2976]
    gn, gm = w_in[:, :, 2976:3000], w_in[:, :, 3000:6072]
    p32 = np.r_[16:32, 0:16]
    p16 = np.r_[8:16, 0:8]
    o = {}
    o['w1'] = np.ascontiguousarray(np.concatenate([cq, ckv, kr, kr[:, :, p32], u2], axis=2))
    qh = qn.reshape(Lh, D, 8, 64)
    kv6 = kvn.reshape(Lh, D, 6, 128)
    z48 = np.zeros((Lh, D, 48), f)

    def pchunk(a0, a1):
        return np.concatenate([a0[:, :, p16], z48, a1[:, :, p16], z48], axis=2)
    PP = [pchunk(qh[:, :, p, :16], qh[:, :, 4 + p, :16]) for p in range(4)]
    PP.append(pchunk(kv6[:, :, 2, 0:16], kv6[:, :, 2, 64:80]))
    PP.append(pchunk(kv6[:, :, 4, 0:16], kv6[:, :, 4, 64:80]))
    qt = [np.concatenate([qh[:, :, p], qh[:, :, 4 + p]], axis=2) for p in range(4)]
    o['w2'] = np.ascontiguousarray(np.concatenate(qt + [kv6[:, :, 0], kv6[:, :, 1], kv6[:, :, 2], kv6[:, :, 4]] + PP + [
                                                        kv6[:, :, 3], kv6[:, :, 5], gn], axis=2))
    o['w3'] = np.ascontiguousarray(gm)
    wuq = inp['w_uq'].reshape(Lh, 384, 8, 96)
    o['wq'] = np.ascontiguousarray(np.concatenate([wuq[..., 32:96], wuq[..., 0:32], wuq[..., 0:32][..., p32]], axis=3).reshape(Lh, 384, 1024))
    wukv = inp['w_ukv'].reshape(Lh, 256, 8, 128)
    o['wkk'] = np.ascontiguousarray(wukv[..., 0:64].reshape(Lh, 256, 512))
    o['wkv'] = np.ascontiguousarray(wukv[..., 64:128].reshape(Lh, 256, 512))
    o['woa'] = inp['w_o_mla']
    o['wob'] = inp['w_conv_out']
    won = inp['w_o_nsa'].reshape(Lh, 8, 64, D)
    o['woc'] = np.ascontiguousarray(np.concatenate([np.concatenate([won[:, p], won[:, 4 + p]], axis=1) for p in range(4)], axis=1))
    o['wout'] = inp['w_out']
    o['wf1'] = inp['w_ff1']
    o['wf2'] = inp['w_ff2']
    for nm, src in (('wck1', 'w_cmp_k1'), ('wcv1', 'w_cmp_v1')):
        a = inp[src].reshape(Lh, 32, 64, 128).transpose(0, 2, 1, 3)
        o[nm] = np.ascontiguousarray(np.concatenate([a, a], axis=1))
    k2 = inp['w_cmp_k2']
    z64 = np.zeros((Lh, 128, 64), f)
    z48 = np.zeros((Lh, 128, 48), f)
    o['wck2'] = np.ascontiguousarray(np.concatenate([z64, k2, z64, z64, k2[:, :, :16][:, :, p16], z48, z64], axis=2))
    o['wcv2'] = inp['w_cmp_v2']
    pc = np.zeros((Lh, 128, NPC), f)

    def colT(v, n):
        return v.reshape(Lh, n, 128).transpose(0, 2, 1)
    pc[:, :, PC_GMIX:PC_GMIX + 8] = colT(inp['g_mix'], 8)
    pc[:, :, PC_GCQ:PC_GCQ + 3] = colT(inp['g_cq'], 3)
    pc[:, :, PC_GCKV:PC_GCKV + 2] = colT(inp['g_ckv'], 2)
    for col, g in ((PC_GQM, inp['g_q_mla']), (PC_GKM, inp['g_k_mla'])):
        pc[:, 0:64, col] = g[:, 32:96]
        pc[:, 64:96, col] = g[:, 0:32]
        pc[:, 96:128, col] = g[:, 0:32][:, p32]
    pc[:, :, PC_BGA:PC_BGA + 4] = colT(inp['b_glu'][:, 0:512], 4)
    pc[:, :, PC_BGG:PC_BGG + 4] = colT(inp['b_glu'][:, 512:1024], 4)
    wd = inp['w_dw'].reshape(Lh, 31, 4, 128)
    pc[:, :, PC_WDW:PC_WDW + 124] = wd.transpose(0, 3, 2, 1).reshape(Lh, 128, 124)
    pc[:, :, PC_BDW:PC_BDW + 4] = colT(inp['b_dw'], 4)
    pc[:, :, PC_GLN:PC_GLN + 4] = colT(inp['g_conv_ln'], 4)
    pc[:, :, PC_BLN:PC_BLN + 4] = colT(inp['b_conv_ln'], 4)
    pc[:, :, PC_BCO:PC_BCO + 8] = colT(inp['b_conv_out'], 8)
    for col, colp, g in ((PC_GQN, PC_GQNP, inp['g_q_nsa']), (PC_GKN, PC_GKNP, inp['g_k_nsa'])):
        pc[:, 0:64, col] = g
        pc[:, 64:128, col] = g
        for s_ in (0, 64):
            pc[:, s_:s_ + 16, colp] = g[:, 0:16][:, p16]
    gk = inp['g_k_nsa']
    pc[:, 0:16, PC_GKCP] = gk[:, 0:16][:, p16]
    pc[:, 64:80, PC_GKCP] = gk[:, 0:16][:, p16]
    pc[:, :, PC_GFFN:PC_GFFN + 8] = colT(inp['g_ffn'], 8)
    for col, pe in ((PC_PEK, inp['pe_cmp_k']), (PC_PEV, inp['pe_cmp_v'])):
        pt = pe.transpose(0, 2, 1)
        pc[:, 0:64, col:col + 32] = pt
        pc[:, 64:128, col:col + 32] = pt
    o['pcol'] = pc
    return {k_: np.ascontiguousarray(v, dtype=v.dtype) for k_, v in o.items()}


class Rot:
    def __init__(self, k, st, name, shape, dt, n, ds=None):
        self.bufs = [k.sb(st, f"{name}{i}", shape, dt) for i in range(n)]
        self.ds = ds
        self.i = -1

    def next(self):
        self.i = (self.i + 1) % len(self.bufs)
        return self.bufs[self.i]

    def d(self):
        return self.ds[self.i]


def build(nl=NL, dbg=False, phases=None):
    ES = contextlib.ExitStack
    nc = bass.Bass("TRN2", target_bir_lowering=False)

    def din(name, shape, dt=F32):
        return nc.dram_tensor(name, list(shape), dt, kind="ExternalInput").ap()

    def scr(name, shape, dt):
        return nc.dram_tensor(name, list(shape), dt, kind="ExternalOutput" if dbg else "Internal").ap()

    x_d = din("x", [S, D])
    out_d = nc.dram_tensor("out", [S, D], F32, kind="ExternalOutput").ap()
    cst = {}
    for nm, shp, dt in (("ident", [128, 128], F32), ("rt", [128, 4, S], F32), ("rtc", [128, 2, 256], F32),
                        ("tri", [128, 2, 128], BF16), ("maskc", [128, 2, S], BF16), ("ov1", [128, 2, 65], BF16),
                        ("E", [128, 32, 128], BF16), ("fb", [128, 32, 2, 64], F32), ("sel", [32, 12, 128], F32),
                        ("bd64", [128, 128], F32)):
        cst[nm] = din(nm, shp, dt)
    Lh = NL
    wd = {}
    for nm, shp in (("w1", [Lh, D, C1]), ("w2", [Lh, D, C2]), ("w3", [Lh, D, 3072]), ("wq", [Lh, 384, 1024]),
                    ("wkk", [Lh, 256, 512]), ("wkv", [Lh, 256, 512]), ("woa", [Lh, 512, D]), ("wob", [Lh, 512, D]),
                    ("woc", [Lh, 512, D]), ("wout", [Lh, D, D]), ("wf1", [Lh, D, 4096]), ("wf2", [Lh, 4096, D]),
                    ("wck1", [Lh, 128, 32, 128]), ("wcv1", [Lh, 128, 32, 128]), ("wck2", [Lh, 128, 384]),
                    ("wcv2", [Lh, 128, 64]), ("pcol", [Lh, 128, NPC])):
        wd[nm] = din(nm, shp)
    xT = scr("xT", [D, S], F32)
    hT = scr("hT", [D, S], BF16)
    h2T = scr("h2T", [D, S], BF16)
    qTm = scr("qTm", [8, 96, S], BF16)
    kTm = scr("kTm", [8, 96, S], BF16)
    vml = scr("vml", [S, 8, 128], BF16)
    uT = scr("uT", [4, 128, 32 + S], BF16)
    oTa = scr("oTa", [4, 128, S], BF16)
    caT = scr("caT", [4, 128, S], BF16)
    qTn = scr("qTn", [4, 128, S], BF16)
    ksT = scr("ksT", [128, S], BF16)
    kwT = scr("kwT", [128, S], BF16)
    kcT = scr("kcT", [128, S], BF16)
    vcT = scr("vcT", [128, S], BF16)
    vsw = scr("vsw", [S, 2, 192], BF16)
    gsT = scr("gsT", [32, S], F32)
    onT = scr("onT", [4, 128, S], BF16)
    xT_v = xT.rearrange("(c p) t -> p c t", p=128)
    hT_v = hT.rearrange("(c p) t -> p c t", p=128)
    h2T_v = h2T.rearrange("(c p) t -> p c t", p=128)

    k = KB(nc)
    R = lambda *bs: [(b, None) for b in bs]

    def A(fn, rd, wr):
        return k.op('act', fn, R(*rd), R(*wr))

    def V(fn, rd, wr):
        return k.op('dve', fn, R(*rd), R(*wr))

    def G(fn, rd, wr):
        return k.op('pool', fn, R(*rd), R(*wr))

    def P(fn, rd, wr, inc=True):
        return k.op('pe', fn, R(*rd), R(*wr), inc=inc)

    def LD(ds, buf, dst, src):
        return k.dma('sp', ds, dst, src, wr=R(buf))

    def STO(ds, buf, dst, src):
        return k.dma('sp', ds, dst, src, rd=R(buf))

    def cp(e, out, in_, rd, wr):
        if e == 'act':
            return A(lambda en: en.activation(out=out, in_=in_, func=AF.Copy), rd, wr)
        return k.op(e, lambda en: en.tensor_copy(out, in_), R(*rd), R(*wr))

    def run(ph):
        return phases is None or ph in phases

    with k.stack, ES() as gst, nc.allow_low_precision(reason="fp32r-rounded stat tiles feed single-pass fp32r ones-matmuls"):
        ident_f = k.sb(gst, "ident_f", [128, 128], F32)
        ident_b = k.sb(gst, "ident_b", [128, 128], BF16)
        ones_f = k.sb(gst, "ones_f", [128, 128], F32)
        bd64 = k.sb(gst, "bd64", [128, 128], F32)
        tri = k.sb(gst, "tri", [128, 2, 128], BF16)
        pcol = k.sb(gst, "pcol", [128, NPC], F32)
        epsc = k.sb(gst, "epsc", [128, 1], F32)
        pb = [k.ps(gst, f"pb{i}", [128, 512]) for i in range(8)]
        dsp = [k.dsem(f"dq{i}") for i in range(28)]
        dsc = k.dsem("dconst")
        LD(dsc, ident_f, ident_f[:], cst["ident"][:, :])
        LD(dsc, bd64, bd64[:], cst["bd64"][:, :])
        LD(dsc, tri, tri[:], cst["tri"][:, :, :])
        V(lambda e: e.tensor_copy(ident_b[:], ident_f[:]), [ident_f], [ident_b])
        V(lambda e: e.memset(ones_f[:], 1.0), [], [ones_f])
        ones_r = k.sb(gst, "ones_r", [128, 128], mybir.dt.float32r)
        bd64_r = k.sb(gst, "bd64_r", [128, 128], mybir.dt.float32r)
        V(lambda e: e.tensor_copy(ones_r[:], ones_f[:]), [ones_f], [ones_r])
        V(lambda e: e.tensor_copy(bd64_r[:], bd64[:]), [bd64], [bd64_r])
        V(lambda e: e.memset(epsc[:], EPS), [], [epsc])

        def pc(col, n=1, rows=slice(0, 128)):
            return pcol[rows, col:col + n]

        conv_rr = [0]

        def load_w(wst, dst, kchunks, ncols, src_fn, scale_col=None, col0=0):
            WST = 1024
            for kc in range(kchunks):
                for c0 in range(0, ncols, WST):
                    w = min(WST, ncols - c0)
                    stg = wst.next()
                    LD(wst.d(), stg, stg[:, 0:w], src_fn(kc)[:, c0:c0 + w])
                    e = ('dve', 'act')[conv_rr[0] % 2]
                    conv_rr[0] += 1
                    o_ = dst[:, kc, col0 + c0:col0 + c0 + w]
                    sc = 1.0 if scale_col is None else pc(scale_col + kc)
                    rd = [stg] if scale_col is None else [stg, pcol]
                    if e == 'act':
                        A(lambda en: en.activation(out=o_, in_=stg[:, 0:w], func=AF.Copy, scale=sc), rd, [dst])
                    else:
                        k.op(e, lambda en: en.tensor_scalar(out=o_, in0=stg[:, 0:w], scalar1=sc, scalar2=1.0,
                                                            op0=ALU.mult, op1=ALU.mult), R(*rd), R(dst))

        F32R = mybir.dt.float32r

        def stat_mm(out_ap, lhs_ap, rhs_ap, rd, wr, start=True, stop=True, inc=True):
            P(lambda e: e.matmul(out_ap, lhs_ap, rhs_ap, start=start, stop=stop), rd, wr, inc=inc)

        def pipeline(units, nst, hooks=None):
            n = len(units)
            for i in range(n + nst - 1):
                for s_ in range(nst):
                    u = i - s_
                    if 0 <= u < n:
                        units[u][s_]()
                if hooks and i in hooks:
                    hooks[i]()

        def rstd_from(psb, rsb, n, rows=slice(0, 128)):
            A(lambda e: e.activation(out=rsb[rows, :], in_=psb[rows, :], func=AF.Ln, bias=epsc[rows, :], scale=1.0 / n), [psb, epsc], [rsb])
            A(lambda e: e.activation(out=rsb[rows, :], in_=rsb[rows, :], func=AF.Exp, scale=-0.5), [rsb], [rsb])

        def phase0():
            with ES() as st:
                xin = Rot(k, st, "xin", [128, 4, D], F32, 2, dsp[0:2])
                xo = Rot(k, st, "xo", [128, 8, TT], F32, 2, dsp[2:4])
                zt = k.sb(st, "zt", [128, 4, 32], BF16)
                V(lambda e: e.memset(zt[:], 0.0), [], [zt])
                STO(dsp[4], zt, uT.rearrange("c p t -> p c t")[:, :, 0:32], zt[:])
                for T in range(NT):
                    b = xin.next()
                    LD(xin.d(), b, b[:], x_d[T * TT:(T + 1) * TT, :].rearrange("(s p) d -> p s d", p=128))
                    o = xo.next()
                    for c in range(8):
                        for s in range(4):
                            P(lambda e: e.transpose(pb[c][:, s * 128:(s + 1) * 128], b[:, s, c * 128:(c + 1) * 128], ident_f[:]),
                              [b, ident_f], [pb[c]], inc=(s == 3))
                        cp('act' if c % 2 else 'dve', o[:, c, :], pb[c][:], [pb[c]], [o])
                    STO(xo.d(), o, xT_v[:, :, T * TT:(T + 1) * TT], o[:])
                k.barrier()

        def phaseA1(l):
            with ES() as st:
                w1 = k.sb(st, "w1", [128, 8, C1], BF16)
                wq = k.sb(st, "wq", [128, 3, 1024], BF16)
                wkk = k.sb(st, "wkk", [128, 2, 512], BF16)
                wkv = k.sb(st, "wkv", [128, 2, 512], BF16)
                xt = Rot(k, st, "xt", [128, 8, TT], F32, 2, dsp[0:2])
                rtt = Rot(k, st, "rtt", [128, 2, TT], F32, 2, dsp[2:4])
                ht = Rot(k, st, "ht", [128, 8, TT], BF16, 2, dsp[4:6])
                qo = Rot(k, st, "qo", [128, TT], BF16, 3, dsp[6:9])
                ko = Rot(k, st, "ko", [128, TT], BF16, 3, dsp[9:12])
                uo = Rot(k, st, "uo", [128, TT], BF16, 3, dsp[12:15])
                vt = Rot(k, st, "vt", [128, 4, 8, 128], BF16, 2, dsp[15:17])
                wst = Rot(k, st, "wst", [128, 1024], F32, 2, dsp[17:19])
                sqb = k.sb(st, "sqb", [128, 8, TT], F32)
                cqn = k.sb(st, "cqn", [128, 3, TT], BF16)
                ckvn = k.sb(st, "ckvn", [128, 2, TT], BF16)
                krt = k.sb(st, "krt", [128, TT], F32)
                sqr = k.sb(st, "sqr", [128, TT], mybir.dt.float32r)
                ss = Rot(k, st, "ss", [128, TT], mybir.dt.float32r, 2)
                rs = Rot(k, st, "rs", [128, TT], F32, 4)
                sqt = Rot(k, st, "sqt", [128, TT], mybir.dt.float32r, 3)
                yq = Rot(k, st, "yq", [128, TT], F32, 2)
                t1 = Rot(k, st, "t1", [128, TT], F32, 2)
                t2 = Rot(k, st, "t2", [128, TT], F32, 2)
                sg = Rot(k, st, "sg", [128, TT], F32, 2)
                LD(dsp[20], pcol, pcol[:], wd["pcol"][l])
                load_w(wst, w1, 8, C1, lambda kc: wd["w1"][l, kc * 128:(kc + 1) * 128, :], PC_GMIX)
                load_w(wst, wq, 3, 1024, lambda kc: wd["wq"][l, kc * 128:(kc + 1) * 128, :], PC_GCQ)
                load_w(wst, wkk, 2, 512, lambda kc: wd["wkk"][l, kc * 128:(kc + 1) * 128, :], PC_GCKV)
                load_w(wst, wkv, 2, 512, lambda kc: wd["wkv"][l, kc * 128:(kc + 1) * 128, :], PC_GCKV)
                for b_ in vt.bufs:
                    V(lambda e: e.memset(b_[:], 1.0), [], [b_])
                bk = [0]

                def bank():
                    bk[0] = (bk[0] + 1) % 6
                    return pb[bk[0]]

                def pre(T):
                    X = xt.next()
                    LD(xt.d(), X, X[:], xT_v[:, :, T * TT:(T + 1) * TT])
                    RT = rtt.next()
                    LD(rtt.d(), RT, RT[:], cst["rt"][:, 0:2, T * TT:(T + 1) * TT])
                    return X, RT
                def norm(T, X):
                    A(lambda e: e.activation(out=sqb[:], in_=X[:], func=AF.Square), [X], [sqb])
                    s_ = ss.next()
                    V(lambda e: e.tensor_reduce(out=s_[:], in_=sqb[:].rearrange("p c t -> p t c"), axis=AX.X, op=ALU.add), [sqb], [s_])
                    stat_mm(pb[6][:], ones_r[:], s_[:], [ones_r, s_], [pb[6]])
                    r_ = rs.next()
                    rstd_from(pb[6], r_, 1024.0)
                    H = ht.next()
                    V(lambda e: e.tensor_tensor(out=H[:], in0=X[:], in1=r_[:].unsqueeze(1).to_broadcast([128, 8, TT]), op=ALU.mult), [X, r_], [H])
                    STO(ht.d(), H, hT_v[:, :, T * TT:(T + 1) * TT], H[:])
                    return H
                nxt = pre(0)
                Hn = norm(0, nxt[0])
                for T in range(NT):
                    X, RT = nxt
                    H = Hn
                    if T + 1 < NT:
                        nxt = pre(T + 1)
                    tsl = slice(T * TT, (T + 1) * TT)

                    def proj(bnk, col0, m):
                        for kc in range(8):
                            P(lambda e: e.matmul(bnk[0:m, :], w1[:, kc, col0:col0 + m], H[:, kc, :], start=(kc == 0), stop=(kc == 7)),
                              [w1, H], [bnk], inc=(kc == 7))
                    for (c0, nch, dst, n) in ((C1_CQ, 3, cqn, 384.0), (C1_CKV, 2, ckvn, 256.0)):
                        bs = [bank() for _ in range(nch)]
                        for c in range(nch):
                            proj(bs[c], c0 + c * 128, 128)
                            A(lambda e: e.activation(out=sqb[:, c, :], in_=bs[c][:], func=AF.Square), [bs[c]], [sqb])
                        s_ = ss.next()
                        V(lambda e: e.tensor_tensor(out=s_[:], in0=sqb[:, 0, :], in1=sqb[:, 1, :], op=ALU.add), [sqb], [s_])
                        if nch == 3:
                            V(lambda e: e.tensor_tensor(out=s_[:], in0=s_[:], in1=sqb[:, 2, :], op=ALU.add), [sqb, s_], [s_])
                        stat_mm(pb[6][:], ones_r[:], s_[:], [ones_r, s_], [pb[6]])
                        r_ = rs.next()
                        rstd_from(pb[6], r_, n)
                        for c in range(nch):
                            V(lambda e: e.tensor_tensor(out=dst[:, c, :], in0=bs[c][:], in1=r_[:], op=ALU.mult), [bs[c], r_], [dst])
                    bq = bank()
                    proj(bq, C1_KR, 64)
                    A(lambda e: e.activation(out=krt[64:128, :], in_=bq[0:64, :], func=AF.Copy), [bq], [krt])
                    A(lambda e: e.activation(out=sqr[64:96, :], in_=krt[64:96, :], func=AF.Square), [krt], [sqr])
                    for c in range(4):
                        ba, bg = bank(), bank()
                        proj(ba, C1_UA + c * 128, 128)
                        proj(bg, C1_UG + c * 128, 128)
                        g_ = sg.next()
                        A(lambda e: e.activation(out=g_[:], in_=bg[:], func=AF.Sigmoid, bias=pc(PC_BGG + c), scale=1.0), [bg, pcol], [g_])
                        U = uo.next()
                        V(lambda e: e.scalar_tensor_tensor(out=U[:], in0=ba[:], scalar=pc(PC_BGA + c), in1=g_[:], op0=ALU.add, op1=ALU.mult), [ba, pcol, g_], [U])
                        STO(uo.d(), U, uT[c, :, 32 + T * TT:32 + (T + 1) * TT], U[:])
                    units = []
                    for h in range(8):
                        for isq in (True, False):
                            stt = {}

                            def s0(h=h, isq=isq, stt=stt):
                                bq = bank()
                                if isq:
                                    for c in range(3):
                                        P(lambda e: e.matmul(bq[:, :], wq[:, c, h * 128:(h + 1) * 128], cqn[:, c, :], start=(c == 0), stop=(c == 2)),
                                          [wq, cqn], [bq], inc=(c == 2))
                                else:
                                    for c in range(2):
                                        P(lambda e: e.matmul(bq[0:64, :], wkk[:, c, h * 64:(h + 1) * 64], ckvn[:, c, :], start=(c == 0), stop=(c == 1)),
                                          [wkk, ckvn], [bq], inc=(c == 1))
                                q_ = sqt.next()
                                nr = 96 if isq else 64
                                A(lambda e: e.activation(out=q_[0:nr, :], in_=bq[0:nr, :], func=AF.Square), [bq], [q_])
                                stt['bq'], stt['q_'] = bq, q_

                            def s1(h=h, isq=isq, stt=stt):
                                q_ = stt['q_']
                                st_ = pb[6 + (h % 2)]
                                if isq:
                                    stat_mm(st_[:], ones_r[0:96, :], q_[0:96, :], [ones_r, q_], [st_])
                                else:
                                    stat_mm(st_[:], ones_r[0:64, :], q_[0:64, :], [ones_r, q_], [st_], start=True, stop=False, inc=False)
                                    stat_mm(st_[:], ones_r[64:96, :], sqr[64:96, :], [ones_r, sqr], [st_], start=False, stop=True)
                                r_ = rs.next()
                                rstd_from(st_, r_, 96.0)
                                stt['r_'] = r_

                            def s2(h=h, isq=isq, stt=stt):
                                bq, r_ = stt['bq'], stt['r_']
                                O_ = (qo if isq else ko).next()
                                ods = (qo if isq else ko).d()
                                gcol = PC_GQM if isq else PC_GKM
                                y_ = yq.next()
                                if isq:
                                    V(lambda e: e.scalar_tensor_tensor(out=y_[:], in0=bq[:], scalar=pc(gcol), in1=r_[:], op0=ALU.mult, op1=ALU.mult), [bq, pcol, r_], [y_])
                                    A(lambda e: e.activation(out=O_[0:64, :], in_=y_[0:64, :], func=AF.Copy), [y_], [O_])
                                else:
                                    V(lambda e: e.scalar_tensor_tensor(out=O_[0:64, :], in0=bq[0:64, :], scalar=pc(gcol, rows=slice(0, 64)), in1=r_[0:64, :], op0=ALU.mult, op1=ALU.mult), [bq, pcol, r_], [O_])
                                    V(lambda e: e.scalar_tensor_tensor(out=y_[64:128, :], in0=krt[64:128, :], scalar=pc(gcol, rows=slice(64, 128)), in1=r_[64:128, :], op0=ALU.mult, op1=ALU.mult), [krt, pcol, r_], [y_])
                                a_, b_ = t1.next(), t2.next()
                                G(lambda e: e.tensor_tensor(out=a_[64:96, :], in0=y_[64:96, :], in1=RT[64:96, 0, :], op=ALU.mult), [y_, RT], [a_])
                                V(lambda e: e.tensor_tensor(out=b_[64:96, :], in0=y_[96:128, :], in1=RT[96:128, 1, :], op=ALU.mult), [y_, RT], [b_])
                                V(lambda e: e.tensor_tensor(out=O_[64:96, :], in0=a_[64:96, :], in1=b_[64:96, :], op=ALU.add), [a_, b_], [O_])
                                STO(ods, O_, (qTm if isq else kTm)[h, :, tsl], O_[0:96, :])
                            units.append([s0, s1, s2])
                    hooks = {}
                    if T + 1 < NT:
                        def hk(T=T):
                            nonlocal Hn
                            Hn = norm(T + 1, nxt[0])
                        hooks[7] = hk
                    pipeline(units, 3, hooks)
                    VT = vt.next()
                    for s in range(4):
                        bq = bank()
                        for c in range(2):
                            P(lambda e: e.matmul(bq[:, :], ckvn[:, c, s * 128:(s + 1) * 128], wkv[:, c, :], start=(c == 0), stop=(c == 1)),
                              [ckvn, wkv], [bq], inc=(c == 1))
                        bqv = bq[:, :].rearrange("p (h d) -> p h d", h=8)
                        cp('dve', VT[:, s, 0:8:2, 0:64], bqv[:, 0:8:2, :], [bq], [VT])
                        cp('act', VT[:, s, 1:8:2, 64:128], bqv[:, 1:8:2, :], [bq], [VT])
                    STO(vt.d(), VT, vml[T * TT:(T + 1) * TT].rearrange("(s p) h c -> p s h c", p=128), VT[:])
                k.barrier()

        def phaseA2(l):
            with ES() as st:
                w2 = k.sb(st, "w2", [128, 8, C2], BF16)
                ht = Rot(k, st, "ht", [128, 8, TT], BF16, 2, dsp[0:2])
                rtt = Rot(k, st, "rtt", [128, 2, TT], F32, 2, dsp[2:4])
                qo = Rot(k, st, "qo", [128, TT], BF16, 3, dsp[4:7])
                vsb = Rot(k, st, "vsb", [128, 4, 2, 192], BF16, 2, dsp[7:9])
                gso = Rot(k, st, "gso", [32, TT], F32, 2, dsp[9:11])
                wst = Rot(k, st, "wst", [128, 1024], F32, 3, dsp[17:20])
                pa = k.sb(st, "pa", [128, 6, TT], F32)
                ypr = Rot(k, st, "ypr", [128, TT], F32, 2)
                rs = Rot(k, st, "rs", [128, TT], F32, 4)
                sqt = Rot(k, st, "sqt", [128, TT], mybir.dt.float32r, 3)
                yq = Rot(k, st, "yq", [128, TT], F32, 2)
                t1 = Rot(k, st, "t1", [128, TT], F32, 2)
                t2 = Rot(k, st, "t2", [128, TT], F32, 2)
                load_w(wst, w2, 8, C2, lambda kc: wd["w2"][l, kc * 128:(kc + 1) * 128, :], PC_GMIX)
                for b_ in vsb.bufs:
                    V(lambda e: e.memset(b_[:], 1.0), [], [b_])
                for b_ in ypr.bufs:
                    V(lambda e: e.memset(b_[:], 0.0), [], [b_])
                bk = [0]

                def bank():
                    bk[0] = (bk[0] + 1) % 6
                    return pb[bk[0]]

                def pre(T):
                    H = ht.next()
                    LD(ht.d(), H, H[:], hT_v[:, :, T * TT:(T + 1) * TT])
                    RT = rtt.next()
                    LD(rtt.d(), RT, RT[:], cst["rt"][:, 2:4, T * TT:(T + 1) * TT])
                    return H, RT
                nxt = pre(0)
                for T in range(NT):
                    H, RT = nxt
                    if T + 1 < NT:
                        nxt = pre(T + 1)
                    tsl = slice(T * TT, (T + 1) * TT)

                    def proj(bnk, col0, m):
                        for kc in range(8):
                            P(lambda e: e.matmul(bnk[0:m, :], w2[:, kc, col0:col0 + m], H[:, kc, :], start=(kc == 0), stop=(kc == 7)),
                              [w2, H], [bnk], inc=(kc == 7))
                    for i in range(6):
                        b = bank()
                        proj(b, C2_PP + i * 128, 128)
                        cp('act' if i % 2 else 'dve', pa[:, i, :], b[:], [b], [pa])
                    units = [(C2_Q + p * 128, PC_GQN, PC_GQNP, p, 0, qTn[p]) for p in range(4)]
                    units += [(C2_KS, PC_GKN, PC_GKNP, 4, 0, ksT), (C2_KW, PC_GKN, PC_GKNP, 5, 0, kwT)]
                    us = []
                    for ui, (col0, gcol, gpcol, pi, s0_, dst) in enumerate(units):
                        stt = {}

                        def s0(col0=col0, stt=stt):
                            b = bank()
                            proj(b, col0, 128)
                            q_ = sqt.next()
                            A(lambda e: e.activation(out=q_[:], in_=b[:], func=AF.Square), [b], [q_])
                            stt['b'], stt['q_'] = b, q_

                        def s1(ui=ui, stt=stt):
                            q_ = stt['q_']
                            st_ = pb[6 + (ui % 2)]
                            stat_mm(st_[:], bd64_r[:], q_[:], [bd64_r, q_], [st_])
                            r_ = rs.next()
                            rstd_from(st_, r_, 64.0)
                            stt['r_'] = r_

                        def s2(gcol=gcol, gpcol=gpcol, pi=pi, dst=dst, stt=stt):
                            b, r_ = stt['b'], stt['r_']
                            y_ = yq.next()
                            V(lambda e: e.scalar_tensor_tensor(out=y_[:], in0=b[:], scalar=pc(gcol), in1=r_[:], op0=ALU.mult, op1=ALU.mult), [b, pcol, r_], [y_])
                            yp = ypr.next()
                            for ro in (0, 64):
                                sr = slice(ro, ro + 16)
                                V(lambda e: e.scalar_tensor_tensor(out=yp[sr, :], in0=pa[sr, pi, :], scalar=pc(gpcol, rows=sr), in1=r_[sr, :],
                                                                   op0=ALU.mult, op1=ALU.mult), [pa, pcol, r_], [yp])
                            a_, b2 = t1.next(), t2.next()
                            G(lambda e: e.tensor_tensor(out=a_[:], in0=y_[:], in1=RT[:, 0, :], op=ALU.mult), [y_, RT], [a_])
                            G(lambda e: e.tensor_tensor(out=b2[:], in0=yp[:], in1=RT[:, 1, :], op=ALU.mult), [yp, RT], [b2])
                            O_ = qo.next()
                            V(lambda e: e.tensor_tensor(out=O_[:], in0=a_[:], in1=b2[:], op=ALU.add), [a_, b2], [O_])
                            STO(qo.d(), O_, dst[:, tsl], O_[:])
                        us.append([s0, s1, s2])
                    pipeline(us, 3)
                    for i, (c0, dst) in enumerate(((C2_KC, kcT), (C2_VC, vcT))):
                        b = bank()
                        proj(b, c0, 128)
                        O_ = qo.next()
                        cp('act' if i % 2 else 'dve', O_[:], b[:], [b], [O_])
                        STO(qo.d(), O_, dst[:, tsl], O_[:])
                    VS = vsb.next()
                    for s in range(4):
                        b = bank()
                        for kc in range(8):
                            P(lambda e: e.matmul(b[:, 0:256], H[:, kc, s * 128:(s + 1) * 128], w2[:, kc, C2_VS:C2_VS + 256], start=(kc == 0), stop=(kc == 7)),
                              [w2, H], [b], inc=(kc == 7))
                        bv = b[:, 0:256].rearrange("p (a g d) -> p a g d", a=2, g=2)
                        cp('dve', VS[:, s, :, 0:64], bv[:, :, 0, :], [b], [VS])
                        cp('act', VS[:, s, :, 128:192], bv[:, :, 1, :], [b], [VS])
                    STO(vsb.d(), VS, vsw[T * TT:(T + 1) * TT, :, :].rearrange("(s p) a c -> p s a c", p=128), VS[:])
                    b = bank()
                    proj(b, C2_GN, 24)
                    GS = gso.next()
                    A(lambda e: e.activation(out=GS[0:24, :], in_=b[0:24, :], func=AF.Sigmoid), [b], [GS])
                    STO(gso.d(), GS, gsT[0:24, tsl], GS[0:24, :])
                k.barrier()

        def two_block(ap64, step):
            return bass.AP(ap64.tensor, ap64.offset, [list(ap64.ap[0]), [step, 2], [1, 64]])

        def phaseB(l):
            with ES() as st:
                vh = k.sb(st, "vh", [128, 32, 8, 128], BF16)
                qh = Rot(k, st, "qh", [128, S], BF16, 2, dsp[0:2])
                kh = Rot(k, st, "kh", [128, S], BF16, 2, dsp[2:4])
                pt = Rot(k, st, "pt", [128, TT], BF16, 4)
                ot = Rot(k, st, "ot", [128, S], BF16, 2, dsp[4:6])
                rl = Rot(k, st, "rl", [128, TT], F32, 2)
                LD(dsp[6], vh, vh[:], vml.rearrange("(j p) h c -> p j h c", p=128))
                sb_ = pb[0:4]
                obs = pb[4:6]
                SC = 96.0 ** -0.5
                LAG = 2

                def ldqk(h):
                    Q = qh.next()
                    LD(qh.d(), Q, Q[0:96, :], qTm[h])
                    K = kh.next()
                    LD(kh.d(), K, K[0:96, :], kTm[h])
                    return Q, K
                nxt = ldqk(0)
                cnt = 0
                for p in range(4):
                    OT = ot.next()
                    for hh in range(2):
                        h = 2 * p + hh
                        Q, K = nxt
                        if h + 1 < 8:
                            nxt = ldqk(h + 1)
                        for T in range(NT):
                            ob = obs[cnt % 2]
                            cnt += 1
                            nj = 4 * T + 4
                            items = []
                            for step in range(nj + LAG):
                                if step < nj:
                                    j = step
                                    r = j - 4 * T
                                    c0 = 128 * r if r > 0 else 0
                                    sk = sb_[step % 4]
                                    PT = pt.next()
                                    P(lambda e: e.matmul(sk[:, c0:TT], K[0:96, j * 128:(j + 1) * 128], Q[0:96, T * TT + c0:(T + 1) * TT], start=True, stop=True),
                                      [K, Q], [sk])
                                    A(lambda e: e.activation(out=PT[:, c0:TT], in_=sk[:, c0:TT], func=AF.Exp, scale=SC), [sk], [PT])
                                    if r >= 0:
                                        V(lambda e: e.tensor_tensor(out=PT[:, c0:c0 + 128], in0=PT[:, c0:c0 + 128], in1=tri[:, 0, :], op=ALU.mult), [PT, tri], [PT])
                                    items.append((j, c0, PT))
                                if step >= LAG:
                                    j, c0, PT = items[step - LAG]
                                    lt = vh[:, j, h, :]
                                    P(lambda e: e.matmul(ob[:, c0:TT], lt, PT[:, c0:TT], start=(j == 0), stop=(j == nj - 1)), [vh, PT], [ob], inc=(j == nj - 1))
                            r_ = rl.next()
                            osl, lsl = (slice(0, 64), slice(64, 128)) if hh == 0 else (slice(64, 128), slice(0, 64))
                            V(lambda e: e.reciprocal(r_[osl, :], ob[lsl, :]), [ob], [r_])
                            V(lambda e: e.tensor_tensor(out=OT[osl, T * TT:(T + 1) * TT], in0=ob[osl, :], in1=r_[osl, :], op=ALU.mult), [ob, r_], [OT])
                    STO(ot.d(), OT, oTa[p], OT[:])
                k.barrier()

        def phaseC(l):
            with ES() as st:
                dg = k.sb(st, "dg", [128, 4, 31, 128], BF16)
                ut = Rot(k, st, "ut", [128, 4, 544], BF16, 2, dsp[0:2])
                ca = Rot(k, st, "ca", [128, 4, TT], BF16, 2, dsp[2:4])
                dt_ = k.sb(st, "cdt", [128, 4, TT], F32)
                sq_ = k.sb(st, "csq", [128, 4, TT], F32)
                s1 = Rot(k, st, "s1", [128, TT], mybir.dt.float32r, 2)
                rs = Rot(k, st, "rs", [128, TT], F32, 2)
                for c in range(4):
                    for kk in range(31):
                        e_ = 'pool' if (c * 31 + kk) % 2 else 'dve'
                        k.op(e_, lambda e: e.tensor_scalar(out=dg[:, c, kk, :], in0=ident_b[:], scalar1=pc(PC_WDW + c * 31 + kk), scalar2=1.0, op0=ALU.mult, op1=ALU.mult),
                             R(ident_b, pcol), R(dg))

                def pre(T):
                    U = ut.next()
                    LD(ut.d(), U, U[:, :, 0:542], uT.rearrange("c p t -> p c t")[:, :, T * TT + 2:T * TT + 544])
                    return U
                vts = Rot(k, st, "cvt2", [128, 4, TT], F32, 2)

                def part1(T, U):
                    vt_ = vts.next()
                    for c in range(4):
                        b = pb[c]
                        for kk in range(31):
                            P(lambda e: e.matmul(b[:], dg[:, c, kk, :], U[:, c, kk:kk + TT], start=(kk == 0), stop=(kk == 30)), [dg, U], [b], inc=(kk == 30))
                        A(lambda e: e.activation(out=vt_[:, c, :], in_=b[:], func=AF.Identity, bias=pc(PC_BDW + c), scale=1.0), [b, pcol], [vt_])
                    return vt_
                nxt = pre(0)
                vnext = part1(0, nxt)
                for T in range(NT):
                    vt_ = vnext
                    if T + 1 < NT:
                        nxt = pre(T + 1)
                        vnext = part1(T + 1, nxt)
                    s_ = s1.next()
                    V(lambda e: e.tensor_reduce(out=s_[:], in_=vt_[:].rearrange("p c t -> p t c"), axis=AX.X, op=ALU.add), [vt_], [s_])
                    stat_mm(pb[4][:], ones_r[:], s_[:], [ones_r, s_], [pb[4]])
                    V(lambda e: e.scalar_tensor_tensor(out=dt_[:], in0=pb[4][:].unsqueeze(1).to_broadcast([128, 4, TT]), scalar=-1.0 / 512, in1=vt_[:], op0=ALU.mult, op1=ALU.add),
                      [pb[4], vt_], [dt_])
                    A(lambda e: e.activation(out=sq_[:], in_=dt_[:], func=AF.Square), [dt_], [sq_])
                    s_ = s1.next()
                    V(lambda e: e.tensor_reduce(out=s_[:], in_=sq_[:].rearrange("p c t -> p t c"), axis=AX.X, op=ALU.add), [sq_], [s_])
                    stat_mm(pb[5][:], ones_r[:], s_[:], [ones_r, s_], [pb[5]])
                    r_ = rs.next()
                    rstd_from(pb[5], r_, 512.0)
                    V(lambda e: e.tensor_tensor(out=dt_[:], in0=dt_[:], in1=r_[:].unsqueeze(1).to_broadcast([128, 4, TT]), op=ALU.mult), [dt_, r_], [dt_])
                    CA = ca.next()
                    for c in range(4):
                        A(lambda e: e.activation(out=CA[:, c, :], in_=dt_[:, c, :], func=AF.Silu, bias=pc(PC_BLN + c), scale=pc(PC_GLN + c)), [dt_, pcol], [CA])
                    STO(ca.d(), CA, caT.rearrange("c p t -> p c t")[:, :, T * TT:(T + 1) * TT], CA[:])
                k.barrier()

        kcmp_g = k.sb(gst, "kcmp_g", [128, 256], BF16)
        vcmp_g = k.sb(gst, "vcmp_g", [128, 2, 192], BF16)

        def phaseD0(l):
            with ES() as st:
                kcs = k.sb(st, "kcs", [128, S], BF16)
                vcs = k.sb(st, "vcs", [128, S], BF16)
                w1k = k.sb(st, "w1k", [128, 1, 4096], BF16)
                w1v = k.sb(st, "w1v", [128, 1, 4096], BF16)
                wk2 = k.sb(st, "wk2", [128, 1, 384], BF16)
                wv2 = k.sb(st, "wv2", [128, 1, 64], BF16)
                pe = k.sb(st, "pe", [128, 64], BF16)
                hk = k.sb(st, "hk", [128, 2, 256], BF16)
                hv = k.sb(st, "hv", [128, 2, 256], BF16)
                bias = k.sb(st, "cbias", [128, 2], F32)
                rtc = k.sb(st, "rtc", [128, 2, 256], F32)
                q_ = k.sb(st, "cq_", [128, 256], F32)
                r_ = k.sb(st, "cr_", [128, 256], F32)
                y_ = k.sb(st, "cy_", [128, 256], F32)
                yp = k.sb(st, "cyp", [128, 256], F32)
                a_ = k.sb(st, "ca_", [128, 256], F32)
                b_ = k.sb(st, "cb_", [128, 256], F32)
                wst = Rot(k, st, "wst", [128, 1024], F32, 3, dsp[17:20])
                LD(dsp[0], kcs, kcs[:], kcT)
                LD(dsp[1], vcs, vcs[:], vcT)
                LD(dsp[2], rtc, rtc[:], cst["rtc"])
                load_w(wst, w1k, 1, 4096, lambda kc: wd["wck1"][l].rearrange("p a b -> p (a b)"))
                load_w(wst, w1v, 1, 4096, lambda kc: wd["wcv1"][l].rearrange("p a b -> p (a b)"))
                load_w(wst, wk2, 1, 384, lambda kc: wd["wck2"][l])
                load_w(wst, wv2, 1, 64, lambda kc: wd["wcv2"][l])
                V(lambda e: e.tensor_copy(pe[:], pcol[:, PC_PEK:PC_PEK + 64]), [pcol], [pe])
                V(lambda e: e.memset(hk[:], 0.0), [], [hk])
                V(lambda e: e.memset(hv[:], 0.0), [], [hv])
                V(lambda e: e.memset(kcmp_g[:], 0.0), [], [kcmp_g])
                V(lambda e: e.memset(vcmp_g[:], 1.0), [], [vcmp_g])
                V(lambda e: e.memset(yp[:], 0.0), [], [yp])
                for i, (w1, src, hdst) in enumerate(((w1k, kcs, hk), (w1v, vcs, hv))):
                    b = pb[0]
                    for l_ in range(32):
                        P(lambda e: e.matmul(b[:, 0:1], w1[0:64, 0, l_ * 128:(l_ + 1) * 128], pe[0:64, i * 32 + l_:i * 32 + l_ + 1], start=(l_ == 0), stop=(l_ == 31)),
                          [w1, pe], [b], inc=(l_ == 31))
                    cp('dve', bias[:, i:i + 1], b[:, 0:1], [b], [bias])
                    for g in range(2):
                        bn = pb[1 + g]
                        rows = slice(g * 64, (g + 1) * 64)
                        for l_ in range(32):
                            P(lambda e: e.matmul(bn[:, 0:255], w1[rows, 0, l_ * 128:(l_ + 1) * 128], src[rows, l_:l_ + 16 * 254 + 1:16], start=(l_ == 0), stop=(l_ == 31)),
                              [w1, src], [bn], inc=(l_ == 31))
                        A(lambda e: e.activation(out=hdst[:, g, 0:255], in_=bn[:, 0:255], func=AF.Silu, bias=bias[:, i:i + 1], scale=1.0), [bn, bias], [hdst])
                b = pb[3]
                P(lambda e: e.matmul(b[:, 0:255], wk2[:, 0, 64:192], hk[:, 0, 0:255], start=True, stop=False), [wk2, hk], [b], inc=False)
                P(lambda e: e.matmul(b[:, 0:255], wk2[:, 0, 0:128], hk[:, 1, 0:255], start=False, stop=True), [wk2, hk], [b])
                b2 = pb[4]
                P(lambda e: e.matmul(b2[:, 0:255], wk2[:, 0, 256:384], hk[:, 0, 0:255], start=True, stop=False), [wk2, hk], [b2], inc=False)
                P(lambda e: e.matmul(b2[:, 0:255], wk2[:, 0, 192:320], hk[:, 1, 0:255], start=False, stop=True), [wk2, hk], [b2])
                A(lambda e: e.activation(out=q_[:, 0:255], in_=b[:, 0:255], func=AF.Square), [b], [q_])
                P(lambda e: e.matmul(pb[5][:, 0:255], bd64[:], q_[:, 0:255], start=True, stop=True), [bd64, q_], [pb[5]])
                A(lambda e: e.activation(out=r_[:, 0:255], in_=pb[5][:, 0:255], func=AF.Sqrt, bias=epsc[:], scale=1.0 / 64), [pb[5], epsc], [r_])
                V(lambda e: e.reciprocal(r_[:, 0:255], r_[:, 0:255]), [r_], [r_])
                V(lambda e: e.scalar_tensor_tensor(out=y_[:, 0:255], in0=b[:, 0:255], scalar=pc(PC_GKN), in1=r_[:, 0:255], op0=ALU.mult, op1=ALU.mult), [b, pcol, r_], [y_])
                for ro in (0, 64):
                    rr = slice(ro, ro + 16)
                    V(lambda e: e.scalar_tensor_tensor(out=yp[rr, 0:255], in0=b2[rr, 0:255], scalar=pc(PC_GKCP, rows=rr), in1=r_[rr, 0:255], op0=ALU.mult, op1=ALU.mult),
                      [b2, pcol, r_], [yp])
                V(lambda e: e.tensor_tensor(out=a_[:, 0:255], in0=y_[:, 0:255], in1=rtc[:, 0, 0:255], op=ALU.mult), [y_, rtc], [a_])
                V(lambda e: e.tensor_tensor(out=b_[:, 0:255], in0=yp[:, 0:255], in1=rtc[:, 1, 0:255], op=ALU.mult), [yp, rtc], [b_])
                V(lambda e: e.tensor_tensor(out=kcmp_g[:, 0:255], in0=a_[:, 0:255], in1=b_[:, 0:255], op=ALU.add), [a_, b_], [kcmp_g])
                for ct in range(2):
                    bv = pb[6 + ct]
                    for g in range(2):
                        P(lambda e: e.matmul(bv[:, g * 64:(g + 1) * 64], hv[:, g, ct * 128:(ct + 1) * 128], wv2[:, 0, :], start=True, stop=True), [hv, wv2], [bv])
                    cp('dve', vcmp_g[:, ct, 0:64], bv[:, 0:64], [bv], [vcmp_g])
                    cp('act', vcmp_g[:, ct, 128:192], bv[:, 64:128], [bv], [vcmp_g])
                k.barrier()

        def phaseD(l):
            with ES() as st:
                ks = k.sb(st, "ks", [128, S], BF16)
                kw = k.sb(st, "kw", [128, S], BF16)
                vs3 = k.sb(st, "vs3", [128, 32, 2, 192], BF16)
                Et = k.sb(st, "Et", [128, 32, 128], BF16)
                selt = k.sb(st, "selt", [32, 12, 128], F32)
                ov1t = k.sb(st, "ov1t", [128, 2, 65], BF16)
                qn = Rot(k, st, "qn", [128, 4, TT], BF16, 2, dsp[0:2])
                mct = Rot(k, st, "mct", [128, 2, TT], BF16, 2, dsp[2:4])
                fbt = Rot(k, st, "fbt", [128, 4, 2, 64], F32, 2, dsp[4:6])
                gst_ = Rot(k, st, "gst", [32, TT], F32, 2, dsp[6:8])
                onb = Rot(k, st, "onb", [128, TT], BF16, 3, dsp[8:11])
                pt = Rot(k, st, "pt", [128, TT], BF16, 8)
                acc = [k.sb(st, f"acc{p}", [128, TT], F32) for p in range(4)]
                impacc = k.sb(st, "impacc", [128, 4, 2, 64], F32)
                selb = Rot(k, st, "selb", [128, 128], F32, 2)
                selbT = Rot(k, st, "selbT", [128, TT], BF16, 2)
                tmp = Rot(k, st, "tmp", [128, TT], F32, 2)
                coef = Rot(k, st, "coef", [128, TT], F32, 2)
                tmp2 = Rot(k, st, "tmp2", [128, TT], F32, 2)
                osb = Rot(k, st, "osb", [128, TT], F32, 4)
                gbs = Rot(k, st, "gbs", [128, TT], F32, 2)
                m8a = Rot(k, st, "m8a", [128, 8], F32, 2)
                m8b = Rot(k, st, "m8b", [128, 8], F32, 2)
                t64 = Rot(k, st, "t64", [128, 64], F32, 2)
                rl4 = Rot(k, st, "rl4", [128, 4], F32, 2)
                LD(dsp[11], ks, ks[:], ksT)
                LD(dsp[12], kw, kw[:], kwT)
                LD(dsp[13], vs3, vs3[:], vsw.rearrange("(j p) a c -> p j a c", p=128))
                LD(dsp[14], Et, Et[:], cst["E"])
                LD(dsp[15], selt, selt[:], cst["sel"])
                selr = k.sb(st, "selr", [32, 12, 128], mybir.dt.float32r)
                V(lambda e: e.tensor_copy(selr[:], selt[:]), [selt], [selr])
                gsr = Rot(k, st, "gsr", [32, TT], mybir.dt.float32r, 2)
                LD(dsp[16], ov1t, ov1t[:], cst["ov1"])
                for b_ in gst_.bufs:
                    V(lambda e: e.memset(b_[:], 0.0), [], [b_])
                sbk = pb[0:4]
                oA, oB, gb, xb = pb[4], pb[5], pb[6], pb[7]
                qTn_v = qTn.rearrange("r p t -> p r t")

                def pre(T):
                    tsl = slice(T * TT, (T + 1) * TT)
                    QN = qn.next()
                    LD(qn.d(), QN, QN[:], qTn_v[:, :, tsl])
                    MC = mct.next()
                    LD(mct.d(), MC, MC[:], cst["maskc"][:, :, tsl])
                    FB = fbt.next()
                    LD(fbt.d(), FB, FB[:], cst["fb"][:, 4 * T:4 * T + 4, :, :])
                    GS = gst_.next()
                    LD(gst_.d(), GS, GS[0:24, :], gsT[0:24, tsl])
                    GR = gsr.next()
                    V(lambda e: e.tensor_copy(GR[:], GS[:]), [GS], [GR])
                    return QN, MC, FB, GR

                def gpre(p, n, GS):
                    P(lambda e: e.matmul(gb[:, :], selr[0:32, p * 3 + n, :], GS[0:32, :], start=True, stop=True), [selr, GS], [gb])
                    g_ = gbs.next()
                    V(lambda e: e.tensor_copy(g_[:], gb[:]), [gb], [g_])
                    return g_

                def finish(p, n, T, g_, first, last):
                    oAs, oBs = osb.next(), osb.next()
                    V(lambda e: e.tensor_copy(oAs[:], oA[:]), [oA], [oAs])
                    V(lambda e: e.tensor_copy(oBs[:], oB[:]), [oB], [oBs])
                    r_ = tmp.next()
                    V(lambda e: e.tensor_scalar(out=r_[0:64, :], in0=oAs[64:128, :], scalar1=1e-18, scalar2=None, op0=ALU.max), [oAs], [r_])
                    V(lambda e: e.tensor_scalar(out=r_[64:128, :], in0=oBs[0:64, :], scalar1=1e-18, scalar2=None, op0=ALU.max), [oBs], [r_])
                    A(lambda e: e.activation(out=r_[:], in_=r_[:], func=AF.Ln), [r_], [r_])
                    A(lambda e: e.activation(out=r_[:], in_=r_[:], func=AF.Exp, scale=-1.0), [r_], [r_])
                    c_ = coef.next()
                    V(lambda e: e.tensor_tensor(out=c_[:], in0=r_[:], in1=g_[:], op=ALU.mult), [r_, g_], [c_])
                    dst = acc[p] if first else tmp2.next()
                    V(lambda e: e.tensor_tensor(out=dst[0:64, :], in0=oAs[0:64, :], in1=c_[0:64, :], op=ALU.mult), [oAs, c_], [dst])
                    V(lambda e: e.tensor_tensor(out=dst[64:128, :], in0=oBs[64:128, :], in1=c_[64:128, :], op=ALU.mult), [oBs, c_], [dst])
                    if last:
                        O_ = onb.next()
                        V(lambda e: e.tensor_tensor(out=O_[:], in0=acc[p][:], in1=dst[:], op=ALU.add), [acc[p], dst], [O_])
                        STO(onb.d(), O_, onT[p, :, T * TT:(T + 1) * TT], O_[:])
                    elif not first:
                        V(lambda e: e.tensor_tensor(out=acc[p][:], in0=acc[p][:], in1=dst[:], op=ALU.add), [acc[p], dst], [acc[p]])

                nxt = pre(0)
                for T in range(NT):
                    QN, MC, FB, GS = nxt
                    if T + 1 < NT:
                        nxt = pre(T + 1)
                    ncts = 2 if T >= 4 else 1
                    for p in range(4):
                        g_c = gpre(p, 0, GS)
                        PTs = {}
                        for ct in range(ncts):
                            sA, sB = sbk[(2 * ct) % 4], sbk[(2 * ct + 1) % 4]
                            P(lambda e: e.matmul(sA[:, :], kcmp_g[0:64, ct * 128:(ct + 1) * 128], QN[0:64, p, :], start=True, stop=True), [kcmp_g, QN], [sA])
                            P(lambda e: e.matmul(sB[:, :], kcmp_g[64:128, ct * 128:(ct + 1) * 128], QN[64:128, p, :], start=True, stop=True), [kcmp_g, QN], [sB])
                            for hd, sk in ((0, sA), (1, sB)):
                                PT = pt.next()
                                A(lambda e: e.activation(out=PT[:], in_=sk[:], func=AF.Exp, scale=0.125), [sk], [PT])
                                if not (ct == 0 and T >= 5):
                                    V(lambda e: e.tensor_tensor(out=PT[:], in0=PT[:], in1=MC[:, ct, :], op=ALU.mult), [PT, MC], [PT])
                                PTs[(hd, ct)] = PT
                        for ct in range(ncts):
                            last = (ct == ncts - 1)
                            P(lambda e: e.matmul(oA[:, :], vcmp_g[:, ct, 0:128], PTs[(0, ct)][:, :], start=(ct == 0), stop=last), [vcmp_g, PTs[(0, ct)]], [oA], inc=last)
                            P(lambda e: e.matmul(oB[:, :], vcmp_g[:, ct, 64:192], PTs[(1, ct)][:, :], start=(ct == 0), stop=last), [vcmp_g, PTs[(1, ct)]], [oB], inc=last)
                        for half in range(2):
                            for qi in range(2):
                                qs = half * 2 + qi
                                for hd in range(2):
                                    col = (qi * 2 + hd) * 65
                                    for ct in range(ncts):
                                        last = (ct == ncts - 1)
                                        P(lambda e: e.matmul(xb[:, col:col + 65], PTs[(hd, ct)][:, qs * 128:(qs + 1) * 128], ov1t[:, ct, :], start=(ct == 0), stop=last),
                                          [PTs[(hd, ct)], ov1t], [xb], inc=last)
                            r4 = rl4.next()
                            V(lambda e: e.tensor_scalar(out=r4[:, 0:4], in0=xb[:, 64:260:65], scalar1=1e-30, scalar2=None, op0=ALU.max), [xb], [r4])
                            V(lambda e: e.reciprocal(r4[:], r4[:]), [r4], [r4])
                            for qi in range(2):
                                qs = half * 2 + qi
                                for hd in range(2):
                                    col = (qi * 2 + hd) * 65
                                    src1 = FB[:, qs, hd, :] if p == 0 else impacc[:, qs, hd, :]
                                    V(lambda e: e.scalar_tensor_tensor(out=impacc[:, qs, hd, :], in0=xb[:, col:col + 64], scalar=r4[:, qi * 2 + hd:qi * 2 + hd + 1], in1=src1,
                                                                       op0=ALU.mult, op1=ALU.add), [xb, r4, FB, impacc], [impacc])
                        finish(p, 0, T, g_c, True, False)
                    SBT = selbT.next()
                    for qs in range(4):
                        SB = selb.next()
                        for g in range(2):
                            a8, t6, b8 = m8a.next(), t64.next(), m8b.next()
                            V(lambda e: e.max(a8[:], impacc[:, qs, g, :]), [impacc], [a8])
                            V(lambda e: e.match_replace(t6[:], a8[:], impacc[:, qs, g, :], -3.0e38), [a8, impacc], [t6])
                            V(lambda e: e.max(b8[:], t6[:]), [t6], [b8])
                            V(lambda e: e.tensor_scalar(out=SB[:, g * 64:(g + 1) * 64], in0=impacc[:, qs, g, :], scalar1=b8[:, 7:8], scalar2=NEG, op0=ALU.is_lt, op1=ALU.mult),
                              [impacc, b8], [SB])
                        P(lambda e: e.transpose(xb[:, qs * 128:(qs + 1) * 128], SB[:, :], ident_f[:]), [SB, ident_f], [xb])
                    cp('dve', SBT[:], xb[:], [xb], [SBT])
                    for br in (2, 1):
                        for p in range(4):
                            g_b = gpre(p, br, GS)
                            if br == 1:
                                js = list(range(0, 4 * T + 4))
                            else:
                                js = [4 * T] + [j_ for j_ in range(max(0, 4 * T - 4), 4 * T + 4) if j_ != 4 * T]
                            nj = len(js)
                            kt = ks if br == 1 else kw
                            LAG = 1
                            items = []
                            for step in range(nj + LAG):
                                if step < nj:
                                    j = js[step]
                                    if j >= 4 * T:
                                        c0, c1, ti = 128 * (j - 4 * T), TT, 0
                                        tc = c0
                                    elif br == 2:
                                        c0, c1, ti = 0, 128 * (j - 4 * T + 5), 1
                                        tc = c1 - 128
                                    else:
                                        c0, c1, ti, tc = 0, TT, None, None
                                    sks = (sbk[(2 * step) % 4], sbk[(2 * step + 1) % 4])
                                    for hd in range(2):
                                        rows = slice(hd * 64, hd * 64 + 64)
                                        P(lambda e: e.matmul(sks[hd][:, c0:c1], kt[rows, j * 128:(j + 1) * 128], QN[rows, p, c0:c1], start=True, stop=(br == 2)),
                                          [kt, QN], [sks[hd]], inc=(br == 2))
                                    if br == 1:
                                        for hd in range(2):
                                            rows = slice(hd * 64, hd * 64 + 64)
                                            P(lambda e: e.matmul(sks[hd][:, c0:c1], Et[rows, j, :], SBT[rows, c0:c1], start=False, stop=True), [Et, SBT], [sks[hd]])
                                    pts = []
                                    for hd in range(2):
                                        PT = pt.next()
                                        A(lambda e: e.activation(out=PT[:, c0:c1], in_=sks[hd][:, c0:c1], func=AF.Exp, scale=0.125), [sks[hd]], [PT])
                                        if ti is not None:
                                            V(lambda e: e.tensor_tensor(out=PT[:, tc:tc + 128], in0=PT[:, tc:tc + 128], in1=tri[:, ti, :], op=ALU.mult), [PT, tri], [PT])
                                        pts.append(PT)
                                    items.append((j, c0, c1, pts))
                                if step >= LAG:
                                    idx = step - LAG
                                    j, c0, c1, pts = items[idx]
                                    a = 0 if br == 1 else 1
                                    P(lambda e: e.matmul(oA[:, c0:c1], vs3[:, j, a, 0:128], pts[0][:, c0:c1], start=(idx == 0), stop=(idx == nj - 1)), [vs3, pts[0]], [oA], inc=(idx == nj - 1))
                                    P(lambda e: e.matmul(oB[:, c0:c1], vs3[:, j, a, 64:192], pts[1][:, c0:c1], start=(idx == 0), stop=(idx == nj - 1)), [vs3, pts[1]], [oB], inc=(idx == nj - 1))
                            finish(p, br, T, g_b, False, br == 1)
                k.barrier()

        def phaseE(l):
            with ES() as st:
                wg = k.sb(st, "wg", [128, 8, 3072], BF16)
                wos = [k.sb(st, f"wo{i}", [128, 4, D], BF16) for i in range(3)]
                wo = k.sb(st, "wout", [128, 8, D], BF16)
                ht = Rot(k, st, "ht", [128, 8, TT], BF16, 2, dsp[0:2])
                oin = [Rot(k, st, f"oin{i}", [128, 4, TT], BF16, 2, dsp[2 + 2 * i:4 + 2 * i]) for i in range(3)]
                xt = Rot(k, st, "xt", [128, 8, TT], F32, 1, dsp[8:9])
                m = k.sb(st, "m", [128, 8, TT], BF16)
                sg = Rot(k, st, "sg", [128, TT], F32, 2)
                ta = Rot(k, st, "ta", [128, TT], F32, 5)
                wst = Rot(k, st, "wst", [128, 1024], F32, 3, dsp[17:20])
                load_w(wst, wg, 8, 3072, lambda kc: wd["w3"][l, kc * 128:(kc + 1) * 128, :], PC_GMIX)
                for i, nm in enumerate(("woa", "wob", "woc")):
                    load_w(wst, wos[i], 4, D, lambda kc: wd[nm][l, kc * 128:(kc + 1) * 128, :])
                load_w(wst, wo, 8, D, lambda kc: wd["wout"][l, kc * 128:(kc + 1) * 128, :])
                srcs = [oTa.rearrange("c p t -> p c t"), caT.rearrange("c p t -> p c t"), onT.rearrange("c p t -> p c t")]
                bk = [0]

                def bank():
                    bk[0] = (bk[0] + 1) % 8
                    return pb[bk[0]]

                def pre(T):
                    tsl = slice(T * TT, (T + 1) * TT)
                    H = ht.next()
                    LD(ht.d(), H, H[:], hT_v[:, :, tsl])
                    Os = []
                    for i in range(3):
                        O_ = oin[i].next()
                        LD(oin[i].d(), O_, O_[:], srcs[i][:, :, tsl])
                        Os.append(O_)
                    return H, Os
                nxt = pre(0)
                for T in range(NT):
                    H, Os = nxt
                    tsl = slice(T * TT, (T + 1) * TT)
                    X = xt.next()
                    LD(xt.d(), X, X[:], xT_v[:, :, tsl])
                    if T + 1 < NT:
                        nxt = pre(T + 1)
                    for r in range(8):
                        terms = []
                        for xi in range(3):
                            bo = bank()
                            for c in range(4):
                                P(lambda e: e.matmul(bo[:], wos[xi][:, c, r * 128:(r + 1) * 128], Os[xi][:, c, :], start=(c == 0), stop=(c == 3)), [wos[xi], Os[xi]], [bo], inc=(c == 3))
                            bg = bank()
                            for kc in range(8):
                                P(lambda e: e.matmul(bg[:], wg[:, kc, xi * 1024 + r * 128:xi * 1024 + (r + 1) * 128], H[:, kc, :], start=(kc == 0), stop=(kc == 7)), [wg, H], [bg], inc=(kc == 7))
                            g_ = sg.next()
                            A(lambda e: e.activation(out=g_[:], in_=bg[:], func=AF.Sigmoid), [bg], [g_])
                            t_ = ta.next()
                            if xi == 1:
                                V(lambda e: e.scalar_tensor_tensor(out=t_[:], in0=bo[:], scalar=pc(PC_BCO + r), in1=g_[:], op0=ALU.add, op1=ALU.mult), [bo, pcol, g_], [t_])
                            else:
                                V(lambda e: e.tensor_tensor(out=t_[:], in0=bo[:], in1=g_[:], op=ALU.mult), [bo, g_], [t_])
                            terms.append(t_)
                        u_ = ta.next()
                        V(lambda e: e.tensor_tensor(out=u_[:], in0=terms[0][:], in1=terms[1][:], op=ALU.add), [terms[0], terms[1]], [u_])
                        G(lambda e: e.tensor_tensor(out=m[:, r, :], in0=u_[:], in1=terms[2][:], op=ALU.add), [u_, terms[2]], [m])
                    for r2 in range(8):
                        bo = bank()
                        for r in range(8):
                            P(lambda e: e.matmul(bo[:], wo[:, r, r2 * 128:(r2 + 1) * 128], m[:, r, :], start=(r == 0), stop=(r == 7)), [wo, m], [bo], inc=(r == 7))
                        V(lambda e: e.tensor_tensor(out=X[:, r2, :], in0=X[:, r2, :], in1=bo[:], op=ALU.add), [X, bo], [X])
                    STO(xt.d(), X, xT_v[:, :, tsl], X[:])
                k.barrier()

        def phaseF(l, hf, last):
            with ES() as st:
                w1h = k.sb(st, "w1h", [128, 8, 2048], BF16)
                w2h = k.sb(st, "w2h", [128, 16, D], BF16)
                xt = Rot(k, st, "xt", [128, 8, TT], F32, 2, dsp[0:2])
                h2 = Rot(k, st, "h2", [128, 8, TT], BF16, 2, dsp[2:4])
                a = k.sb(st, "a", [128, 16, TT], BF16)
                rl = Rot(k, st, "rl", [128, TT], F32, 2)
                wst = Rot(k, st, "wst", [128, 1024], F32, 4, dsp[17:21])
                if hf == 0:
                    sqb = k.sb(st, "sqb", [128, 8, TT], F32)
                    ss = Rot(k, st, "ss", [128, TT], mybir.dt.float32r, 2)
                    rs = Rot(k, st, "rs", [128, TT], F32, 2)
                fin = last and hf == 1
                if fin:
                    outb = Rot(k, st, "outb", [128, 4, D], F32, 1, dsp[4:5])
                load_w(wst, w1h, 8, 2048, lambda kc: wd["wf1"][l, kc * 128:(kc + 1) * 128, hf * 2048:(hf + 1) * 2048], PC_GFFN)
                load_w(wst, w2h, 16, D, lambda kc: wd["wf2"][l, hf * 2048 + kc * 128:hf * 2048 + (kc + 1) * 128, :])
                bk = [0]

                def bank():
                    bk[0] = (bk[0] + 1) % 8
                    return pb[bk[0]]

                def pre(T):
                    tsl = slice(T * TT, (T + 1) * TT)
                    X = xt.next()
                    LD(xt.d(), X, X[:], xT_v[:, :, tsl])
                    H2 = h2.next()
                    if hf == 1:
                        LD(h2.d(), H2, H2[:], h2T_v[:, :, tsl])
                    return X, H2
                nxt = pre(0)
                for T in range(NT):
                    X, H2 = nxt
                    tsl = slice(T * TT, (T + 1) * TT)
                    if T + 1 < NT:
                        nxt = pre(T + 1)
                    if hf == 0:
                        A(lambda e: e.activation(out=sqb[:], in_=X[:], func=AF.Square), [X], [sqb])
                        s_ = ss.next()
                        V(lambda e: e.tensor_reduce(out=s_[:], in_=sqb[:].rearrange("p c t -> p t c"), axis=AX.X, op=ALU.add), [sqb], [s_])
                        bs = bank()
                        stat_mm(bs[:], ones_r[:], s_[:], [ones_r, s_], [bs])
                        r_ = rs.next()
                        rstd_from(bs, r_, 1024.0)
                        V(lambda e: e.tensor_tensor(out=H2[:], in0=X[:], in1=r_[:].unsqueeze(1).to_broadcast([128, 8, TT]), op=ALU.mult), [X, r_], [H2])
                        STO(h2.d(), H2, h2T_v[:, :, tsl], H2[:])
                    for f in range(16):
                        b = bank()
                        for kc in range(8):
                            P(lambda e: e.matmul(b[:], w1h[:, kc, f * 128:(f + 1) * 128], H2[:, kc, :], start=(kc == 0), stop=(kc == 7)), [w1h, H2], [b], inc=(kc == 7))
                        r_ = rl.next()
                        A(lambda e: e.activation(out=r_[:], in_=b[:], func=AF.Relu), [b], [r_])
                        (V if f % 2 else G)(lambda e: e.tensor_tensor(out=a[:, f, :], in0=r_[:], in1=r_[:], op=ALU.mult), [r_], [a])
                    for r2 in range(8):
                        b = bank()
                        for f in range(16):
                            P(lambda e: e.matmul(b[:], w2h[:, f, r2 * 128:(r2 + 1) * 128], a[:, f, :], start=(f == 0), stop=(f == 15)), [w2h, a], [b], inc=(f == 15))
                        V(lambda e: e.tensor_tensor(out=X[:, r2, :], in0=X[:, r2, :], in1=b[:], op=ALU.add), [X, b], [X])
                    if fin:
                        OB = outb.next()
                        for s in range(4):
                            for half in range(2):
                                b = bank()
                                for i in range(4):
                                    r2 = half * 4 + i
                                    P(lambda e: e.transpose(b[:, i * 128:(i + 1) * 128], X[:, r2, s * 128:(s + 1) * 128], ident_f[:]), [X, ident_f], [b], inc=(i == 3))
                                cp('act' if half else 'dve', OB[:, s, half * 512:(half + 1) * 512], b[:], [b], [OB])
                        STO(outb.d(), OB, out_d[tsl, :].rearrange("(s p) d -> p s d", p=128), OB[:])
                    else:
                        STO(xt.d(), X, xT_v[:, :, tsl], X[:])
                k.barrier()

        if run('0'):
            phase0()
        for l in range(nl):
            if run('A1'):
                phaseA1(l)
            else:
                LD(dsp[20], pcol, pcol[:], wd["pcol"][l])
            if run('A2'):
                phaseA2(l)
            if run('B'):
                phaseB(l)
            if run('C'):
                phaseC(l)
            if run('D0'):
                phaseD0(l)
            if run('D'):
                phaseD(l)
            if run('E'):
                phaseE(l)
            if run('F'):
                phaseF(l, 0, False)
                phaseF(l, 1, l == nl - 1)
        k.barrier()
    build.n_ins = k.n_ins
    return nc


_CACHE = {}


def kernel(**inputs):
    inputs = {k_: np.asarray(v) for k_, v in inputs.items()}
    consts = _host_consts()
    prep = _host_prep(inputs)
    nc = build()
    x = inputs['x']
    shared = dict(consts)
    shared.update(prep)
    in_maps = []
    for b in range(8):
        m = dict(shared)
        m['x'] = np.ascontiguousarray(x[b])
        in_maps.append(m)
    res = run_bass_kernel_spmd(nc, in_maps, core_ids=list(range(8)))
    return np.stack([np.asarray(r['out'], dtype=np.float32) for r in res.results], axis=0)
```

```python
import contextlib
import numpy as np
import ml_dtypes
import concourse.bass as bass
import concourse.mybir as mybir
from concourse.bass_utils import run_bass_kernel_spmd

F32 = mybir.dt.float32
BF16 = mybir.dt.bfloat16
AF = mybir.ActivationFunctionType
ALU = mybir.AluOpType
AX = mybir.AxisListType
NPBF = ml_dtypes.bfloat16

SEM_LIMIT = 30000


class DSem:
    def __init__(self, sem):
        self.sem = sem
        self.issued = 0


class Buf:
    def __init__(self, name, t=None):
        self.name = name
        self.t = t
        self.w = {}
        self.r = {}

    def __getitem__(self, idx):
        return self.t[idx]

    def _keys(self, key):
        if key is None:
            return list(set(self.w.keys()) | set(self.r.keys()) | {None})
        return [key, None]

    def deps_read(self, key):
        return [self.w[k] for k in self._keys(key) if k in self.w]

    def deps_write(self, key):
        d = []
        for k in self._keys(key):
            if k in self.w:
                d.append(self.w[k])
            d.extend(self.r.get(k, []))
        return d

    def add_read(self, key, i):
        self.r.setdefault(key, []).append(i)
        if len(self.r[key]) > 24:
            last = {}
            for x in self.r[key]:
                last[(x[0], x[1] if x[0] == 'e' else id(x[1]))] = x
            self.r[key] = list(last.values())

    def set_write(self, key, i):
        if key is None:
            self.w = {None: i}
            self.r = {}
        else:
            self.w[key] = i
            self.r[key] = []


class KB:
    def __init__(self, nc):
        self.nc = nc
        self.eng = {'pe': nc.tensor, 'act': nc.scalar, 'dve': nc.vector, 'pool': nc.gpsimd, 'sp': nc.sync}
        self.stack = contextlib.ExitStack()
        self.sem = {}
        self.cnt = {}
        self.nsem = 0
        self.known = {e: {} for e in self.eng}
        self.dsems = []
        self.allsems = []
        for e in ('pe', 'act', 'dve', 'pool'):
            self._rot(e)
        self.n_ins = 0

    def _newsem(self, name):
        s = self.stack.enter_context(self.nc.semaphore(f"{name}_{self.nsem}"))
        self.nsem += 1
        return s

    def _rot(self, e):
        self.sem[e] = self._newsem("s" + e)
        self.cnt[e] = 0

    def dsem(self, name="d"):
        d = DSem(self._newsem(name))
        self.dsems.append(d)
        return d

    def sb(self, st, name, shape, dt):
        self.nsem += 1
        name = f"sb_{name}_{self.nsem}"
        t = st.enter_context(self.nc.sbuf_tensor(name, list(shape), dt))
        return Buf(name, t)

    def ps(self, st, name, shape, dt=F32):
        self.nsem += 1
        name = f"ps_{name}_{self.nsem}"
        t = st.enter_context(self.nc.psum_tensor(name, list(shape), dt))
        return Buf(name, t)

    def _wait(self, e, dep):
        eng = self.eng[e]
        if dep[0] == 'e':
            _, src, sem, val = dep
            if src == 'pe' and e == 'pe':
                return
            k = id(sem)
            if self.known[e].get(k, 0) >= val:
                return
            eng.wait_ge(sem, val)
            self.known[e][k] = val
        else:
            _, ds, kk = dep
            k = id(ds.sem)
            if self.known[e].get(k, 0) >= kk:
                return
            eng.wait_ge(ds.sem, ds.issued)
            self.known[e][k] = ds.issued

    def _deps(self, e, rd, wr):
        deps = []
        for b, k in rd:
            deps.extend(b.deps_read(k))
        for b, k in wr:
            deps.extend(b.deps_write(k))
        for d in deps:
            self._wait(e, d)

    def _post(self, myid, rd, wr):
        for b, k in rd:
            b.add_read(k, myid)
        for b, k in wr:
            b.set_write(k, myid)

    def op(self, e, fn, rd=(), wr=(), inc=True):
        self._deps(e, rd, wr)
        ins = fn(self.eng[e])
        self.n_ins += 1
        sem = self.sem[e]
        if inc:
            self.cnt[e] += 1
            ins.then_inc(sem, 1)
            myid = ('e', e, sem, self.cnt[e])
            if self.cnt[e] >= SEM_LIMIT:
                self._rot(e)
        else:
            myid = ('e', e, sem, self.cnt[e] + 1)
        self._post(myid, rd, wr)
        return myid

    def dma(self, q, ds, out, in_, rd=(), wr=()):
        self._deps(q, rd, wr)
        ins = self.eng[q].dma_start(out=out, in_=in_)
        self.n_ins += 1
        ds.issued += 16
        ins.then_inc(ds.sem, 16)
        myid = ('d', ds, ds.issued)
        self._post(myid, rd, wr)
        return myid

    def barrier(self, engines=('pe', 'act', 'dve', 'pool', 'sp')):
        for e in engines:
            for src in ('pe', 'act', 'dve', 'pool'):
                if self.cnt[src] > 0:
                    if src == e:
                        pass
                    self._wait_raw(e, self.sem[src], self.cnt[src])
            for ds in self.dsems:
                if ds.issued > 0:
                    self._wait_raw(e, ds.sem, ds.issued)

    def _wait_raw(self, e, sem, val):
        k = id(sem)
        if self.known[e].get(k, 0) >= val:
            return
        self.eng[e].wait_ge(sem, val)
        self.known[e][k] = val


S = 4096
D = 1024
NL = 4
TT = 512
NT = S // TT
EPS = 1e-6
THETA = 500000.0
NEG = -30000.0
C1_CQ, C1_CKV, C1_KR, C1_UA, C1_UG, C1 = 0, 384, 640, 704, 1216, 1728
C2_Q, C2_KC, C2_VC, C2_KS, C2_KW, C2_PP, C2_VS, C2_VW, C2_GN, C2 = \
    0, 512, 640, 768, 896, 1024, 1792, 1920, 2048, 2072
PC_GMIX, PC_GCQ, PC_GCKV, PC_GQM, PC_GKM, PC_BGA, PC_BGG, PC_WDW, PC_BDW, PC_GLN, PC_BLN, PC_BCO, \
    PC_GQN, PC_GQNP, PC_GKN, PC_GKNP, PC_GKCP, PC_GFFN, PC_PEK, PC_PEV, NPC = \
    0, 8, 11, 13, 14, 15, 19, 23, 147, 151, 155, 159, 167, 168, 169, 170, 171, 172, 180, 212, 244


def _host_consts():
    c = {}
    c['ident'] = np.eye(128, dtype=np.float32)
    pos = np.arange(S, dtype=np.float32)
    rt = np.zeros((128, 4, S), np.float32)
    inv = np.power(np.float32(THETA), -np.arange(16, dtype=np.float32) * np.float32(2.0 / 32)).astype(np.float32)
    ang = (pos[None, :] * inv[:, None]).astype(np.float32)
    cos, sin = np.cos(ang).astype(np.float32), np.sin(ang).astype(np.float32)
    rt[64:80, 0], rt[80:96, 0] = cos, cos
    rt[64:80, 1], rt[80:96, 1] = -sin, sin
    rt[96:112, 1], rt[112:128, 1] = -sin, sin
    inv8 = np.power(np.float32(THETA), -np.arange(8, dtype=np.float32) * np.float32(2.0 / 16)).astype(np.float32)

    def nsa_tab(p):
        a = (p[None, :].astype(np.float32) * inv8[:, None]).astype(np.float32)
        cc, ss = np.cos(a).astype(np.float32), np.sin(a).astype(np.float32)
        Cf = np.ones((128, p.shape[0]), np.float32)
        Sf = np.zeros((128, p.shape[0]), np.float32)
        for b in (0, 64):
            Cf[b:b + 8], Cf[b + 8:b + 16] = cc, cc
            Sf[b:b + 8], Sf[b + 8:b + 16] = -ss, ss
        return Cf, Sf
    rt[:, 2], rt[:, 3] = nsa_tab(pos)
    c['rt'] = rt
    cpos = np.arange(256) * 16 + 31
    Cc, Sc = nsa_tab(cpos.astype(np.float32))
    c['rtc'] = np.stack([Cc, Sc], axis=1).astype(np.float32)
    kk = np.arange(128)[:, None]
    qq = np.arange(128)[None, :]
    tri = np.zeros((128, 2, 128), np.float32)
    tri[:, 0] = (kk <= qq)
    tri[:, 1] = (kk > qq)
    c['tri'] = tri.astype(NPBF)
    cidx = np.arange(256).reshape(2, 128).T
    mc = ((cidx[:, :, None] * 16 + 31) <= np.arange(S)[None, None, :]) & (cidx[:, :, None] < 255)
    c['maskc'] = mc.astype(np.float32).astype(NPBF)
    cs = cidx * 16
    ss_ = np.arange(64) * 64
    ov = (cs[:, :, None] < ss_[None, None, :] + 64) & (cs[:, :, None] + 32 > ss_[None, None, :]) & (cidx[:, :, None] < 255)
    ov1 = np.concatenate([ov.astype(np.float32), np.ones((128, 2, 1), np.float32)], axis=2)
    c['ov1'] = ov1.astype(NPBF)
    n = np.arange(64)[:, None, None]
    j = np.arange(32)[None, :, None]
    k2 = np.arange(128)[None, None, :]
    E = (n == 2 * j + k2 // 64).astype(np.float32)
    c['E'] = np.concatenate([E, E], axis=0).astype(NPBF)
    t = (np.arange(32)[None, :] * 128 + np.arange(128)[:, None])
    bt = t // 64
    jb = np.arange(64)[None, None, :]
    forced = (jb == 0) | (jb == bt[:, :, None]) | (jb == bt[:, :, None] - 1)
    fut = jb > bt[:, :, None]
    fb = np.where(fut, -1e9, np.where(forced, 1e6, 0.0)).astype(np.float32)
    c['fb'] = np.ascontiguousarray(np.broadcast_to(fb[:, :, None, :], (128, 32, 2, 64))).astype(np.float32)
    sel = np.zeros((32, 12, 128), np.float32)
    for p in range(4):
        for nb in range(3):
            sel[3 * p + nb, p * 3 + nb, 0:64] = 1.0
            sel[3 * (4 + p) + nb, p * 3 + nb, 64:128] = 1.0
    c['sel'] = sel
    bd = np.zeros((128, 128), np.float32)
    bd[0:64, 0:64] = 1.0
    bd[64:128, 64:128] = 1.0
    c['bd64'] = bd
    return c


def _host_prep(inp):
    f = np.float32
    w_in = inp['w_in']
    Lh = w_in.shape[0]
    cq, ckv, kr = w_in[:, :, 0:384], w_in[:, :, 384:640], w_in[:, :, 640:672]
    u2, qn, kvn = w_in[:, :, 672:1696], w_in[:, :, 1696:2208], w_in[:, :, 2208:2976]
    gn, gm = w_in[:, :, 2976:3000], w_in[:, :, 3000:6072]
    p32 = np.r_[16:32, 0:16]
    p16 = np.r_[8:16, 0:8]
    o = {}
    o['w1'] = np.ascontiguousarray(np.concatenate([cq, ckv, kr, kr[:, :, p32], u2], axis=2))
    qh = qn.reshape(Lh, D, 8, 64)
    kv6 = kvn.reshape(Lh, D, 6, 128)
    z48 = np.zeros((Lh, D, 48), f)

    def pchunk(a0, a1):
        return np.concatenate([a0[:, :, p16], z48, a1[:, :, p16], z48], axis=2)
    PP = [pchunk(qh[:, :, p, :16], qh[:, :, 4 + p, :16]) for p in range(4)]
    PP.append(pchunk(kv6[:, :, 2, 0:16], kv6[:, :, 2, 64:80]))
    PP.append(pchunk(kv6[:, :, 4, 0:16], kv6[:, :, 4, 64:80]))
    qt = [np.concatenate([qh[:, :, p], qh[:, :, 4 + p]], axis=2) for p in range(4)]
    o['w2'] = np.ascontiguousarray(np.concatenate(qt + [kv6[:, :, 0], kv6[:, :, 1], kv6[:, :, 2], kv6[:, :, 4]] + PP + [
                                                        kv6[:, :, 3], kv6[:, :, 5], gn], axis=2))
    o['w3'] = np.ascontiguousarray(gm)
    wuq = inp['w_uq'].reshape(Lh, 384, 8, 96)
    o['wq'] = np.ascontiguousarray(np.concatenate([wuq[..., 32:96], wuq[..., 0:32], wuq[..., 0:32][..., p32]], axis=3).reshape(Lh, 384, 1024))
    wukv = inp['w_ukv'].reshape(Lh, 256, 8, 128)
    o['wkk'] = np.ascontiguousarray(wukv[..., 0:64].reshape(Lh, 256, 512))
    o['wkv'] = np.ascontiguousarray(wukv[..., 64:128].reshape(Lh, 256, 512))
    o['woa'] = inp['w_o_mla']
    o['wob'] = inp['w_conv_out']
    won = inp['w_o_nsa'].reshape(Lh, 8, 64, D)
    o['woc'] = np.ascontiguousarray(np.concatenate([np.concatenate([won[:, p], won[:, 4 + p]], axis=1) for p in range(4)], axis=1))
    o['wout'] = inp['w_out']
    o['wf1'] = inp['w_ff1']
    o['wf2'] = inp['w_ff2']
    for nm, src in (('wck1', 'w_cmp_k1'), ('wcv1', 'w_cmp_v1')):
        a = inp[src].reshape(Lh, 32, 64, 128).transpose(0, 2, 1, 3)
        o[nm] = np.ascontiguousarray(np.concatenate([a, a], axis=1))
    k2 = inp['w_cmp_k2']
    z64 = np.zeros((Lh, 128, 64), f)
    z48 = np.zeros((Lh, 128, 48), f)
    o['wck2'] = np.ascontiguousarray(np.concatenate([z64, k2, z64, z64, k2[:, :, :16][:, :, p16], z48, z64], axis=2))
    o['wcv2'] = inp['w_cmp_v2']
    pc = np.zeros((Lh, 128, NPC), f)

    def colT(v, n):
        return v.reshape(Lh, n, 128).transpose(0, 2, 1)
    pc[:, :, PC_GMIX:PC_GMIX + 8] = colT(inp['g_mix'], 8)
    pc[:, :, PC_GCQ:PC_GCQ + 3] = colT(inp['g_cq'], 3)
    pc[:, :, PC_GCKV:PC_GCKV + 2] = colT(inp['g_ckv'], 2)
    for col, g in ((PC_GQM, inp['g_q_mla']), (PC_GKM, inp['g_k_mla'])):
        pc[:, 0:64, col] = g[:, 32:96]
        pc[:, 64:96, col] = g[:, 0:32]
        pc[:, 96:128, col] = g[:, 0:32][:, p32]
    pc[:, :, PC_BGA:PC_BGA + 4] = colT(inp['b_glu'][:, 0:512], 4)
    pc[:, :, PC_BGG:PC_BGG + 4] = colT(inp['b_glu'][:, 512:1024], 4)
    wd = inp['w_dw'].reshape(Lh, 31, 4, 128)
    pc[:, :, PC_WDW:PC_WDW + 124] = wd.transpose(0, 3, 2, 1).reshape(Lh, 128, 124)
    pc[:, :, PC_BDW:PC_BDW + 4] = colT(inp['b_dw'], 4)
    pc[:, :, PC_GLN:PC_GLN + 4] = colT(inp['g_conv_ln'], 4)
    pc[:, :, PC_BLN:PC_BLN + 4] = colT(inp['b_conv_ln'], 4)
    pc[:, :, PC_BCO:PC_BCO + 8] = colT(inp['b_conv_out'], 8)
    for col, colp, g in ((PC_GQN, PC_GQNP, inp['g_q_nsa']), (PC_GKN, PC_GKNP, inp['g_k_nsa'])):
        pc[:, 0:64, col] = g
        pc[:, 64:128, col] = g
        for s_ in (0, 64):
            pc[:, s_:s_ + 16, colp] = g[:, 0:16][:, p16]
    gk = inp['g_k_nsa']
    pc[:, 0:16, PC_GKCP] = gk[:, 0:16][:, p16]
    pc[:, 64:80, PC_GKCP] = gk[:, 0:16][:, p16]
    pc[:, :, PC_GFFN:PC_GFFN + 8] = colT(inp['g_ffn'], 8)
    for col, pe in ((PC_PEK, inp['pe_cmp_k']), (PC_PEV, inp['pe_cmp_v'])):
        pt = pe.transpose(0, 2, 1)
        pc[:, 0:64, col:col + 32] = pt
        pc[:, 64:128, col:col + 32] = pt
    o['pcol'] = pc
    return {k_: np.ascontiguousarray(v, dtype=v.dtype) for k_, v in o.items()}


class Rot:
    def __init__(self, k, st, name, shape, dt, n, ds=None):
        self.bufs = [k.sb(st, f"{name}{i}", shape, dt) for i in range(n)]
        self.ds = ds
        self.i = -1

    def next(self):
        self.i = (self.i + 1) % len(self.bufs)
        return self.bufs[self.i]

    def d(self):
        return self.ds[self.i]


def build(nl=NL, dbg=False, phases=None):
    ES = contextlib.ExitStack
    nc = bass.Bass("TRN2", target_bir_lowering=False)

    def din(name, shape, dt=F32):
        return nc.dram_tensor(name, list(shape), dt, kind="ExternalInput").ap()

    def scr(name, shape, dt):
        return nc.dram_tensor(name, list(shape), dt, kind="ExternalOutput" if dbg else "Internal").ap()

    x_d = din("x", [S, D])
    out_d = nc.dram_tensor("out", [S, D], F32, kind="ExternalOutput").ap()
    cst = {}
    for nm, shp, dt in (("ident", [128, 128], F32), ("rt", [128, 4, S], F32), ("rtc", [128, 2, 256], F32),
                        ("tri", [128, 2, 128], BF16), ("maskc", [128, 2, S], BF16), ("ov1", [128, 2, 65], BF16),
                        ("E", [128, 32, 128], BF16), ("fb", [128, 32, 2, 64], F32), ("sel", [32, 12, 128], F32),
                        ("bd64", [128, 128], F32)):
        cst[nm] = din(nm, shp, dt)
    Lh = NL
    wd = {}
    for nm, shp in (("w1", [Lh, D, C1]), ("w2", [Lh, D, C2]), ("w3", [Lh, D, 3072]), ("wq", [Lh, 384, 1024]),
                    ("wkk", [Lh, 256, 512]), ("wkv", [Lh, 256, 512]), ("woa", [Lh, 512, D]), ("wob", [Lh, 512, D]),
                    ("woc", [Lh, 512, D]), ("wout", [Lh, D, D]), ("wf1", [Lh, D, 4096]), ("wf2", [Lh, 4096, D]),
                    ("wck1", [Lh, 128, 32, 128]), ("wcv1", [Lh, 128, 32, 128]), ("wck2", [Lh, 128, 384]),
                    ("wcv2", [Lh, 128, 64]), ("pcol", [Lh, 128, NPC])):
        wd[nm] = din(nm, shp)
    xT = scr("xT", [D, S], F32)
    hT = scr("hT", [D, S], BF16)
    h2T = scr("h2T", [D, S], BF16)
    qTm = scr("qTm", [8, 96, S], BF16)
    kTm = scr("kTm", [8, 96, S], BF16)
    vml = scr("vml", [S, 8, 128], BF16)
    uT = scr("uT", [4, 128, 32 + S], BF16)
    oTa = scr("oTa", [4, 128, S], BF16)
    caT = scr("caT", [4, 128, S], BF16)
    qTn = scr("qTn", [4, 128, S], BF16)
    ksT = scr("ksT", [128, S], BF16)
    kwT = scr("kwT", [128, S], BF16)
    kcT = scr("kcT", [128, S], BF16)
    vcT = scr("vcT", [128, S], BF16)
    vsw = scr("vsw", [S, 2, 192], BF16)
    gsT = scr("gsT", [32, S], F32)
    onT = scr("onT", [4, 128, S], BF16)
    xT_v = xT.rearrange("(c p) t -> p c t", p=128)
    hT_v = hT.rearrange("(c p) t -> p c t", p=128)
    h2T_v = h2T.rearrange("(c p) t -> p c t", p=128)

    k = KB(nc)
    R = lambda *bs: [(b, None) for b in bs]

    def A(fn, rd, wr):
        return k.op('act', fn, R(*rd), R(*wr))

    def V(fn, rd, wr):
        return k.op('dve', fn, R(*rd), R(*wr))

    def G(fn, rd, wr):
        return k.op('pool', fn, R(*rd), R(*wr))

    def P(fn, rd, wr, inc=True):
        return k.op('pe', fn, R(*rd), R(*wr), inc=inc)

    def LD(ds, buf, dst, src):
        return k.dma('sp', ds, dst, src, wr=R(buf))

    def STO(ds, buf, dst, src):
        return k.dma('sp', ds, dst, src, rd=R(buf))

    def cp(e, out, in_, rd, wr):
        if e == 'act':
            return A(lambda en: en.activation(out=out, in_=in_, func=AF.Copy), rd, wr)
        return k.op(e, lambda en: en.tensor_copy(out, in_), R(*rd), R(*wr))

    def run(ph):
        return phases is None or ph in phases

    with k.stack, ES() as gst, nc.allow_low_precision(reason="fp32r-rounded stat tiles feed single-pass fp32r ones-matmuls"):
        ident_f = k.sb(gst, "ident_f", [128, 128], F32)
        ident_b = k.sb(gst, "ident_b", [128, 128], BF16)
        ones_f = k.sb(gst, "ones_f", [128, 128], F32)
        bd64 = k.sb(gst, "bd64", [128, 128], F32)
        tri = k.sb(gst, "tri", [128, 2, 128], BF16)
        pcol = k.sb(gst, "pcol", [128, NPC], F32)
        epsc = k.sb(gst, "epsc", [128, 1], F32)
        pb = [k.ps(gst, f"pb{i}", [128, 512]) for i in range(8)]
        dsp = [k.dsem(f"dq{i}") for i in range(28)]
        dsc = k.dsem("dconst")
        LD(dsc, ident_f, ident_f[:], cst["ident"][:, :])
        LD(dsc, bd64, bd64[:], cst["bd64"][:, :])
        LD(dsc, tri, tri[:], cst["tri"][:, :, :])
        V(lambda e: e.tensor_copy(ident_b[:], ident_f[:]), [ident_f], [ident_b])
        V(lambda e: e.memset(ones_f[:], 1.0), [], [ones_f])
        ones_r = k.sb(gst, "ones_r", [128, 128], mybir.dt.float32r)
        bd64_r = k.sb(gst, "bd64_r", [128, 128], mybir.dt.float32r)
        V(lambda e: e.tensor_copy(ones_r[:], ones_f[:]), [ones_f], [ones_r])
        V(lambda e: e.tensor_copy(bd64_r[:], bd64[:]), [bd64], [bd64_r])
        V(lambda e: e.memset(epsc[:], EPS), [], [epsc])

        def pc(col, n=1, rows=slice(0, 128)):
            return pcol[rows, col:col + n]

        conv_rr = [0]

        def load_w(wst, dst, kchunks, ncols, src_fn, scale_col=None, col0=0):
            WST = 1024
            for kc in range(kchunks):
                for c0 in range(0, ncols, WST):
                    w = min(WST, ncols - c0)
                    stg = wst.next()
                    LD(wst.d(), stg, stg[:, 0:w], src_fn(kc)[:, c0:c0 + w])
                    e = ('dve', 'act')[conv_rr[0] % 2]
                    conv_rr[0] += 1
                    o_ = dst[:, kc, col0 + c0:col0 + c0 + w]
                    sc = 1.0 if scale_col is None else pc(scale_col + kc)
                    rd = [stg] if scale_col is None else [stg, pcol]
                    if e == 'act':
                        A(lambda en: en.activation(out=o_, in_=stg[:, 0:w], func=AF.Copy, scale=sc), rd, [dst])
                    else:
                        k.op(e, lambda en: en.tensor_scalar(out=o_, in0=stg[:, 0:w], scalar1=sc, scalar2=1.0,
                                                            op0=ALU.mult, op1=ALU.mult), R(*rd), R(dst))

        F32R = mybir.dt.float32r

        def stat_mm(out_ap, lhs_ap, rhs_ap, rd, wr, start=True, stop=True, inc=True):
            P(lambda e: e.matmul(out_ap, lhs_ap, rhs_ap, start=start, stop=stop), rd, wr, inc=inc)

        def pipeline(units, nst, hooks=None):
            n = len(units)
            for i in range(n + nst - 1):
                for s_ in range(nst):
                    u = i - s_
                    if 0 <= u < n:
                        units[u][s_]()
                if hooks and i in hooks:
                    hooks[i]()

        def rstd_from(psb, rsb, n, rows=slice(0, 128)):
            A(lambda e: e.activation(out=rsb[rows, :], in_=psb[rows, :], func=AF.Ln, bias=epsc[rows, :], scale=1.0 / n), [psb, epsc], [rsb])
            A(lambda e: e.activation(out=rsb[rows, :], in_=rsb[rows, :], func=AF.Exp, scale=-0.5), [rsb], [rsb])

        def phase0():
            with ES() as st:
                xin = Rot(k, st, "xin", [128, 4, D], F32, 2, dsp[0:2])
                xo = Rot(k, st, "xo", [128, 8, TT], F32, 2, dsp[2:4])
                zt = k.sb(st, "zt", [128, 4, 32], BF16)
                V(lambda e: e.memset(zt[:], 0.0), [], [zt])
                STO(dsp[4], zt, uT.rearrange("c p t -> p c t")[:, :, 0:32], zt[:])
                for T in range(NT):
                    b = xin.next()
                    LD(xin.d(), b, b[:], x_d[T * TT:(T + 1) * TT, :].rearrange("(s p) d -> p s d", p=128))
                    o = xo.next()
                    for c in range(8):
                        for s in range(4):
                            P(lambda e: e.transpose(pb[c][:, s * 128:(s + 1) * 128], b[:, s, c * 128:(c + 1) * 128], ident_f[:]),
                              [b, ident_f], [pb[c]], inc=(s == 3))
                        cp('act' if c % 2 else 'dve', o[:, c, :], pb[c][:], [pb[c]], [o])
                    STO(xo.d(), o, xT_v[:, :, T * TT:(T + 1) * TT], o[:])
                k.barrier()

        def phaseA1(l):
            with ES() as st:
                w1 = k.sb(st, "w1", [128, 8, C1], BF16)
                wq = k.sb(st, "wq", [128, 3, 1024], BF16)
                wkk = k.sb(st, "wkk", [128, 2, 512], BF16)
                wkv = k.sb(st, "wkv", [128, 2, 512], BF16)
                xt = Rot(k, st, "xt", [128, 8, TT], F32, 2, dsp[0:2])
                rtt = Rot(k, st, "rtt", [128, 2, TT], F32, 2, dsp[2:4])
                ht = Rot(k, st, "ht", [128, 8, TT], BF16, 2, dsp[4:6])
                qo = Rot(k, st, "qo", [128, TT], BF16, 3, dsp[6:9])
                ko = Rot(k, st, "ko", [128, TT], BF16, 3, dsp[9:12])
                uo = Rot(k, st, "uo", [128, TT], BF16, 3, dsp[12:15])
                vt = Rot(k, st, "vt", [128, 4, 8, 128], BF16, 2, dsp[15:17])
                wst = Rot(k, st, "wst", [128, 1024], F32, 2, dsp[17:19])
                sqb = k.sb(st, "sqb", [128, 8, TT], F32)
                cqn = k.sb(st, "cqn", [128, 3, TT], BF16)
                ckvn = k.sb(st, "ckvn", [128, 2, TT], BF16)
                krt = k.sb(st, "krt", [128, TT], F32)
                sqr = k.sb(st, "sqr", [128, TT], mybir.dt.float32r)
                ss = Rot(k, st, "ss", [128, TT], mybir.dt.float32r, 2)
                rs = Rot(k, st, "rs", [128, TT], F32, 4)
                sqt = Rot(k, st, "sqt", [128, TT], mybir.dt.float32r, 3)
                yq = Rot(k, st, "yq", [128, TT], F32, 2)
                t1 = Rot(k, st, "t1", [128, TT], F32, 2)
                t2 = Rot(k, st, "t2", [128, TT], F32, 2)
                sg = Rot(k, st, "sg", [128, TT], F32, 2)
                LD(dsp[20], pcol, pcol[:], wd["pcol"][l])
                load_w(wst, w1, 8, C1, lambda kc: wd["w1"][l, kc * 128:(kc + 1) * 128, :], PC_GMIX)
                load_w(wst, wq, 3, 1024, lambda kc: wd["wq"][l, kc * 128:(kc + 1) * 128, :], PC_GCQ)
                load_w(wst, wkk, 2, 512, lambda kc: wd["wkk"][l, kc * 128:(kc + 1) * 128, :], PC_GCKV)
                load_w(wst, wkv, 2, 512, lambda kc: wd["wkv"][l, kc * 128:(kc + 1) * 128, :], PC_GCKV)
                for b_ in vt.bufs:
                    V(lambda e: e.memset(b_[:], 1.0), [], [b_])
                bk = [0]

                def bank():
                    bk[0] = (bk[0] + 1) % 6
                    return pb[bk[0]]

                def pre(T):
                    X = xt.next()
                    LD(xt.d(), X, X[:], xT_v[:, :, T * TT:(T + 1) * TT])
                    RT = rtt.next()
                    LD(rtt.d(), RT, RT[:], cst["rt"][:, 0:2, T * TT:(T + 1) * TT])
                    return X, RT
                def norm(T, X):
                    A(lambda e: e.activation(out=sqb[:], in_=X[:], func=AF.Square), [X], [sqb])
                    s_ = ss.next()
                    V(lambda e: e.tensor_reduce(out=s_[:], in_=sqb[:].rearrange("p c t -> p t c"), axis=AX.X, op=ALU.add), [sqb], [s_])
                    stat_mm(pb[6][:], ones_r[:], s_[:], [ones_r, s_], [pb[6]])
                    r_ = rs.next()
                    rstd_from(pb[6], r_, 1024.0)
                    H = ht.next()
                    V(lambda e: e.tensor_tensor(out=H[:], in0=X[:], in1=r_[:].unsqueeze(1).to_broadcast([128, 8, TT]), op=ALU.mult), [X, r_], [H])
                    STO(ht.d(), H, hT_v[:, :, T * TT:(T + 1) * TT], H[:])
                    return H
                nxt = pre(0)
                Hn = norm(0, nxt[0])
                for T in range(NT):
                    X, RT = nxt
                    H = Hn
                    if T + 1 < NT:
                        nxt = pre(T + 1)
                    tsl = slice(T * TT, (T + 1) * TT)

                    def proj(bnk, col0, m):
                        for kc in range(8):
                            P(lambda e: e.matmul(bnk[0:m, :], w1[:, kc, col0:col0 + m], H[:, kc, :], start=(kc == 0), stop=(kc == 7)),
                              [w1, H], [bnk], inc=(kc == 7))
                    for (c0, nch, dst, n) in ((C1_CQ, 3, cqn, 384.0), (C1_CKV, 2, ckvn, 256.0)):
                        bs = [bank() for _ in range(nch)]
                        for c in range(nch):
                            proj(bs[c], c0 + c * 128, 128)
                            A(lambda e: e.activation(out=sqb[:, c, :], in_=bs[c][:], func=AF.Square), [bs[c]], [sqb])
                        s_ = ss.next()
                        V(lambda e: e.tensor_tensor(out=s_[:], in0=sqb[:, 0, :], in1=sqb[:, 1, :], op=ALU.add), [sqb], [s_])
                        if nch == 3:
                            V(lambda e: e.tensor_tensor(out=s_[:], in0=s_[:], in1=sqb[:, 2, :], op=ALU.add), [sqb, s_], [s_])
                        stat_mm(pb[6][:], ones_r[:], s_[:], [ones_r, s_], [pb[6]])
                        r_ = rs.next()
                        rstd_from(pb[6], r_, n)
                        for c in range(nch):
                            V(lambda e: e.tensor_tensor(out=dst[:, c, :], in0=bs[c][:], in1=r_[:], op=ALU.mult), [bs[c], r_], [dst])
                    bq = bank()
                    proj(bq, C1_KR, 64)
                    A(lambda e: e.activation(out=krt[64:128, :], in_=bq[0:64, :], func=AF.Copy), [bq], [krt])
                    A(lambda e: e.activation(out=sqr[64:96, :], in_=krt[64:96, :], func=AF.Square), [krt], [sqr])
                    for c in range(4):
                        ba, bg = bank(), bank()
                        proj(ba, C1_UA + c * 128, 128)
                        proj(bg, C1_UG + c * 128, 128)
                        g_ = sg.next()
                        A(lambda e: e.activation(out=g_[:], in_=bg[:], func=AF.Sigmoid, bias=pc(PC_BGG + c), scale=1.0), [bg, pcol], [g_])
                        U = uo.next()
                        V(lambda e: e.scalar_tensor_tensor(out=U[:], in0=ba[:], scalar=pc(PC_BGA + c), in1=g_[:], op0=ALU.add, op1=ALU.mult), [ba, pcol, g_], [U])
                        STO(uo.d(), U, uT[c, :, 32 + T * TT:32 + (T + 1) * TT], U[:])
                    units = []
                    for h in range(8):
                        for isq in (True, False):
                            stt = {}

                            def s0(h=h, isq=isq, stt=stt):
                                bq = bank()
                                if isq:
                                    for c in range(3):
                                        P(lambda e: e.matmul(bq[:, :], wq[:, c, h * 128:(h + 1) * 128], cqn[:, c, :], start=(c == 0), stop=(c == 2)),
                                          [wq, cqn], [bq], inc=(c == 2))
                                else:
                                    for c in range(2):
                                        P(lambda e: e.matmul(bq[0:64, :], wkk[:, c, h * 64:(h + 1) * 64], ckvn[:, c, :], start=(c == 0), stop=(c == 1)),
                                          [wkk, ckvn], [bq], inc=(c == 1))
                                q_ = sqt.next()
                                nr = 96 if isq else 64
                                A(lambda e: e.activation(out=q_[0:nr, :], in_=bq[0:nr, :], func=AF.Square), [bq], [q_])
                                stt['bq'], stt['q_'] = bq, q_

                            def s1(h=h, isq=isq, stt=stt):
                                q_ = stt['q_']
                                st_ = pb[6 + (h % 2)]
                                if isq:
                                    stat_mm(st_[:], ones_r[0:96, :], q_[0:96, :], [ones_r, q_], [st_])
                                else:
                                    stat_mm(st_[:], ones_r[0:64, :], q_[0:64, :], [ones_r, q_], [st_], start=True, stop=False, inc=False)
                                    stat_mm(st_[:], ones_r[64:96, :], sqr[64:96, :], [ones_r, sqr], [st_], start=False, stop=True)
                                r_ = rs.next()
                                rstd_from(st_, r_, 96.0)
                                stt['r_'] = r_

                            def s2(h=h, isq=isq, stt=stt):
                                bq, r_ = stt['bq'], stt['r_']
                                O_ = (qo if isq else ko).next()
                                ods = (qo if isq else ko).d()
                                gcol = PC_GQM if isq else PC_GKM
                                y_ = yq.next()
                                if isq:
                                    V(lambda e: e.scalar_tensor_tensor(out=y_[:], in0=bq[:], scalar=pc(gcol), in1=r_[:], op0=ALU.mult, op1=ALU.mult), [bq, pcol, r_], [y_])
                                    A(lambda e: e.activation(out=O_[0:64, :], in_=y_[0:64, :], func=AF.Copy), [y_], [O_])
                                else:
                                    V(lambda e: e.scalar_tensor_tensor(out=O_[0:64, :], in0=bq[0:64, :], scalar=pc(gcol, rows=slice(0, 64)), in1=r_[0:64, :], op0=ALU.mult, op1=ALU.mult), [bq, pcol, r_], [O_])
                                    V(lambda e: e.scalar_tensor_tensor(out=y_[64:128, :], in0=krt[64:128, :], scalar=pc(gcol, rows=slice(64, 128)), in1=r_[64:128, :], op0=ALU.mult, op1=ALU.mult), [krt, pcol, r_], [y_])
                                a_, b_ = t1.next(), t2.next()
                                G(lambda e: e.tensor_tensor(out=a_[64:96, :], in0=y_[64:96, :], in1=RT[64:96, 0, :], op=ALU.mult), [y_, RT], [a_])
                                V(lambda e: e.tensor_tensor(out=b_[64:96, :], in0=y_[96:128, :], in1=RT[96:128, 1, :], op=ALU.mult), [y_, RT], [b_])
                                V(lambda e: e.tensor_tensor(out=O_[64:96, :], in0=a_[64:96, :], in1=b_[64:96, :], op=ALU.add), [a_, b_], [O_])
                                STO(ods, O_, (qTm if isq else kTm)[h, :, tsl], O_[0:96, :])
                            units.append([s0, s1, s2])
                    hooks = {}
                    if T + 1 < NT:
                        def hk(T=T):
                            nonlocal Hn
                            Hn = norm(T + 1, nxt[0])
                        hooks[7] = hk
                    pipeline(units, 3, hooks)
                    VT = vt.next()
                    for s in range(4):
                        bq = bank()
                        for c in range(2):
                            P(lambda e: e.matmul(bq[:, :], ckvn[:, c, s * 128:(s + 1) * 128], wkv[:, c, :], start=(c == 0), stop=(c == 1)),
                              [ckvn, wkv], [bq], inc=(c == 1))
                        bqv = bq[:, :].rearrange("p (h d) -> p h d", h=8)
                        cp('dve', VT[:, s, 0:8:2, 0:64], bqv[:, 0:8:2, :], [bq], [VT])
                        cp('act', VT[:, s, 1:8:2, 64:128], bqv[:, 1:8:2, :], [bq], [VT])
                    STO(vt.d(), VT, vml[T * TT:(T + 1) * TT].rearrange("(s p) h c -> p s h c", p=128), VT[:])
                k.barrier()

        def phaseA2(l):
            with ES() as st:
                w2 = k.sb(st, "w2", [128, 8, C2], BF16)
                ht = Rot(k, st, "ht", [128, 8, TT], BF16, 2, dsp[0:2])
                rtt = Rot(k, st, "rtt", [128, 2, TT], F32, 2, dsp[2:4])
                qo = Rot(k, st, "qo", [128, TT], BF16, 3, dsp[4:7])
                vsb = Rot(k, st, "vsb", [128, 4, 2, 192], BF16, 2, dsp[7:9])
                gso = Rot(k, st, "gso", [32, TT], F32, 2, dsp[9:11])
                wst = Rot(k, st, "wst", [128, 1024], F32, 3, dsp[17:20])
                pa = k.sb(st, "pa", [128, 6, TT], F32)
                ypr = Rot(k, st, "ypr", [128, TT], F32, 2)
                rs = Rot(k, st, "rs", [128, TT], F32, 4)
                sqt = Rot(k, st, "sqt", [128, TT], mybir.dt.float32r, 3)
                yq = Rot(k, st, "yq", [128, TT], F32, 2)
                t1 = Rot(k, st, "t1", [128, TT], F32, 2)
                t2 = Rot(k, st, "t2", [128, TT], F32, 2)
                load_w(wst, w2, 8, C2, lambda kc: wd["w2"][l, kc * 128:(kc + 1) * 128, :], PC_GMIX)
                for b_ in vsb.bufs:
                    V(lambda e: e.memset(b_[:], 1.0), [], [b_])
                for b_ in ypr.bufs:
                    V(lambda e: e.memset(b_[:], 0.0), [], [b_])
                bk = [0]

                def bank():
                    bk[0] = (bk[0] + 1) % 6
                    return pb[bk[0]]

                def pre(T):
                    H = ht.next()
                    LD(ht.d(), H, H[:], hT_v[:, :, T * TT:(T + 1) * TT])
                    RT = rtt.next()
                    LD(rtt.d(), RT, RT[:], cst["rt"][:, 2:4, T * TT:(T + 1) * TT])
                    return H, RT
                nxt = pre(0)
                for T in range(NT):
                    H, RT = nxt
                    if T + 1 < NT:
                        nxt = pre(T + 1)
                    tsl = slice(T * TT, (T + 1) * TT)

                    def proj(bnk, col0, m):
                        for kc in range(8):
                            P(lambda e: e.matmul(bnk[0:m, :], w2[:, kc, col0:col0 + m], H[:, kc, :], start=(kc == 0), stop=(kc == 7)),
                              [w2, H], [bnk], inc=(kc == 7))
                    for i in range(6):
                        b = bank()
                        proj(b, C2_PP + i * 128, 128)
                        cp('act' if i % 2 else 'dve', pa[:, i, :], b[:], [b], [pa])
                    units = [(C2_Q + p * 128, PC_GQN, PC_GQNP, p, 0, qTn[p]) for p in range(4)]
                    units += [(C2_KS, PC_GKN, PC_GKNP, 4, 0, ksT), (C2_KW, PC_GKN, PC_GKNP, 5, 0, kwT)]
                    us = []
                    for ui, (col0, gcol, gpcol, pi, s0_, dst) in enumerate(units):
                        stt = {}

                        def s0(col0=col0, stt=stt):
                            b = bank()
                            proj(b, col0, 128)
                            q_ = sqt.next()
                            A(lambda e: e.activation(out=q_[:], in_=b[:], func=AF.Square), [b], [q_])
                            stt['b'], stt['q_'] = b, q_

                        def s1(ui=ui, stt=stt):
                            q_ = stt['q_']
                            st_ = pb[6 + (ui % 2)]
                            stat_mm(st_[:], bd64_r[:], q_[:], [bd64_r, q_], [st_])
                            r_ = rs.next()
                            rstd_from(st_, r_, 64.0)
                            stt['r_'] = r_

                        def s2(gcol=gcol, gpcol=gpcol, pi=pi, dst=dst, stt=stt):
                            b, r_ = stt['b'], stt['r_']
                            y_ = yq.next()
                            V(lambda e: e.scalar_tensor_tensor(out=y_[:], in0=b[:], scalar=pc(gcol), in1=r_[:], op0=ALU.mult, op1=ALU.mult), [b, pcol, r_], [y_])
                            yp = ypr.next()
                            for ro in (0, 64):
                                sr = slice(ro, ro + 16)
                                V(lambda e: e.scalar_tensor_tensor(out=yp[sr, :], in0=pa[sr, pi, :], scalar=pc(gpcol, rows=sr), in1=r_[sr, :],
                                                                   op0=ALU.mult, op1=ALU.mult), [pa, pcol, r_], [yp])
                            a_, b2 = t1.next(), t2.next()
                            G(lambda e: e.tensor_tensor(out=a_[:], in0=y_[:], in1=RT[:, 0, :], op=ALU.mult), [y_, RT], [a_])
                            G(lambda e: e.tensor_tensor(out=b2[:], in0=yp[:], in1=RT[:, 1, :], op=ALU.mult), [yp, RT], [b2])
                            O_ = qo.next()
                            V(lambda e: e.tensor_tensor(out=O_[:], in0=a_[:], in1=b2[:], op=ALU.add), [a_, b2], [O_])
                            STO(qo.d(), O_, dst[:, tsl], O_[:])
                        us.append([s0, s1, s2])
                    pipeline(us, 3)
                    for i, (c0, dst) in enumerate(((C2_KC, kcT), (C2_VC, vcT))):
                        b = bank()
                        proj(b, c0, 128)
                        O_ = qo.next()
                        cp('act' if i % 2 else 'dve', O_[:], b[:], [b], [O_])
                        STO(qo.d(), O_, dst[:, tsl], O_[:])
                    VS = vsb.next()
                    for s in range(4):
                        b = bank()
                        for kc in range(8):
                            P(lambda e: e.matmul(b[:, 0:256], H[:, kc, s * 128:(s + 1) * 128], w2[:, kc, C2_VS:C2_VS + 256], start=(kc == 0), stop=(kc == 7)),
                              [w2, H], [b], inc=(kc == 7))
                        bv = b[:, 0:256].rearrange("p (a g d) -> p a g d", a=2, g=2)
                        cp('dve', VS[:, s, :, 0:64], bv[:, :, 0, :], [b], [VS])
                        cp('act', VS[:, s, :, 128:192], bv[:, :, 1, :], [b], [VS])
                    STO(vsb.d(), VS, vsw[T * TT:(T + 1) * TT, :, :].rearrange("(s p) a c -> p s a c", p=128), VS[:])
                    b = bank()
                    proj(b, C2_GN, 24)
                    GS = gso.next()
                    A(lambda e: e.activation(out=GS[0:24, :], in_=b[0:24, :], func=AF.Sigmoid), [b], [GS])
                    STO(gso.d(), GS, gsT[0:24, tsl], GS[0:24, :])
                k.barrier()

        def two_block(ap64, step):
            return bass.AP(ap64.tensor, ap64.offset, [list(ap64.ap[0]), [step, 2], [1, 64]])

        def phaseB(l):
            with ES() as st:
                vh = k.sb(st, "vh", [128, 32, 8, 128], BF16)
                qh = Rot(k, st, "qh", [128, S], BF16, 2, dsp[0:2])
                kh = Rot(k, st, "kh", [128, S], BF16, 2, dsp[2:4])
                pt = Rot(k, st, "pt", [128, TT], BF16, 6)
                ot = Rot(k, st, "ot", [128, S], BF16, 2, dsp[4:6])
                rl = Rot(k, st, "rl", [128, TT], F32, 2)
                LD(dsp[6], vh, vh[:], vml.rearrange("(j p) h c -> p j h c", p=128))
                sb_ = pb[0:4]
                obs = pb[4:6]
                SC = 96.0 ** -0.5
                LAG = 3

                def ldqk(h):
                    Q = qh.next()
                    LD(qh.d(), Q, Q[0:96, :], qTm[h])
                    K = kh.next()
                    LD(kh.d(), K, K[0:96, :], kTm[h])
                    return Q, K
                nxt = ldqk(0)
                cnt = 0
                for p in range(4):
                    OT = ot.next()
                    for hh in range(2):
                        h = 2 * p + hh
                        Q, K = nxt
                        if h + 1 < 8:
                            nxt = ldqk(h + 1)
                        steps = [(T, j) for T in range(NT) for j in range(4 * T + 4)]
                        obT = {}
                        items = []
                        for si in range(len(steps) + LAG):
                            if si < len(steps):
                                T, j = steps[si]
                                r = j - 4 * T
                                c0 = 128 * r if r > 0 else 0
                                sk = sb_[si % 4]
                                PT = pt.next()
                                P(lambda e: e.matmul(sk[:, c0:TT], K[0:96, j * 128:(j + 1) * 128], Q[0:96, T * TT + c0:(T + 1) * TT], start=True, stop=True),
                                  [K, Q], [sk])
                                A(lambda e: e.activation(out=PT[:, c0:TT], in_=sk[:, c0:TT], func=AF.Exp, scale=SC), [sk], [PT])
                                if r >= 0:
                                    V(lambda e: e.tensor_tensor(out=PT[:, c0:c0 + 128], in0=PT[:, c0:c0 + 128], in1=tri[:, 0, :], op=ALU.mult), [PT, tri], [PT])
                                items.append((T, j, c0, PT))
                            if si >= LAG:
                                T, j, c0, PT = items[si - LAG]
                                nj = 4 * T + 4
                                if j == 0:
                                    obT[T] = obs[cnt % 2]
                                    cnt += 1
                                ob = obT[T]
                                lt = vh[:, j, h, :]
                                P(lambda e: e.matmul(ob[:, c0:TT], lt, PT[:, c0:TT], start=(j == 0), stop=(j == nj - 1)), [vh, PT], [ob], inc=(j == nj - 1))
                                if j == nj - 1:
                                    r_ = rl.next()
                                    osl, lsl = (slice(0, 64), slice(64, 128)) if hh == 0 else (slice(64, 128), slice(0, 64))
                                    V(lambda e: e.reciprocal(r_[osl, :], ob[lsl, :]), [ob], [r_])
                                    V(lambda e: e.tensor_tensor(out=OT[osl, T * TT:(T + 1) * TT], in0=ob[osl, :], in1=r_[osl, :], op=ALU.mult), [ob, r_], [OT])
                    STO(ot.d(), OT, oTa[p], OT[:])
                k.barrier()

        def phaseC(l):
            with ES() as st:
                dg = k.sb(st, "dg", [128, 4, 31, 128], BF16)
                ut = Rot(k, st, "ut", [128, 4, 544], BF16, 2, dsp[0:2])
                ca = Rot(k, st, "ca", [128, 4, TT], BF16, 2, dsp[2:4])
                dt_ = k.sb(st, "cdt", [128, 4, TT], F32)
                sq_ = k.sb(st, "csq", [128, 4, TT], F32)
                s1 = Rot(k, st, "s1", [128, TT], mybir.dt.float32r, 2)
                rs = Rot(k, st, "rs", [128, TT], F32, 2)
                for c in range(4):
                    for kk in range(31):
                        e_ = 'pool' if (c * 31 + kk) % 2 else 'dve'
                        k.op(e_, lambda e: e.tensor_scalar(out=dg[:, c, kk, :], in0=ident_b[:], scalar1=pc(PC_WDW + c * 31 + kk), scalar2=1.0, op0=ALU.mult, op1=ALU.mult),
                             R(ident_b, pcol), R(dg))

                def pre(T):
                    U = ut.next()
                    LD(ut.d(), U, U[:, :, 0:542], uT.rearrange("c p t -> p c t")[:, :, T * TT + 2:T * TT + 544])
                    return U
                vts = Rot(k, st, "cvt2", [128, 4, TT], F32, 2)

                def part1(T, U):
                    vt_ = vts.next()
                    for c in range(4):
                        b = pb[c]
                        for kk in range(31):
                            P(lambda e: e.matmul(b[:], dg[:, c, kk, :], U[:, c, kk:kk + TT], start=(kk == 0), stop=(kk == 30)), [dg, U], [b], inc=(kk == 30))
                        A(lambda e: e.activation(out=vt_[:, c, :], in_=b[:], func=AF.Identity, bias=pc(PC_BDW + c), scale=1.0), [b, pcol], [vt_])
                    return vt_
                nxt = pre(0)
                vnext = part1(0, nxt)
                for T in range(NT):
                    vt_ = vnext
                    if T + 1 < NT:
                        nxt = pre(T + 1)
                        vnext = part1(T + 1, nxt)
                    s_ = s1.next()
                    V(lambda e: e.tensor_reduce(out=s_[:], in_=vt_[:].rearrange("p c t -> p t c"), axis=AX.X, op=ALU.add), [vt_], [s_])
                    stat_mm(pb[4][:], ones_r[:], s_[:], [ones_r, s_], [pb[4]])
                    V(lambda e: e.scalar_tensor_tensor(out=dt_[:], in0=pb[4][:].unsqueeze(1).to_broadcast([128, 4, TT]), scalar=-1.0 / 512, in1=vt_[:], op0=ALU.mult, op1=ALU.add),
                      [pb[4], vt_], [dt_])
                    A(lambda e: e.activation(out=sq_[:], in_=dt_[:], func=AF.Square), [dt_], [sq_])
                    s_ = s1.next()
                    V(lambda e: e.tensor_reduce(out=s_[:], in_=sq_[:].rearrange("p c t -> p t c"), axis=AX.X, op=ALU.add), [sq_], [s_])
                    stat_mm(pb[5][:], ones_r[:], s_[:], [ones_r, s_], [pb[5]])
                    r_ = rs.next()
                    rstd_from(pb[5], r_, 512.0)
                    V(lambda e: e.tensor_tensor(out=dt_[:], in0=dt_[:], in1=r_[:].unsqueeze(1).to_broadcast([128, 4, TT]), op=ALU.mult), [dt_, r_], [dt_])
                    CA = ca.next()
                    for c in range(4):
                        A(lambda e: e.activation(out=CA[:, c, :], in_=dt_[:, c, :], func=AF.Silu, bias=pc(PC_BLN + c), scale=pc(PC_GLN + c)), [dt_, pcol], [CA])
                    STO(ca.d(), CA, caT.rearrange("c p t -> p c t")[:, :, T * TT:(T + 1) * TT], CA[:])
                k.barrier()

        kcmp_g = k.sb(gst, "kcmp_g", [128, 256], BF16)
        vcmp_g = k.sb(gst, "vcmp_g", [128, 2, 192], BF16)

        def phaseD0(l):
            with ES() as st:
                kcs = k.sb(st, "kcs", [128, S], BF16)
                vcs = k.sb(st, "vcs", [128, S], BF16)
                w1k = k.sb(st, "w1k", [128, 1, 4096], BF16)
                w1v = k.sb(st, "w1v", [128, 1, 4096], BF16)
                wk2 = k.sb(st, "wk2", [128, 1, 384], BF16)
                wv2 = k.sb(st, "wv2", [128, 1, 64], BF16)
                pe = k.sb(st, "pe", [128, 64], BF16)
                hk = k.sb(st, "hk", [128, 2, 256], BF16)
                hv = k.sb(st, "hv", [128, 2, 256], BF16)
                bias = k.sb(st, "cbias", [128, 2], F32)
                rtc = k.sb(st, "rtc", [128, 2, 256], F32)
                q_ = k.sb(st, "cq_", [128, 256], F32)
                r_ = k.sb(st, "cr_", [128, 256], F32)
                y_ = k.sb(st, "cy_", [128, 256], F32)
                yp = k.sb(st, "cyp", [128, 256], F32)
                a_ = k.sb(st, "ca_", [128, 256], F32)
                b_ = k.sb(st, "cb_", [128, 256], F32)
                wst = Rot(k, st, "wst", [128, 1024], F32, 3, dsp[17:20])
                LD(dsp[0], kcs, kcs[:], kcT)
                LD(dsp[1], vcs, vcs[:], vcT)
                LD(dsp[2], rtc, rtc[:], cst["rtc"])
                load_w(wst, w1k, 1, 4096, lambda kc: wd["wck1"][l].rearrange("p a b -> p (a b)"))
                load_w(wst, w1v, 1, 4096, lambda kc: wd["wcv1"][l].rearrange("p a b -> p (a b)"))
                load_w(wst, wk2, 1, 384, lambda kc: wd["wck2"][l])
                load_w(wst, wv2, 1, 64, lambda kc: wd["wcv2"][l])
                V(lambda e: e.tensor_copy(pe[:], pcol[:, PC_PEK:PC_PEK + 64]), [pcol], [pe])
                V(lambda e: e.memset(hk[:], 0.0), [], [hk])
                V(lambda e: e.memset(hv[:], 0.0), [], [hv])
                V(lambda e: e.memset(kcmp_g[:], 0.0), [], [kcmp_g])
                V(lambda e: e.memset(vcmp_g[:], 1.0), [], [vcmp_g])
                V(lambda e: e.memset(yp[:], 0.0), [], [yp])
                for i, (w1, src, hdst) in enumerate(((w1k, kcs, hk), (w1v, vcs, hv))):
                    b = pb[0]
                    for l_ in range(32):
                        P(lambda e: e.matmul(b[:, 0:1], w1[0:64, 0, l_ * 128:(l_ + 1) * 128], pe[0:64, i * 32 + l_:i * 32 + l_ + 1], start=(l_ == 0), stop=(l_ == 31)),
                          [w1, pe], [b], inc=(l_ == 31))
                    cp('dve', bias[:, i:i + 1], b[:, 0:1], [b], [bias])
                    for g in range(2):
                        bn = pb[1 + g]
                        rows = slice(g * 64, (g + 1) * 64)
                        for l_ in range(32):
                            P(lambda e: e.matmul(bn[:, 0:255], w1[rows, 0, l_ * 128:(l_ + 1) * 128], src[rows, l_:l_ + 16 * 254 + 1:16], start=(l_ == 0), stop=(l_ == 31)),
                              [w1, src], [bn], inc=(l_ == 31))
                        A(lambda e: e.activation(out=hdst[:, g, 0:255], in_=bn[:, 0:255], func=AF.Silu, bias=bias[:, i:i + 1], scale=1.0), [bn, bias], [hdst])
                b = pb[3]
                P(lambda e: e.matmul(b[:, 0:255], wk2[:, 0, 64:192], hk[:, 0, 0:255], start=True, stop=False), [wk2, hk], [b], inc=False)
                P(lambda e: e.matmul(b[:, 0:255], wk2[:, 0, 0:128], hk[:, 1, 0:255], start=False, stop=True), [wk2, hk], [b])
                b2 = pb[4]
                P(lambda e: e.matmul(b2[:, 0:255], wk2[:, 0, 256:384], hk[:, 0, 0:255], start=True, stop=False), [wk2, hk], [b2], inc=False)
                P(lambda e: e.matmul(b2[:, 0:255], wk2[:, 0, 192:320], hk[:, 1, 0:255], start=False, stop=True), [wk2, hk], [b2])
                A(lambda e: e.activation(out=q_[:, 0:255], in_=b[:, 0:255], func=AF.Square), [b], [q_])
                P(lambda e: e.matmul(pb[5][:, 0:255], bd64[:], q_[:, 0:255], start=True, stop=True), [bd64, q_], [pb[5]])
                A(lambda e: e.activation(out=r_[:, 0:255], in_=pb[5][:, 0:255], func=AF.Sqrt, bias=epsc[:], scale=1.0 / 64), [pb[5], epsc], [r_])
                V(lambda e: e.reciprocal(r_[:, 0:255], r_[:, 0:255]), [r_], [r_])
                V(lambda e: e.scalar_tensor_tensor(out=y_[:, 0:255], in0=b[:, 0:255], scalar=pc(PC_GKN), in1=r_[:, 0:255], op0=ALU.mult, op1=ALU.mult), [b, pcol, r_], [y_])
                for ro in (0, 64):
                    rr = slice(ro, ro + 16)
                    V(lambda e: e.scalar_tensor_tensor(out=yp[rr, 0:255], in0=b2[rr, 0:255], scalar=pc(PC_GKCP, rows=rr), in1=r_[rr, 0:255], op0=ALU.mult, op1=ALU.mult),
                      [b2, pcol, r_], [yp])
                V(lambda e: e.tensor_tensor(out=a_[:, 0:255], in0=y_[:, 0:255], in1=rtc[:, 0, 0:255], op=ALU.mult), [y_, rtc], [a_])
                V(lambda e: e.tensor_tensor(out=b_[:, 0:255], in0=yp[:, 0:255], in1=rtc[:, 1, 0:255], op=ALU.mult), [yp, rtc], [b_])
                V(lambda e: e.tensor_tensor(out=kcmp_g[:, 0:255], in0=a_[:, 0:255], in1=b_[:, 0:255], op=ALU.add), [a_, b_], [kcmp_g])
                for ct in range(2):
                    bv = pb[6 + ct]
                    for g in range(2):
                        P(lambda e: e.matmul(bv[:, g * 64:(g + 1) * 64], hv[:, g, ct * 128:(ct + 1) * 128], wv2[:, 0, :], start=True, stop=True), [hv, wv2], [bv])
                    cp('dve', vcmp_g[:, ct, 0:64], bv[:, 0:64], [bv], [vcmp_g])
                    cp('act', vcmp_g[:, ct, 128:192], bv[:, 64:128], [bv], [vcmp_g])
                k.barrier()

        def phaseD(l):
            with ES() as st:
                ks = k.sb(st, "ks", [128, S], BF16)
                kw = k.sb(st, "kw", [128, S], BF16)
                vs3 = k.sb(st, "vs3", [128, 32, 2, 192], BF16)
                Et = k.sb(st, "Et", [128, 32, 128], BF16)
                selt = k.sb(st, "selt", [32, 12, 128], F32)
                ov1t = k.sb(st, "ov1t", [128, 2, 65], BF16)
                qn = Rot(k, st, "qn", [128, 4, TT], BF16, 2, dsp[0:2])
                mct = Rot(k, st, "mct", [128, 2, TT], BF16, 2, dsp[2:4])
                fbt = Rot(k, st, "fbt", [128, 4, 2, 64], F32, 2, dsp[4:6])
                gst_ = Rot(k, st, "gst", [32, TT], F32, 2, dsp[6:8])
                onb = Rot(k, st, "onb", [128, TT], BF16, 3, dsp[8:11])
                pt = Rot(k, st, "pt", [128, TT], BF16, 8)
                acc = [k.sb(st, f"acc{p}", [128, TT], F32) for p in range(4)]
                impacc = k.sb(st, "impacc", [128, 4, 2, 64], F32)
                selb = Rot(k, st, "selb", [128, 128], F32, 2)
                selbT = Rot(k, st, "selbT", [128, TT], BF16, 2)
                tmp = Rot(k, st, "tmp", [128, TT], F32, 2)
                coef = Rot(k, st, "coef", [128, TT], F32, 2)
                tmp2 = Rot(k, st, "tmp2", [128, TT], F32, 2)
                osb = Rot(k, st, "osb", [128, TT], F32, 4)
                gbs = Rot(k, st, "gbs", [128, TT], F32, 2)
                m8a = Rot(k, st, "m8a", [128, 8], F32, 2)
                m8b = Rot(k, st, "m8b", [128, 8], F32, 2)
                t64 = Rot(k, st, "t64", [128, 64], F32, 2)
                rl4 = Rot(k, st, "rl4", [128, 4], F32, 2)
                LD(dsp[11], ks, ks[:], ksT)
                LD(dsp[12], kw, kw[:], kwT)
                LD(dsp[13], vs3, vs3[:], vsw.rearrange("(j p) a c -> p j a c", p=128))
                LD(dsp[14], Et, Et[:], cst["E"])
                LD(dsp[15], selt, selt[:], cst["sel"])
                selr = k.sb(st, "selr", [32, 12, 128], mybir.dt.float32r)
                V(lambda e: e.tensor_copy(selr[:], selt[:]), [selt], [selr])
                gsr = Rot(k, st, "gsr", [32, TT], mybir.dt.float32r, 2)
                LD(dsp[16], ov1t, ov1t[:], cst["ov1"])
                for b_ in gst_.bufs:
                    V(lambda e: e.memset(b_[:], 0.0), [], [b_])
                sbk = pb[0:4]
                oA, oB, gb, xb = pb[4], pb[5], pb[6], pb[7]
                qTn_v = qTn.rearrange("r p t -> p r t")

                def pre(T):
                    tsl = slice(T * TT, (T + 1) * TT)
                    QN = qn.next()
                    LD(qn.d(), QN, QN[:], qTn_v[:, :, tsl])
                    MC = mct.next()
                    LD(mct.d(), MC, MC[:], cst["maskc"][:, :, tsl])
                    FB = fbt.next()
                    LD(fbt.d(), FB, FB[:], cst["fb"][:, 4 * T:4 * T + 4, :, :])
                    GS = gst_.next()
                    LD(gst_.d(), GS, GS[0:24, :], gsT[0:24, tsl])
                    GR = gsr.next()
                    V(lambda e: e.tensor_copy(GR[:], GS[:]), [GS], [GR])
                    return QN, MC, FB, GR

                def gpre(p, n, GS):
                    P(lambda e: e.matmul(gb[:, :], selr[0:32, p * 3 + n, :], GS[0:32, :], start=True, stop=True), [selr, GS], [gb])
                    g_ = gbs.next()
                    V(lambda e: e.tensor_copy(g_[:], gb[:]), [gb], [g_])
                    return g_

                def finish(p, n, T, g_, first, last):
                    oAs, oBs = osb.next(), osb.next()
                    V(lambda e: e.tensor_copy(oAs[:], oA[:]), [oA], [oAs])
                    V(lambda e: e.tensor_copy(oBs[:], oB[:]), [oB], [oBs])
                    r_ = tmp.next()
                    V(lambda e: e.tensor_scalar(out=r_[0:64, :], in0=oAs[64:128, :], scalar1=1e-18, scalar2=None, op0=ALU.max), [oAs], [r_])
                    V(lambda e: e.tensor_scalar(out=r_[64:128, :], in0=oBs[0:64, :], scalar1=1e-18, scalar2=None, op0=ALU.max), [oBs], [r_])
                    A(lambda e: e.activation(out=r_[:], in_=r_[:], func=AF.Ln), [r_], [r_])
                    A(lambda e: e.activation(out=r_[:], in_=r_[:], func=AF.Exp, scale=-1.0), [r_], [r_])
                    c_ = coef.next()
                    V(lambda e: e.tensor_tensor(out=c_[:], in0=r_[:], in1=g_[:], op=ALU.mult), [r_, g_], [c_])
                    dst = acc[p] if first else tmp2.next()
                    V(lambda e: e.tensor_tensor(out=dst[0:64, :], in0=oAs[0:64, :], in1=c_[0:64, :], op=ALU.mult), [oAs, c_], [dst])
                    V(lambda e: e.tensor_tensor(out=dst[64:128, :], in0=oBs[64:128, :], in1=c_[64:128, :], op=ALU.mult), [oBs, c_], [dst])
                    if last:
                        O_ = onb.next()
                        V(lambda e: e.tensor_tensor(out=O_[:], in0=acc[p][:], in1=dst[:], op=ALU.add), [acc[p], dst], [O_])
                        STO(onb.d(), O_, onT[p, :, T * TT:(T + 1) * TT], O_[:])
                    elif not first:
                        V(lambda e: e.tensor_tensor(out=acc[p][:], in0=acc[p][:], in1=dst[:], op=ALU.add), [acc[p], dst], [acc[p]])

                nxt = pre(0)
                for T in range(NT):
                    QN, MC, FB, GS = nxt
                    if T + 1 < NT:
                        nxt = pre(T + 1)
                    ncts = 2 if T >= 4 else 1
                    for p in range(4):
                        g_c = gpre(p, 0, GS)
                        PTs = {}
                        for ct in range(ncts):
                            sA, sB = sbk[(2 * ct) % 4], sbk[(2 * ct + 1) % 4]
                            P(lambda e: e.matmul(sA[:, :], kcmp_g[0:64, ct * 128:(ct + 1) * 128], QN[0:64, p, :], start=True, stop=True), [kcmp_g, QN], [sA])
                            P(lambda e: e.matmul(sB[:, :], kcmp_g[64:128, ct * 128:(ct + 1) * 128], QN[64:128, p, :], start=True, stop=True), [kcmp_g, QN], [sB])
                            for hd, sk in ((0, sA), (1, sB)):
                                PT = pt.next()
                                A(lambda e: e.activation(out=PT[:], in_=sk[:], func=AF.Exp, scale=0.125), [sk], [PT])
                                if not (ct == 0 and T >= 5):
                                    V(lambda e: e.tensor_tensor(out=PT[:], in0=PT[:], in1=MC[:, ct, :], op=ALU.mult), [PT, MC], [PT])
                                PTs[(hd, ct)] = PT
                        for ct in range(ncts):
                            last = (ct == ncts - 1)
                            P(lambda e: e.matmul(oA[:, :], vcmp_g[:, ct, 0:128], PTs[(0, ct)][:, :], start=(ct == 0), stop=last), [vcmp_g, PTs[(0, ct)]], [oA], inc=last)
                            P(lambda e: e.matmul(oB[:, :], vcmp_g[:, ct, 64:192], PTs[(1, ct)][:, :], start=(ct == 0), stop=last), [vcmp_g, PTs[(1, ct)]], [oB], inc=last)
                        for half in range(2):
                            for qi in range(2):
                                qs = half * 2 + qi
                                for hd in range(2):
                                    col = (qi * 2 + hd) * 65
                                    for ct in range(ncts):
                                        last = (ct == ncts - 1)
                                        P(lambda e: e.matmul(xb[:, col:col + 65], PTs[(hd, ct)][:, qs * 128:(qs + 1) * 128], ov1t[:, ct, :], start=(ct == 0), stop=last),
                                          [PTs[(hd, ct)], ov1t], [xb], inc=last)
                            r4 = rl4.next()
                            V(lambda e: e.tensor_scalar(out=r4[:, 0:4], in0=xb[:, 64:260:65], scalar1=1e-30, scalar2=None, op0=ALU.max), [xb], [r4])
                            V(lambda e: e.reciprocal(r4[:], r4[:]), [r4], [r4])
                            for qi in range(2):
                                qs = half * 2 + qi
                                for hd in range(2):
                                    col = (qi * 2 + hd) * 65
                                    src1 = FB[:, qs, hd, :] if p == 0 else impacc[:, qs, hd, :]
                                    V(lambda e: e.scalar_tensor_tensor(out=impacc[:, qs, hd, :], in0=xb[:, col:col + 64], scalar=r4[:, qi * 2 + hd:qi * 2 + hd + 1], in1=src1,
                                                                       op0=ALU.mult, op1=ALU.add), [xb, r4, FB, impacc], [impacc])
                        finish(p, 0, T, g_c, True, False)
                    SBT = selbT.next()
                    for qs in range(4):
                        SB = selb.next()
                        for g in range(2):
                            a8, t6, b8 = m8a.next(), t64.next(), m8b.next()
                            V(lambda e: e.max(a8[:], impacc[:, qs, g, :]), [impacc], [a8])
                            V(lambda e: e.match_replace(t6[:], a8[:], impacc[:, qs, g, :], -3.0e38), [a8, impacc], [t6])
                            V(lambda e: e.max(b8[:], t6[:]), [t6], [b8])
                            V(lambda e: e.tensor_scalar(out=SB[:, g * 64:(g + 1) * 64], in0=impacc[:, qs, g, :], scalar1=b8[:, 7:8], scalar2=NEG, op0=ALU.is_lt, op1=ALU.mult),
                              [impacc, b8], [SB])
                        P(lambda e: e.transpose(xb[:, qs * 128:(qs + 1) * 128], SB[:, :], ident_f[:]), [SB, ident_f], [xb])
                    cp('dve', SBT[:], xb[:], [xb], [SBT])
                    for br in (2, 1):
                        for p in range(4):
                            g_b = gpre(p, br, GS)
                            if br == 1:
                                js = list(range(0, 4 * T + 4))
                            else:
                                js = [4 * T] + [j_ for j_ in range(max(0, 4 * T - 4), 4 * T + 4) if j_ != 4 * T]
                            nj = len(js)
                            kt = ks if br == 1 else kw
                            LAG = 1
                            items = []
                            for step in range(nj + LAG):
                                if step < nj:
                                    j = js[step]
                                    if j >= 4 * T:
                                        c0, c1, ti = 128 * (j - 4 * T), TT, 0
                                        tc = c0
                                    elif br == 2:
                                        c0, c1, ti = 0, 128 * (j - 4 * T + 5), 1
                                        tc = c1 - 128
                                    else:
                                        c0, c1, ti, tc = 0, TT, None, None
                                    sks = (sbk[(2 * step) % 4], sbk[(2 * step + 1) % 4])
                                    for hd in range(2):
                                        rows = slice(hd * 64, hd * 64 + 64)
                                        P(lambda e: e.matmul(sks[hd][:, c0:c1], kt[rows, j * 128:(j + 1) * 128], QN[rows, p, c0:c1], start=True, stop=(br == 2)),
                                          [kt, QN], [sks[hd]], inc=(br == 2))
                                    if br == 1:
                                        for hd in range(2):
                                            rows = slice(hd * 64, hd * 64 + 64)
                                            P(lambda e: e.matmul(sks[hd][:, c0:c1], Et[rows, j, :], SBT[rows, c0:c1], start=False, stop=True), [Et, SBT], [sks[hd]])
                                    pts = []
                                    for hd in range(2):
                                        PT = pt.next()
                                        A(lambda e: e.activation(out=PT[:, c0:c1], in_=sks[hd][:, c0:c1], func=AF.Exp, scale=0.125), [sks[hd]], [PT])
                                        if ti is not None:
                                            V(lambda e: e.tensor_tensor(out=PT[:, tc:tc + 128], in0=PT[:, tc:tc + 128], in1=tri[:, ti, :], op=ALU.mult), [PT, tri], [PT])
                                        pts.append(PT)
                                    items.append((j, c0, c1, pts))
                                if step >= LAG:
                                    idx = step - LAG
                                    j, c0, c1, pts = items[idx]
                                    a = 0 if br == 1 else 1
                                    P(lambda e: e.matmul(oA[:, c0:c1], vs3[:, j, a, 0:128], pts[0][:, c0:c1], start=(idx == 0), stop=(idx == nj - 1)), [vs3, pts[0]], [oA], inc=(idx == nj - 1))
                                    P(lambda e: e.matmul(oB[:, c0:c1], vs3[:, j, a, 64:192], pts[1][:, c0:c1], start=(idx == 0), stop=(idx == nj - 1)), [vs3, pts[1]], [oB], inc=(idx == nj - 1))
                            finish(p, br, T, g_b, False, br == 1)
                k.barrier()

        def phaseE(l):
            with ES() as st:
                wg = k.sb(st, "wg", [128, 8, 3072], BF16)
                wos = [k.sb(st, f"wo{i}", [128, 4, D], BF16) for i in range(3)]
                wo = k.sb(st, "wout", [128, 8, D], BF16)
                ht = Rot(k, st, "ht", [128, 8, TT], BF16, 2, dsp[0:2])
                oin = [Rot(k, st, f"oin{i}", [128, 4, TT], BF16, 2, dsp[2 + 2 * i:4 + 2 * i]) for i in range(3)]
                xt = Rot(k, st, "xt", [128, 8, TT], F32, 1, dsp[8:9])
                m = k.sb(st, "m", [128, 8, TT], BF16)
                sg = Rot(k, st, "sg", [128, TT], F32, 2)
                ta = Rot(k, st, "ta", [128, TT], F32, 5)
                wst = Rot(k, st, "wst", [128, 1024], F32, 3, dsp[17:20])
                load_w(wst, wg, 8, 3072, lambda kc: wd["w3"][l, kc * 128:(kc + 1) * 128, :], PC_GMIX)
                for i, nm in enumerate(("woa", "wob", "woc")):
                    load_w(wst, wos[i], 4, D, lambda kc: wd[nm][l, kc * 128:(kc + 1) * 128, :])
                load_w(wst, wo, 8, D, lambda kc: wd["wout"][l, kc * 128:(kc + 1) * 128, :])
                srcs = [oTa.rearrange("c p t -> p c t"), caT.rearrange("c p t -> p c t"), onT.rearrange("c p t -> p c t")]
                bk = [0]

                def bank():
                    bk[0] = (bk[0] + 1) % 8
                    return pb[bk[0]]

                def pre(T):
                    tsl = slice(T * TT, (T + 1) * TT)
                    H = ht.next()
                    LD(ht.d(), H, H[:], hT_v[:, :, tsl])
                    Os = []
                    for i in range(3):
                        O_ = oin[i].next()
                        LD(oin[i].d(), O_, O_[:], srcs[i][:, :, tsl])
                        Os.append(O_)
                    return H, Os
                nxt = pre(0)
                for T in range(NT):
                    H, Os = nxt
                    tsl = slice(T * TT, (T + 1) * TT)
                    X = xt.next()
                    LD(xt.d(), X, X[:], xT_v[:, :, tsl])
                    if T + 1 < NT:
                        nxt = pre(T + 1)
                    for r in range(8):
                        terms = []
                        for xi in range(3):
                            bo = bank()
                            for c in range(4):
                                P(lambda e: e.matmul(bo[:], wos[xi][:, c, r * 128:(r + 1) * 128], Os[xi][:, c, :], start=(c == 0), stop=(c == 3)), [wos[xi], Os[xi]], [bo], inc=(c == 3))
                            bg = bank()
                            for kc in range(8):
                                P(lambda e: e.matmul(bg[:], wg[:, kc, xi * 1024 + r * 128:xi * 1024 + (r + 1) * 128], H[:, kc, :], start=(kc == 0), stop=(kc == 7)), [wg, H], [bg], inc=(kc == 7))
                            g_ = sg.next()
                            A(lambda e: e.activation(out=g_[:], in_=bg[:], func=AF.Sigmoid), [bg], [g_])
                            t_ = ta.next()
                            if xi == 1:
                                V(lambda e: e.scalar_tensor_tensor(out=t_[:], in0=bo[:], scalar=pc(PC_BCO + r), in1=g_[:], op0=ALU.add, op1=ALU.mult), [bo, pcol, g_], [t_])
                            else:
                                V(lambda e: e.tensor_tensor(out=t_[:], in0=bo[:], in1=g_[:], op=ALU.mult), [bo, g_], [t_])
                            terms.append(t_)
                        u_ = ta.next()
                        V(lambda e: e.tensor_tensor(out=u_[:], in0=terms[0][:], in1=terms[1][:], op=ALU.add), [terms[0], terms[1]], [u_])
                        G(lambda e: e.tensor_tensor(out=m[:, r, :], in0=u_[:], in1=terms[2][:], op=ALU.add), [u_, terms[2]], [m])
                    for r2 in range(8):
                        bo = bank()
                        for r in range(8):
                            P(lambda e: e.matmul(bo[:], wo[:, r, r2 * 128:(r2 + 1) * 128], m[:, r, :], start=(r == 0), stop=(r == 7)), [wo, m], [bo], inc=(r == 7))
                        V(lambda e: e.tensor_tensor(out=X[:, r2, :], in0=X[:, r2, :], in1=bo[:], op=ALU.add), [X, bo], [X])
                    STO(xt.d(), X, xT_v[:, :, tsl], X[:])
                k.barrier()

        def phaseF(l, hf, last):
            with ES() as st:
                w1h = k.sb(st, "w1h", [128, 8, 2048], BF16)
                w2h = k.sb(st, "w2h", [128, 16, D], BF16)
                xt = Rot(k, st, "xt", [128, 8, TT], F32, 2, dsp[0:2])
                h2 = Rot(k, st, "h2", [128, 8, TT], BF16, 2, dsp[2:4])
                a = k.sb(st, "a", [128, 16, TT], BF16)
                rl = Rot(k, st, "rl", [128, TT], F32, 2)
                wst = Rot(k, st, "wst", [128, 1024], F32, 4, dsp[17:21])
                if hf == 0:
                    sqb = k.sb(st, "sqb", [128, 8, TT], F32)
                    ss = Rot(k, st, "ss", [128, TT], mybir.dt.float32r, 2)
                    rs = Rot(k, st, "rs", [128, TT], F32, 2)
                fin = last and hf == 1
                if fin:
                    outb = Rot(k, st, "outb", [128, 4, D], F32, 1, dsp[4:5])
                load_w(wst, w1h, 8, 2048, lambda kc: wd["wf1"][l, kc * 128:(kc + 1) * 128, hf * 2048:(hf + 1) * 2048], PC_GFFN)
                load_w(wst, w2h, 16, D, lambda kc: wd["wf2"][l, hf * 2048 + kc * 128:hf * 2048 + (kc + 1) * 128, :])
                bk = [0]

                def bank():
                    bk[0] = (bk[0] + 1) % 8
                    return pb[bk[0]]

                def pre(T):
                    tsl = slice(T * TT, (T + 1) * TT)
                    X = xt.next()
                    LD(xt.d(), X, X[:], xT_v[:, :, tsl])
                    H2 = h2.next()
                    if hf == 1:
                        LD(h2.d(), H2, H2[:], h2T_v[:, :, tsl])
                    return X, H2
                nxt = pre(0)
                for T in range(NT):
                    X, H2 = nxt
                    tsl = slice(T * TT, (T + 1) * TT)
                    if T + 1 < NT:
                        nxt = pre(T + 1)
                    if hf == 0:
                        A(lambda e: e.activation(out=sqb[:], in_=X[:], func=AF.Square), [X], [sqb])
                        s_ = ss.next()
                        V(lambda e: e.tensor_reduce(out=s_[:], in_=sqb[:].rearrange("p c t -> p t c"), axis=AX.X, op=ALU.add), [sqb], [s_])
                        bs = bank()
                        stat_mm(bs[:], ones_r[:], s_[:], [ones_r, s_], [bs])
                        r_ = rs.next()
                        rstd_from(bs, r_, 1024.0)
                        V(lambda e: e.tensor_tensor(out=H2[:], in0=X[:], in1=r_[:].unsqueeze(1).to_broadcast([128, 8, TT]), op=ALU.mult), [X, r_], [H2])
                        STO(h2.d(), H2, h2T_v[:, :, tsl], H2[:])
                    for f in range(16):
                        b = bank()
                        for kc in range(8):
                            P(lambda e: e.matmul(b[:], w1h[:, kc, f * 128:(f + 1) * 128], H2[:, kc, :], start=(kc == 0), stop=(kc == 7)), [w1h, H2], [b], inc=(kc == 7))
                        r_ = rl.next()
                        A(lambda e: e.activation(out=r_[:], in_=b[:], func=AF.Relu), [b], [r_])
                        (V if f % 2 else G)(lambda e: e.tensor_tensor(out=a[:, f, :], in0=r_[:], in1=r_[:], op=ALU.mult), [r_], [a])
                    for r2 in range(8):
                        b = bank()
                        for f in range(16):
                            P(lambda e: e.matmul(b[:], w2h[:, f, r2 * 128:(r2 + 1) * 128], a[:, f, :], start=(f == 0), stop=(f == 15)), [w2h, a], [b], inc=(f == 15))
                        V(lambda e: e.tensor_tensor(out=X[:, r2, :], in0=X[:, r2, :], in1=b[:], op=ALU.add), [X, b], [X])
                    if fin:
                        OB = outb.next()
                        for s in range(4):
                            for half in range(2):
                                b = bank()
                                for i in range(4):
                                    r2 = half * 4 + i
                                    P(lambda e: e.transpose(b[:, i * 128:(i + 1) * 128], X[:, r2, s * 128:(s + 1) * 128], ident_f[:]), [X, ident_f], [b], inc=(i == 3))
                                cp('act' if half else 'dve', OB[:, s, half * 512:(half + 1) * 512], b[:], [b], [OB])
                        STO(outb.d(), OB, out_d[tsl, :].rearrange("(s p) d -> p s d", p=128), OB[:])
                    else:
                        STO(xt.d(), X, xT_v[:, :, tsl], X[:])
                k.barrier()

        if run('0'):
            phase0()
        for l in range(nl):
            if run('A1'):
                phaseA1(l)
            else:
                LD(dsp[20], pcol, pcol[:], wd["pcol"][l])
            if run('A2'):
                phaseA2(l)
            if run('B'):
                phaseB(l)
            if run('C'):
                phaseC(l)
            if run('D0'):
                phaseD0(l)
            if run('D'):
                phaseD(l)
            if run('E'):
                phaseE(l)
            if run('F'):
                phaseF(l, 0, False)
                phaseF(l, 1, l == nl - 1)
        k.barrier()
    build.n_ins = k.n_ins
    return nc


_CACHE = {}


def kernel(**inputs):
    inputs = {k_: np.asarray(v) for k_, v in inputs.items()}
    consts = _host_consts()
    prep = _host_prep(inputs)
    nc = build()
    x = inputs['x']
    shared = dict(consts)
    shared.update(prep)
    in_maps = []
    for b in range(8):
        m = dict(shared)
        m['x'] = np.ascontiguousarray(x[b])
        in_maps.append(m)
    res = run_bass_kernel_spmd(nc, in_maps, core_ids=list(range(8)))
    return np.stack([np.asarray(r['out'], dtype=np.float32) for r in res.results], axis=0)
```
